# Optimizing a Trainium2 kernel written in Bass

```python
import math
import jax, jax.numpy as jnp
from jax import lax
import numpy as np

D_MODEL = 1024
BATCH = 8
SEQ = 8192
DEPTH = 1

CHUNK = 64
Q_BLOCK = 128
SB_HEADS = 8
SB_HEAD_DIM = 64
SB_WIDTH = SB_HEADS * SB_HEAD_DIM
DA_HEADS = 4
DA_HEAD_DIM = 64
DA_V_DIM = 2 * DA_HEAD_DIM
DA_QK_WIDTH = DA_HEADS * 2 * DA_HEAD_DIM
DA_V_WIDTH = DA_HEADS * DA_V_DIM
D_FF = 2816
N_SUB = 3
EPS = 1e-6
NEG = -1e30
IN_SIZES = (SB_WIDTH, SB_WIDTH, SB_WIDTH, DA_QK_WIDTH, DA_QK_WIDTH, DA_V_WIDTH, D_MODEL, D_MODEL)
IN_WIDTH = sum(IN_SIZES)
IN_SPLITS = tuple(int(v) for v in np.cumsum(IN_SIZES)[:-1])

kernel_name = "hybrid_stickbreak_diffattn_macaron_block"


def rms_norm(x, g):
    x32 = x.astype(jnp.float32)
    y = x32 * lax.rsqrt(jnp.mean(x32 * x32, axis=-1, keepdims=True) + EPS)
    return (y * g.astype(jnp.float32)).astype(x.dtype)


def swiglu(u, w_gate, w_up, w_down):
    return (jax.nn.silu(u @ w_gate) * (u @ w_up)) @ w_down


def to_blocks(t):
    b, h, s, d = t.shape
    return t.reshape(b, h, s // Q_BLOCK, Q_BLOCK, d).transpose(2, 0, 1, 3, 4)


def from_blocks(t):
    nb, b, h, qb, d = t.shape
    return t.transpose(1, 2, 0, 3, 4).reshape(b, h, nb * qb, d)


def stick_breaking_attention(q, k, v):
    s_len = q.shape[2]
    scale = 1.0 / math.sqrt(q.shape[-1])
    spos = jnp.arange(s_len)
    tpos = spos.reshape(s_len // Q_BLOCK, Q_BLOCK)

    def block(args):
        qi, ti = args
        z = jnp.einsum('bhqd,bhkd->bhqk', qi, k).astype(jnp.float32) * scale
        before = spos[None, :] < ti[:, None]
        log_fail = jnp.where(before, jax.nn.log_sigmoid(-z), 0.0)
        later = lax.cumsum(log_fail, axis=3, reverse=True) - log_fail
        w = jnp.where(before, jnp.exp(jax.nn.log_sigmoid(z) + later), 0.0)
        return jnp.einsum('bhqk,bhkd->bhqd', w.astype(v.dtype), v)

    return from_blocks(lax.map(block, (to_blocks(q), tpos)))


def differential_attention(q1, q2, k1, k2, v, lam):
    s_len = q1.shape[2]
    n_heads = q1.shape[1]
    scale = 1.0 / math.sqrt(q1.shape[-1])
    slopes = jnp.asarray(2.0 ** (-8.0 * (np.arange(n_heads) + 1) / n_heads), dtype=jnp.float32)
    spos = jnp.arange(s_len)
    tpos = spos.reshape(s_len // Q_BLOCK, Q_BLOCK)

    def block(args):
        qi1, qi2, ti = args
        allowed = (spos[None, :] // CHUNK) <= (ti[:, None] // CHUNK)
        dist = jnp.abs(ti[:, None] - spos[None, :]).astype(jnp.float32)
        bias = -slopes[:, None, None] * dist
        s1 = jnp.einsum('bhqd,bhkd->bhqk', qi1, k1).astype(jnp.float32) * scale + bias
        s2 = jnp.einsum('bhqd,bhkd->bhqk', qi2, k2).astype(jnp.float32) * scale + bias
        p = (jax.nn.softmax(jnp.where(allowed, s1, NEG), axis=-1)
             - lam * jax.nn.softmax(jnp.where(allowed, s2, NEG), axis=-1))
        return jnp.einsum('bhqk,bhkd->bhqd', p.astype(v.dtype), v)

    return from_blocks(lax.map(block, (to_blocks(q1), to_blocks(q2), tpos)))


def token_mixer(u, w_in, lq1, lk1, lq2, lk2, subln, w_branch_sb, w_branch_da, w_out, lambda_init):
    b, s, _ = u.shape
    proj = u @ w_in
    qa, ka, va, qd, kd, vd, ga, gd = jnp.split(proj, IN_SPLITS, axis=-1)

    def heads(t, h):
        return t.reshape(b, s, h, -1).transpose(0, 2, 1, 3)

    ya = stick_breaking_attention(heads(qa, SB_HEADS), heads(ka, SB_HEADS), heads(va, SB_HEADS))
    ya = ya.transpose(0, 2, 1, 3).reshape(b, s, SB_WIDTH)

    qd = qd.reshape(b, s, DA_HEADS, 2, DA_HEAD_DIM).transpose(0, 2, 3, 1, 4)
    kd = kd.reshape(b, s, DA_HEADS, 2, DA_HEAD_DIM).transpose(0, 2, 3, 1, 4)
    f32 = jnp.float32
    lam = (jnp.exp(jnp.sum(lq1.astype(f32) * lk1.astype(f32)))
           - jnp.exp(jnp.sum(lq2.astype(f32) * lk2.astype(f32))) + lambda_init)
    yd = differential_attention(qd[:, :, 0], qd[:, :, 1], kd[:, :, 0], kd[:, :, 1],
                                heads(vd, DA_HEADS), lam)
    yd = rms_norm(yd, subln) * (1.0 - lambda_init)
    yd = yd.transpose(0, 2, 1, 3).reshape(b, s, DA_V_WIDTH)

    merged = jax.nn.sigmoid(ga) * (ya @ w_branch_sb) + jax.nn.sigmoid(gd) * (yd @ w_branch_da)
    return merged @ w_out


def setup_inputs(seed: int = 0) -> dict:
    key = jax.random.key(seed)
    ks = jax.random.split(key, 24)
    f32 = jnp.float32

    def dense(k, shape, fan_in):
        return jax.random.normal(k, shape, f32) * fan_in ** -0.5

    L, D, F = DEPTH, D_MODEL, D_FF
    return {
        "x": jax.random.normal(ks[0], (BATCH, SEQ, D), f32),
        "c": jax.random.normal(ks[1], (BATCH, D), f32),
        "w_ada": dense(ks[2], (L, D, N_SUB * 3 * D), D),
        "b_ada": 0.01 * jax.random.normal(ks[3], (L, N_SUB * 3 * D), f32),
        "norm_pre": 1.0 + 0.05 * jax.random.normal(ks[4], (L, N_SUB, D), f32),
        "norm_post": 1.0 + 0.05 * jax.random.normal(ks[5], (L, N_SUB, D), f32),
        "ffn1_w_gate": dense(ks[6], (L, D, F), D),
        "ffn1_w_up": dense(ks[7], (L, D, F), D),
        "ffn1_w_down": dense(ks[8], (L, F, D), F),
        "w_in": dense(ks[9], (L, D, IN_WIDTH), D),
        "da_lambda_q1": 0.1 * jax.random.normal(ks[10], (L, DA_HEAD_DIM), f32),
        "da_lambda_k1": 0.1 * jax.random.normal(ks[11], (L, DA_HEAD_DIM), f32),
        "da_lambda_q2": 0.1 * jax.random.normal(ks[12], (L, DA_HEAD_DIM), f32),
        "da_lambda_k2": 0.1 * jax.random.normal(ks[13], (L, DA_HEAD_DIM), f32),
        "da_subln": 1.0 + 0.05 * jax.random.normal(ks[14], (L, DA_V_DIM), f32),
        "w_branch_sb": dense(ks[15], (L, SB_WIDTH, D), SB_WIDTH),
        "w_branch_da": dense(ks[16], (L, DA_V_WIDTH, D), DA_V_WIDTH),
        "w_out": dense(ks[17], (L, D, D), D),
        "ffn2_w_gate": dense(ks[18], (L, D, F), D),
        "ffn2_w_up": dense(ks[19], (L, D, F), D),
        "ffn2_w_down": dense(ks[20], (L, F, D), F),
    }


def reference(x, c, w_ada, b_ada, norm_pre, norm_post, ffn1_w_gate, ffn1_w_up, ffn1_w_down,
              w_in, da_lambda_q1, da_lambda_k1, da_lambda_q2, da_lambda_k2, da_subln,
              w_branch_sb, w_branch_da, w_out, ffn2_w_gate, ffn2_w_up, ffn2_w_down):
    b = x.shape[0]
    h = x
    for l in range(DEPTH):
        lambda_init = 0.8 - 0.6 * math.exp(-0.3 * l)
        mod = (jax.nn.silu(c) @ w_ada[l] + b_ada[l]).reshape(b, N_SUB, 3, D_MODEL)

        def sublayer(h, i, fn, resid_w):
            shift = mod[:, i, 0, None, :]
            scale = mod[:, i, 1, None, :]
            gate = mod[:, i, 2, None, :]
            u = rms_norm(h, norm_pre[l, i]) * (1.0 + scale) + shift
            return h + resid_w * gate * rms_norm(fn(u), norm_post[l, i])

        h = sublayer(h, 0, lambda u: swiglu(u, ffn1_w_gate[l], ffn1_w_up[l], ffn1_w_down[l]), 0.5)
        h = sublayer(h, 1, lambda u: token_mixer(u, w_in[l], da_lambda_q1[l], da_lambda_k1[l],
                                                 da_lambda_q2[l], da_lambda_k2[l], da_subln[l],
                                                 w_branch_sb[l], w_branch_da[l], w_out[l],
                                                 lambda_init), 1.0)
        h = sublayer(h, 2, lambda u: swiglu(u, ffn2_w_gate[l], ffn2_w_up[l], ffn2_w_down[l]), 0.5)
    return h
```

```python
import math
import numpy as np
from contextlib import ExitStack
import concourse.bass as bass
import concourse.mybir as mybir
from concourse.bass_utils import run_bass_kernel_spmd

F32 = mybir.dt.float32
BF16 = mybir.dt.bfloat16
AF = mybir.ActivationFunctionType
ALU = mybir.AluOpType
AX = mybir.AxisListType

D = 1024
DFF = 2816
NF = DFF // 128
EPS = 1e-6
LAMBDA_INIT = 0.8 - 0.6 * math.exp(-0.3 * 0)
SLOPES = [2.0 ** (-8.0 * (h + 1) / 4) for h in range(4)]
NEGBIG = -30000.0


class Sched:
    EPOCH = 30000

    def __init__(self, nc, es):
        self.nc = nc
        self.es = es
        self.engs = {'pe': nc.tensor, 'act': nc.scalar, 'dve': nc.vector, 'pool': nc.gpsimd, 'sp': nc.sync}
        self.cnt = {k: 0 for k in self.engs}
        self.last = {}
        self.sems = {}
        self.semval = {}
        self.waited = {}
        self.res = {}
        self.nsem = 0
        self.dmakeys = set()

    def sem(self, key):
        if key not in self.sems:
            self.sems[key] = self.es.enter_context(self.nc.semaphore("s%d" % self.nsem))
            self.nsem += 1
            self.semval[key] = 0
        return self.sems[key]

    def _deps(self, eng, reads, writes):
        deps = {}

        def add(tok, kind):
            if tok is None:
                return
            k, v = tok
            if isinstance(k, tuple) and k[0] == 'E' and k[1] == eng:
                if eng == 'pe' or kind == 'war':
                    return
            if v > deps.get(k, 0):
                deps[k] = v
        for r in reads:
            st = self.res.get(r)
            if st:
                add(st[0], 'raw')
        for w in writes:
            st = self.res.get(w)
            if st:
                add(st[0], 'waw')
                for t in st[1].values():
                    add(t, 'war')
        for k, v in deps.items():
            if self.waited.get((eng, k), 0) < v:
                self.waited[(eng, k)] = v
                self.engs[eng].wait_ge(self.sems[k], v)

    def _commit(self, tok, reads, writes):
        for r in reads:
            st = self.res.setdefault(r, [None, {}])
            st[1][tok[0]] = tok
        for w in writes:
            self.res[w] = [tok, {}]

    def op(self, eng, fn, reads=(), writes=()):
        self._deps(eng, reads, writes)
        self.cnt[eng] += 1
        key = ('E', eng, self.cnt[eng] // self.EPOCH)
        s = self.sem(key)
        self.semval[key] += 1
        tok = (key, self.semval[key])
        fn(self.engs[eng]).then_inc(s, 1)
        self.last[eng] = tok
        self._commit(tok, reads, writes)
        return tok

    def dma(self, eng, out, in_, reads=(), writes=(), semkey=None, **kw):
        self._deps(eng, reads, writes)
        s = self.sem(semkey)
        self.dmakeys.add(semkey)
        self.semval[semkey] += 16
        tok = (semkey, self.semval[semkey])
        self.engs[eng].dma_start(out=out, in_=in_, **kw).then_inc(s, 16)
        self._commit(tok, reads, writes)
        return tok

    def barrier(self):
        for e in self.engs:
            for f, tok in self.last.items():
                if f == e:
                    continue
                k, v = tok
                if self.waited.get((e, k), 0) < v:
                    self.waited[(e, k)] = v
                    self.engs[e].wait_ge(self.sems[k], v)
            for k in self.dmakeys:
                v = self.semval[k]
                if v and self.waited.get((e, k), 0) < v:
                    self.waited[(e, k)] = v
                    self.engs[e].wait_ge(self.sems[k], v)
        self.res = {}


def build_program(S_LEN, debug=False):
    nc = bass.Bass("TRN2", target_bir_lowering=False)
    NQ = S_LEN // 512
    NKT = S_LEN // 128

    def din(name, shape, dt=F32):
        return nc.dram_tensor(name, shape, dt, kind="ExternalInput").ap()

    def dscr(name, shape, dt):
        return nc.dram_tensor(name, shape, dt, kind=("ExternalOutput" if debug else "Internal")).ap()

    xT = din("xT", [D, S_LEN])
    cT = din("cT", [128, 8])
    w_ada = din("w_ada", [D, 9 * D])
    b_ada_r = din("b_ada_r", [128, 72])
    npre_r = din("npre_r", [128, 24])
    npost_r = din("npost_r", [128, 24])
    ffn_w = []
    for i in (1, 2):
        ffn_w.append((din("f%d_wg" % i, [D, DFF]), din("f%d_wu" % i, [D, DFF]), din("f%d_wd" % i, [DFF, D])))
    w_in = din("w_in", [D, 5120])
    lam_r = din("lam_r", [128, 256])
    subln_r = din("subln_r", [128, 1])
    w_bsb = din("w_bsb", [512, D])
    w_bda = din("w_bda", [512, D])
    w_out = din("w_out", [D, D])
    outT = nc.dram_tensor("outT", [D, S_LEN], F32, kind="ExternalOutput").ap()

    h1T = dscr("h1T", [D, S_LEN], F32)
    h2T = dscr("h2T", [D, S_LEN], F32)
    qaT = dscr("qaT", [512, S_LEN], BF16)
    kaT = dscr("kaT", [512, S_LEN], BF16)
    vaS = dscr("vaS", [S_LEN, 512], BF16)
    qdT = dscr("qdT", [512, S_LEN], BF16)
    kdT = dscr("kdT", [512, S_LEN], BF16)
    vdS = dscr("vdS", [S_LEN, 512], BF16)
    yaT = dscr("yaT", [512, S_LEN], BF16)
    ydT = dscr("ydT", [512, S_LEN], BF16)

    with ExitStack() as es:
        S = Sched(nc, es)

        def sbt(st, name, shape, dt):
            return st.enter_context(nc.sbuf_tensor(name, shape, dt))

        pp = [es.enter_context(nc.psum_tensor("pp%d" % i, [128, 1024], F32)) for i in range(4)]
        banks = []
        for i in range(4):
            banks.append(pp[i][:, 0:512])
            banks.append(pp[i][:, 512:1024])

        ones = sbt(es, "ones", [128, 128], BF16)
        negones = sbt(es, "negones", [128, 128], BF16)
        negui = sbt(es, "negui", [128, 128], BF16)
        negI = sbt(es, "negI", [128, 128], BF16)
        modsb = sbt(es, "modsb", [128, 72], F32)
        Acoef = sbt(es, "Acoef", [128, 24], F32)
        Gcoef = sbt(es, "Gcoef", [128, 24], F32)
        neglam = sbt(es, "neglam", [128, 1], F32)
        sub8 = sbt(es, "sub8", [128, 1], F32)

        S.op('pool', lambda e: e.memset(ones[:], 1.0), writes=['ones'])
        S.op('pool', lambda e: e.memset(negones[:], -1.0), writes=['negones'])
        S.op('pool', lambda e: e.affine_select(out=negui[:], in_=negones[:], pattern=[[-1, 128]],
                                               compare_op=ALU.is_ge, fill=0.0, base=0, channel_multiplier=1),
             reads=['negones'], writes=['negui'])
        S.op('pool', lambda e: e.affine_select(out=negI[:], in_=negones[:], pattern=[[-1, 128]],
                                               compare_op=ALU.is_equal, fill=0.0, base=0, channel_multiplier=1),
             reads=['negones'], writes=['negI'])

        with ExitStack() as ph:
            cs = sbt(ph, "cs", [128, 8], F32)
            cs_in = sbt(ph, "cs_in", [128, 8], F32)
            bada = sbt(ph, "bada", [128, 72], F32)
            npre = sbt(ph, "npre", [128, 24], F32)
            npost = sbt(ph, "npost", [128, 24], F32)
            lamt = sbt(ph, "lamt", [128, 256], F32)
            lprod = sbt(ph, "lprod", [128, 128], F32)
            lsum = sbt(ph, "lsum", [128, 2], F32)
            lexp = sbt(ph, "lexp", [128, 2], F32)
            subt = sbt(ph, "subt", [128, 1], F32)
            GW = 1152
            wa = [sbt(ph, "wa%d" % i, [128, 8, GW], F32) for i in range(2)]
            S.dma('sp', cs_in[:], cT, writes=['cs_in'], semkey='d_c')
            S.dma('sp', bada[:], b_ada_r, writes=['bada'], semkey='d_b')
            S.dma('sp', npre[:], npre_r, writes=['npre'], semkey='d_np')
            S.dma('sp', npost[:], npost_r, writes=['npost'], semkey='d_npo')
            S.dma('sp', lamt[:], lam_r, writes=['lamt'], semkey='d_lam')
            S.dma('sp', subt[:], subln_r, writes=['subt'], semkey='d_sub')
            S.op('act', lambda e: e.activation(out=cs[:], in_=cs_in[:], func=AF.Silu), reads=['cs_in'], writes=['cs'])
            w_ada_v = w_ada.rearrange("(kc p) n -> p kc n", p=128)
            modps = banks[0]
            for g in range(8):
                b = g % 2
                S.dma('sp', wa[b][:], w_ada_v[:, :, g * GW:(g + 1) * GW], writes=[('wa', b)], semkey=('d_wa', b))
                for jj in range(9):
                    j = g * 9 + jj
                    for kc in range(8):
                        S.op('pe', lambda e, b=b, jj=jj, kc=kc, j=j: e.matmul(
                            modps[:, j:j + 1], lhsT=wa[b][:, kc, jj * 128:(jj + 1) * 128], rhs=cs[:, kc:kc + 1],
                            start=(kc == 0), stop=(kc == 7)),
                            reads=[('wa', b), 'cs'], writes=['modps'])
            S.op('dve', lambda e: e.tensor_tensor(out=modsb[:], in0=modps[:, 0:72], in1=bada[:], op=ALU.add),
                 reads=['modps', 'bada'], writes=['modsb'])
            for i in range(3):
                rw = 1.0 if i == 1 else 0.5
                S.op('dve', lambda e, i=i: e.scalar_tensor_tensor(
                    out=Acoef[:, i * 8:(i + 1) * 8], in0=modsb[:, (i * 3 + 1) * 8:(i * 3 + 2) * 8], scalar=1.0,
                    in1=npre[:, i * 8:(i + 1) * 8], op0=ALU.add, op1=ALU.mult),
                    reads=['modsb', 'npre'], writes=['Acoef'])
                S.op('dve', lambda e, i=i, rw=rw: e.scalar_tensor_tensor(
                    out=Gcoef[:, i * 8:(i + 1) * 8], in0=modsb[:, (i * 3 + 2) * 8:(i * 3 + 3) * 8], scalar=rw,
                    in1=npost[:, i * 8:(i + 1) * 8], op0=ALU.mult, op1=ALU.mult),
                    reads=['modsb', 'npost'], writes=['Gcoef'])
            S.op('dve', lambda e: e.tensor_tensor(out=lprod[:, 0:64], in0=lamt[:, 0:64], in1=lamt[:, 64:128], op=ALU.mult),
                 reads=['lamt'], writes=['lprod'])
            S.op('dve', lambda e: e.tensor_tensor(out=lprod[:, 64:128], in0=lamt[:, 128:192], in1=lamt[:, 192:256], op=ALU.mult),
                 reads=['lamt'], writes=['lprod2'])
            S.op('dve', lambda e: e.reduce_sum(out=lsum[:, 0:1], in_=lprod[:, 0:64], axis=AX.X),
                 reads=['lprod'], writes=['lsum'])
            S.op('dve', lambda e: e.reduce_sum(out=lsum[:, 1:2], in_=lprod[:, 64:128], axis=AX.X),
                 reads=['lprod2'], writes=['lsum2'])
            S.op('act', lambda e: e.activation(out=lexp[:], in_=lsum[:], func=AF.Exp), reads=['lsum', 'lsum2'], writes=['lexp'])
            S.op('dve', lambda e: e.scalar_tensor_tensor(out=neglam[:], in0=lexp[:, 1:2], scalar=-LAMBDA_INIT,
                                                         in1=lexp[:, 0:1], op0=ALU.add, op1=ALU.subtract),
                 reads=['lexp'], writes=['neglam'])
            S.op('dve', lambda e: e.tensor_scalar(out=sub8[:], in0=subt[:], scalar1=1.0 - LAMBDA_INIT, scalar2=None,
                                                  op0=ALU.mult),
                 reads=['subt'], writes=['sub8'])
            S.barrier()

        def load_w_bf16(dst, src, kcn, ncols, key, col0=0):
            v = src.rearrange("(kc p) n -> p kc n", p=128)
            c = 0
            while c < ncols:
                w = min(1024, ncols - c)
                S.dma('pool', dst[:, :, c:c + w], v[:, :, col0 + c:col0 + c + w], writes=[key], semkey=('dw', key))
                c += w

        def rstd_from(bank, TT, inv_n, ln_t, rstd, bkey, rkey):
            S.op('act', lambda e: e.activation(out=ln_t[:, 0:TT], in_=bank[:, 0:TT], func=AF.Ln, bias=EPS, scale=inv_n),
                 reads=[bkey], writes=[rkey + '_ln'])
            S.op('act', lambda e: e.activation(out=rstd[:, 0:TT], in_=ln_t[:, 0:TT], func=AF.Exp, scale=-0.5),
                 reads=[rkey + '_ln'], writes=[rkey])

        def prenorm(h, hkey, u, ukey, sq, i_sub, TT, bank, bkey, ln_t, rstd, tmp, do_sq=True):
            if do_sq:
                S.op('dve', lambda e: e.tensor_tensor(out=sq[:], in0=h[:], in1=h[:], op=ALU.mult), reads=[hkey], writes=['sq'])
            for c in range(8):
                S.op('pe', lambda e, c=c: e.matmul(bank[:, 0:TT], lhsT=ones[:], rhs=sq[:, c, :], start=(c == 0), stop=(c == 7)),
                     reads=['sq', 'ones'], writes=[bkey])
            rstd_from(bank, TT, 1.0 / D, ln_t, rstd, bkey, 'rstd')
            for c in range(8):
                col = i_sub * 8 + c
                S.op('dve', lambda e, c=c, col=col: e.scalar_tensor_tensor(
                    out=tmp[c % 2][:, 0:TT], in0=h[:, c, :], scalar=Acoef[:, col:col + 1], in1=rstd[:, 0:TT],
                    op0=ALU.mult, op1=ALU.mult), reads=[hkey, 'rstd', 'Acoef'], writes=[('tmp', c % 2)])
                scol = (i_sub * 3 + 0) * 8 + c
                S.op('act', lambda e, c=c, scol=scol: e.activation(
                    out=u[:, c, :], in_=tmp[c % 2][:, 0:TT], func=AF.Identity, bias=modsb[:, scol:scol + 1], scale=1.0),
                    reads=[('tmp', c % 2), 'modsb'], writes=[ukey])

        def postnorm_resid(y, h, hkey, sq, i_sub, TT, bank, bkey, ln_t, rstd, tmp):
            S.op('dve', lambda e: e.tensor_tensor(out=sq[:], in0=y[:], in1=y[:], op=ALU.mult), reads=['y'], writes=['sq'])
            for c in range(8):
                S.op('pe', lambda e, c=c: e.matmul(bank[:, 0:TT], lhsT=ones[:], rhs=sq[:, c, :], start=(c == 0), stop=(c == 7)),
                     reads=['sq', 'ones'], writes=[bkey])
            rstd_from(bank, TT, 1.0 / D, ln_t, rstd, bkey, 'rstd')
            for c in range(8):
                col = i_sub * 8 + c
                S.op('dve', lambda e, c=c, col=col: e.scalar_tensor_tensor(
                    out=tmp[c % 2][:, 0:TT], in0=y[:, c, :], scalar=Gcoef[:, col:col + 1], in1=rstd[:, 0:TT],
                    op0=ALU.mult, op1=ALU.mult), reads=['y', 'rstd', 'Gcoef'], writes=[('tmp', c % 2)])
                S.op('dve', lambda e, c=c: e.tensor_tensor(out=h[:, c, :], in0=tmp[c % 2][:, 0:TT], in1=h[:, c, :], op=ALU.add),
                     reads=[('tmp', c % 2), hkey], writes=[hkey])

        def ffn_phase(src, dst, wts, i_sub, tag):
            TT = 256
            NT = S_LEN // TT
            with ExitStack() as ph:
                wg = sbt(ph, "wg" + tag, [128, 8, DFF], BF16)
                wu = sbt(ph, "wu" + tag, [128, 8, DFF], BF16)
                wd = sbt(ph, "wd" + tag, [128, NF, D], BF16)
                xt = [sbt(ph, "xt%d%s" % (i, tag), [128, 8, TT], F32) for i in range(3)]
                u = [sbt(ph, "u%d%s" % (i, tag), [128, 8, TT], BF16) for i in range(2)]
                sq = sbt(ph, "sq" + tag, [128, 8, TT], BF16)
                act = sbt(ph, "act" + tag, [128, NF, TT], BF16)
                y = sbt(ph, "y" + tag, [128, 8, TT], F32)
                st = [sbt(ph, "st%d%s" % (i, tag), [128, TT], F32) for i in range(2)]
                tmp = [sbt(ph, "tmp%d%s" % (i, tag), [128, TT], F32) for i in range(2)]
                ln_t = sbt(ph, "ln" + tag, [128, TT], F32)
                rstd = sbt(ph, "rstd" + tag, [128, TT], F32)
                load_w_bf16(wg, wts[0], 8, DFF, 'wg')
                load_w_bf16(wu, wts[1], 8, DFF, 'wu')
                load_w_bf16(wd, wts[2], NF, D, 'wd')
                srcv = src.rearrange("(c p) t -> p c t", p=128)
                dstv = dst.rearrange("(c p) t -> p c t", p=128)

                def load(t):
                    s = t % 3
                    S.dma('sp', xt[s][:], srcv[:, :, t * TT:(t + 1) * TT], writes=[('x', s)], semkey=('dx', s))

                def pre(t):
                    prenorm(xt[t % 3], ('x', t % 3), u[t % 2], ('u', t % 2), sq, i_sub, TT, banks[7], 'b7', ln_t, rstd, tmp)

                load(0)
                if NT > 1:
                    load(1)
                pre(0)
                for t in range(NT):
                    s = t % 3
                    ut = u[t % 2]
                    ukey = ('u', t % 2)
                    if t + 2 < NT:
                        load(t + 2)
                    for f in range(NF):
                        bk = banks[f % 4]
                        bkey = 'b%d' % (f % 4)
                        for k in range(8):
                            S.op('pe', lambda e, bk=bk, f=f, k=k: e.matmul(
                                bk[:, 0:TT], lhsT=wg[:, k, f * 128:(f + 1) * 128], rhs=ut[:, k, :], start=(k == 0), stop=(k == 7)),
                                reads=['wg', ukey], writes=[bkey])
                        for k in range(8):
                            S.op('pe', lambda e, bk=bk, f=f, k=k: e.matmul(
                                bk[:, 256:256 + TT], lhsT=wu[:, k, f * 128:(f + 1) * 128], rhs=ut[:, k, :], start=(k == 0), stop=(k == 7)),
                                reads=['wu', ukey], writes=[bkey])
                        S.op('act', lambda e, bk=bk, f=f: e.activation(out=st[f % 2][:], in_=bk[:, 0:TT], func=AF.Silu),
                             reads=[bkey], writes=[('st', f % 2)])
                        S.op('dve', lambda e, bk=bk, f=f: e.tensor_tensor(out=act[:, f, :], in0=st[f % 2][:], in1=bk[:, 256:256 + TT], op=ALU.mult),
                             reads=[bkey, ('st', f % 2)], writes=['act'])
                    if t + 1 < NT:
                        pre(t + 1)
                    for c in range(8):
                        bk = banks[4 + c % 3]
                        bkey = 'b%d' % (4 + c % 3)
                        for f in range(NF):
                            S.op('pe', lambda e, bk=bk, f=f, c=c: e.matmul(
                                bk[:, 0:TT], lhsT=wd[:, f, c * 128:(c + 1) * 128], rhs=act[:, f, :], start=(f == 0), stop=(f == NF - 1)),
                                reads=['wd', 'act'], writes=[bkey])
                        S.op('act', lambda e, bk=bk, c=c: e.activation(out=y[:, c, :], in_=bk[:, 0:TT], func=AF.Copy),
                             reads=[bkey], writes=['y'])
                    postnorm_resid(y, xt[s], ('x', s), sq, i_sub, TT, banks[7], 'b7', ln_t, rstd, tmp)
                    S.dma('sp', dstv[:, :, t * TT:(t + 1) * TT], xt[s][:], reads=[('x', s)], semkey=('dxo', s))
                S.barrier()

        def m1_phase():
            TT = 512
            NT = S_LEN // TT
            with ExitStack() as ph:
                win = sbt(ph, "win", [128, 8, 3072], BF16)
                xt = [sbt(ph, "m1x%d" % i, [128, 8, TT], F32) for i in range(2)]
                u2 = [sbt(ph, "m1u%d" % i, [128, 8, TT], BF16) for i in range(2)]
                sq = sbt(ph, "m1sq", [128, 8, TT], BF16)
                tmp = [sbt(ph, "m1tmp%d" % i, [128, TT], F32) for i in range(2)]
                ln_t = sbt(ph, "m1ln", [128, TT], F32)
                rstd = sbt(ph, "m1rstd", [128, TT], F32)
                stg = {}
                for nm in ('qa', 'ka', 'qd', 'kd', 'va', 'vd'):
                    stg[nm] = [sbt(ph, "stg_%s%d" % (nm, i), [128, 4, 512], BF16) for i in range(2)]
                load_w_bf16(win, w_in, 8, 3072, 'win')
                srcv = h1T.rearrange("(c p) t -> p c t", p=128)
                fm = [('qa', 0, 0.125, qaT), ('ka', 512, 1.0, kaT), ('qd', 1536, 0.125, qdT), ('kd', 2048, 1.0, kdT)]
                tm = [('va', 1024, vaS), ('vd', 2560, vdS)]
                S.dma('sp', xt[0][:], srcv[:, :, 0:TT], writes=[('x', 0)], semkey=('dx', 0))
                bi = 0
                for t in range(NT):
                    s = t % 2
                    if t + 1 < NT:
                        S.dma('sp', xt[1 - s][:], srcv[:, :, (t + 1) * TT:(t + 2) * TT], writes=[('x', 1 - s)], semkey=('dx', 1 - s))
                    if t == 0:
                        prenorm(xt[0], ('x', 0), u2[0], ('u', 0), sq, 1, TT, banks[7], 'b7', ln_t, rstd, tmp)
                    u = u2[s]
                    ukey = ('u', s)
                    for (nm, c0, scl, dram) in fm:
                        sg = stg[nm][t % 2]
                        skey = ('stg', nm, t % 2)
                        for j in range(4):
                            bk = banks[bi % 6]
                            bkey = 'b%d' % (bi % 6)
                            bi += 1
                            for k in range(8):
                                S.op('pe', lambda e, bk=bk, k=k, j=j, c0=c0: e.matmul(
                                    bk[:], lhsT=win[:, k, c0 + j * 128:c0 + (j + 1) * 128], rhs=u[:, k, :], start=(k == 0), stop=(k == 7)),
                                    reads=['win', ukey], writes=[bkey])
                            S.op('act', lambda e, bk=bk, j=j, sg=sg, scl=scl: e.activation(out=sg[:, j, :], in_=bk[:], func=AF.Copy, scale=scl),
                                 reads=[bkey], writes=[skey])
                        S.dma('sp', dram.rearrange("(c p) t -> p c t", p=128)[:, :, t * TT:(t + 1) * TT], sg[:],
                              reads=[skey], semkey=('dstg', nm, t % 2))
                    if t + 1 < NT:
                        prenorm(xt[1 - s], ('x', 1 - s), u2[1 - s], ('u', 1 - s), sq, 1, TT, banks[7], 'b7', ln_t, rstd, tmp)
                    for (nm, c0, dram) in tm:
                        sg = stg[nm][t % 2]
                        skey = ('stg', nm, t % 2)
                        for j in range(4):
                            bk = banks[bi % 6]
                            bkey = 'b%d' % (bi % 6)
                            bi += 1
                            for k in range(8):
                                S.op('pe', lambda e, bk=bk, k=k, j=j, c0=c0: e.matmul(
                                    bk[:], lhsT=u[:, k, j * 128:(j + 1) * 128], rhs=win[:, k, c0:c0 + 512], start=(k == 0), stop=(k == 7)),
                                    reads=['win', ukey], writes=[bkey])
                            S.op('dve', lambda e, bk=bk, j=j, sg=sg: e.tensor_copy(out=sg[:, j, :], in_=bk[:]),
                                 reads=[bkey], writes=[skey])
                        S.dma('sp', dram[t * TT:(t + 1) * TT, :].rearrange("(s p) f -> p s f", p=128), sg[:],
                              reads=[skey], semkey=('dstg', nm, t % 2))
                S.barrier()

        def sb_phase():
            with ExitStack() as ph:
                vsb = sbt(ph, "vsb", [128, NKT, 512], BF16)
                kT = [sbt(ph, "kT%d" % i, [128, S_LEN], BF16) for i in range(2)]
                qT = [sbt(ph, "qT%d" % i, [128, 512], BF16) for i in range(2)]
                onesw = sbt(ph, "onesw", [128, 512], BF16)
                msb = [sbt(ph, "msb%d" % j, [128, 512], BF16) for j in range(4)]
                e_t = [sbt(ph, "e_t%d" % i, [128, 1024], F32) for i in range(2)]
                sp_t = [sbt(ph, "sp_t%d" % i, [128, 1024], BF16) for i in range(3)]
                w_t = [sbt(ph, "w_t%d" % i, [128, 1024], BF16) for i in range(3)]
                cbf = [sbt(ph, "cbf%d" % i, [128, 512], BF16) for i in range(3)]
                ystg = [sbt(ph, "ystg%d" % i, [128, 512], BF16) for i in range(2)]
                S.op('pool', lambda e: e.memset(onesw[:], 1.0), writes=['onesw'])
                for b_ in range(2):
                    S.op('pool', lambda e, b_=b_: e.memset(kT[b_][64:128, :], 0.0), writes=[('kTz', b_)])
                    S.op('pool', lambda e, b_=b_: e.memset(qT[b_][64:128, :], 0.0), writes=[('qTz', b_)])
                for j in range(4):
                    S.op('pool', lambda e, j=j: e.affine_select(out=msb[j][:], in_=onesw[:], pattern=[[1, 512]],
                                                                 compare_op=ALU.is_ge, fill=0.0, base=-128 * j - 1,
                                                                 channel_multiplier=-1),
                         reads=['onesw'], writes=[('msb', j)])
                S.dma('sp', vsb[:], vaS.rearrange("(i p) f -> p i f", p=128), writes=['vsb'], semkey='d_vsb')
                pairs = []
                sidx = 0
                for h in range(8):
                    for Q in range(NQ):
                        top = 4 * Q + 3
                        for ia in range(top, -1, -2):
                            ib = ia - 1
                            pairs.append(dict(h=h, Q=Q, ia=ia, ib=ib, first=(ia == top), last=(ib == 0), sidx=sidx,
                                              ja=(ia - 4 * Q if ia >= 4 * Q else None),
                                              jb=(ib - 4 * Q if ib >= 4 * Q else None), p=len(pairs)))
                        sidx += 1
                YK, YKEY = banks[6], 'b6'
                CK, CKEY = banks[7], 'b7'

                def load_k(h):
                    S.dma('sp', kT[h % 2][0:64, :], kaT[h * 64:(h + 1) * 64, :], writes=[('kT', h % 2)], semkey=('dk', h % 2))

                def load_q(h, Q, sidx):
                    S.dma('sp', qT[sidx % 2][0:64, :], qaT[h * 64:(h + 1) * 64, Q * 512:(Q + 1) * 512],
                          writes=[('qT', sidx % 2)], semkey=('dq', sidx % 2))

                def P1(pr):
                    h, Q, p = pr['h'], pr['Q'], pr['p']
                    if pr['first']:
                        if Q == 0:
                            if h == 0:
                                load_k(0)
                                load_q(0, 0, 0)
                            if h + 1 < 8:
                                load_k(h + 1)
                        nh, nQ = (h, Q + 1) if Q + 1 < NQ else (h + 1, 0)
                        if nh < 8:
                            load_q(nh, nQ, pr['sidx'] + 1)
                    zp = pp[p % 3]
                    zkey = ('zp', p % 3)
                    kt = kT[h % 2]
                    qt = qT[pr['sidx'] % 2]
                    rd = [('kT', h % 2), ('qT', pr['sidx'] % 2), ('kTz', h % 2), ('qTz', pr['sidx'] % 2)]
                    for half, i in ((0, pr['ia']), (1, pr['ib'])):
                        S.op('pe', lambda e, half=half, i=i: e.matmul(zp[:, half * 512:(half + 1) * 512], lhsT=kt[:, i * 128:(i + 1) * 128],
                                                                      rhs=qt[:], start=True, stop=False),
                             reads=rd, writes=[zkey])

                def A1(pr):
                    p = pr['p']
                    zp = pp[p % 3]
                    zkey = ('zp', p % 3)
                    et = e_t[p % 2]
                    spt = sp_t[p % 3]
                    S.op('act', lambda e: e.activation(out=et[:], in_=zp[:], func=AF.Exp), reads=[zkey], writes=[('e', p % 2)])
                    S.op('act', lambda e: e.activation(out=spt[:], in_=et[:], func=AF.Ln, bias=1.0, scale=1.0),
                         reads=[('e', p % 2)], writes=[('sp', p % 3)])
                    for half, j in ((0, pr['ja']), (1, pr['jb'])):
                        if j is not None:
                            S.op('pool', lambda e, half=half, j=j: e.tensor_tensor(
                                out=spt[:, half * 512:(half + 1) * 512], in0=spt[:, half * 512:(half + 1) * 512], in1=msb[j][:], op=ALU.mult),
                                reads=[('sp', p % 3), ('msb', j)], writes=[('sp', p % 3)])

                def P2(pr):
                    p = pr['p']
                    zp = pp[p % 3]
                    zkey = ('zp', p % 3)
                    spt = sp_t[p % 3]
                    spk = ('sp', p % 3)
                    za, zb_ = zp[:, 0:512], zp[:, 512:1024]
                    spa, spb = spt[:, 0:512], spt[:, 512:1024]
                    pc = (p - 1) % 3
                    S.op('pe', lambda e: e.matmul(za, lhsT=negui[:], rhs=spa, start=False, stop=True, skip_group_check=True),
                         reads=['negui', spk, zkey], writes=[zkey])
                    if not pr['first']:
                        S.op('pe', lambda e: e.matmul(za, lhsT=negI[:], rhs=cbf[pc][:], start=False, stop=True, skip_group_check=True),
                             reads=['negI', ('cbf', pc), zkey], writes=[zkey])
                    S.op('pe', lambda e: e.matmul(zb_, lhsT=negui[:], rhs=spb, start=False, stop=True, skip_group_check=True),
                         reads=['negui', spk, zkey], writes=[zkey])
                    S.op('pe', lambda e: e.matmul(zb_, lhsT=negones[:], rhs=spa, start=False, stop=True, skip_group_check=True),
                         reads=['negones', spk, zkey], writes=[zkey])
                    if not pr['first']:
                        S.op('pe', lambda e: e.matmul(zb_, lhsT=negI[:], rhs=cbf[pc][:], start=False, stop=True, skip_group_check=True),
                             reads=['negI', ('cbf', pc), zkey], writes=[zkey])
                    if not pr['last']:
                        S.op('pe', lambda e: e.matmul(CK, lhsT=ones[:], rhs=spa, start=pr['first'], stop=False, skip_group_check=True),
                             reads=['ones', spk] + ([] if pr['first'] else [CKEY]), writes=[CKEY])
                        S.op('pe', lambda e: e.matmul(CK, lhsT=ones[:], rhs=spb, start=False, stop=True, skip_group_check=True),
                             reads=['ones', spk, CKEY], writes=[CKEY])
                        S.op('dve', lambda e: e.tensor_copy(out=cbf[p % 3][:], in_=CK), reads=[CKEY], writes=[('cbf', p % 3)])

                def A2(pr):
                    p = pr['p']
                    zp = pp[p % 3]
                    zkey = ('zp', p % 3)
                    wt = w_t[p % 3]
                    S.op('act', lambda e: e.activation(out=wt[:], in_=zp[:], func=AF.Exp), reads=[zkey], writes=[('w', p % 3)])
                    for half, j in ((0, pr['ja']), (1, pr['jb'])):
                        if j is not None:
                            S.op('pool', lambda e, half=half, j=j: e.tensor_tensor(
                                out=wt[:, half * 512:(half + 1) * 512], in0=wt[:, half * 512:(half + 1) * 512], in1=msb[j][:], op=ALU.mult),
                                reads=[('w', p % 3), ('msb', j)], writes=[('w', p % 3)])

                def P3(pr):
                    h, Q, p = pr['h'], pr['Q'], pr['p']
                    wt = w_t[p % 3]
                    hp = h // 2
                    ho = (h % 2) * 64
                    for half, i in ((0, pr['ia']), (1, pr['ib'])):
                        st_ = pr['first'] and half == 0
                        S.op('pe', lambda e, half=half, i=i, st_=st_: e.matmul(
                            YK, lhsT=vsb[:, i, hp * 128:(hp + 1) * 128], rhs=wt[:, half * 512:(half + 1) * 512],
                            start=st_, stop=(pr['last'] and half == 1), skip_group_check=True),
                            reads=['vsb', ('w', p % 3)] + ([] if st_ else [YKEY]), writes=[YKEY])
                    if pr['last']:
                        sl = pr['sidx'] % 2
                        S.op('dve', lambda e: e.tensor_copy(out=ystg[sl][ho:ho + 64, :], in_=YK[ho:ho + 64, :]), reads=[YKEY], writes=[('ystg', sl)])
                        S.dma('sp', yaT[h * 64:(h + 1) * 64, Q * 512:(Q + 1) * 512], ystg[sl][ho:ho + 64, :], reads=[('ystg', sl)], semkey=('dyo', sl))

                n = len(pairs)
                P1(pairs[0])
                for s in range(n + 2):
                    if 0 <= s - 1 < n:
                        P2(pairs[s - 1])
                    if s < n:
                        A1(pairs[s])
                    if 0 <= s - 1 < n:
                        A2(pairs[s - 1])
                    if s + 1 < n:
                        P1(pairs[s + 1])
                    if 0 <= s - 2 < n:
                        P3(pairs[s - 2])
                S.barrier()

        def da_phase():
            with ExitStack() as ph:
                vda = [sbt(ph, "vda%d" % i, [128, NKT, 128], BF16) for i in range(2)]
                kaug = [[sbt(ph, "kaug%d_%d" % (b, m), [128, S_LEN], BF16) for m in range(2)] for b in range(2)]
                qaug = [[[sbt(ph, "qaug%d_%d_%d" % (h, b, m), [128, 512], BF16) for m in range(2)] for b in range(2)] for h in range(4)]
                fullb = [[sbt(ph, "fb%d_%d" % (h, j), [128, 512], F32) for j in range(4)] for h in range(4)]
                bvi = sbt(ph, "bvi", [128, 64], F32)
                bv = [sbt(ph, "bv%d" % h, [128, 64], F32) for h in range(4)]
                augf = sbt(ph, "augf", [128, 512], F32)
                dtmp = sbt(ph, "dtmp", [128, 512], F32)
                dtmp2 = sbt(ph, "dtmp2", [128, 512], F32)
                arg_t = [sbt(ph, "arg%d" % i, [128, 1024], F32) for i in range(2)]
                w_t = [sbt(ph, "dw%d" % i, [128, 1024], BF16) for i in range(3)]
                wsum_t = [sbt(ph, "wsum%d" % m, [128, 512], F32) for m in range(2)]
                ones32 = sbt(ph, "ones32", [128, 128], F32)
                S.op('pool', lambda e: e.memset(ones32[:], 1.0), writes=['ones32'])
                r_t = [sbt(ph, "dr%d" % i, [128, 512], F32) for i in range(2)]
                t_t = [sbt(ph, "dt%d" % i, [128, 512], F32) for i in range(2)]
                yc_t = [sbt(ph, "dyc%d" % i, [128, 512], F32) for i in range(2)]
                yo = [sbt(ph, "dyo%d" % i, [128, 512], BF16) for i in range(2)]
                S.op('pool', lambda e: e.iota(out=augf[64:96, :], pattern=[[2, 256], [0, 2]], base=0, channel_multiplier=0,
                                              allow_small_or_imprecise_dtypes=True), writes=['augfa'])
                S.op('pool', lambda e: e.iota(out=augf[96:128, :], pattern=[[0, 256], [1, 2]], base=0, channel_multiplier=0,
                                              allow_small_or_imprecise_dtypes=True), writes=['augfb'])
                for h in range(4):
                    for b in range(2):
                        for m in range(2):
                            S.op('dve', lambda e, h=h, b=b, m=m: e.tensor_scalar(
                                out=qaug[h][b][m][64:128, :], in0=augf[64:128, :], scalar1=-SLOPES[h], scalar2=None, op0=ALU.mult),
                                reads=['augfa', 'augfb'], writes=[('qaug_aug', h, b, m)])
                for b in range(2):
                    for m in range(2):
                        ka = kaug[b][m]
                        S.op('pool', lambda e, ka=ka: e.memset(ka[64:128, :], 0.0), writes=[('kaug_aug', b, m)])
                        S.op('pool', lambda e, ka=ka: e.memset(ka[64:65, :], 1.0), reads=[('kaug_aug', b, m)], writes=[('kaug_aug', b, m)])
                        S.op('pool', lambda e, ka=ka: e.memset(ka[96:97, :], 1.0), reads=[('kaug_aug', b, m)], writes=[('kaug_aug', b, m)])
                S.op('pool', lambda e: e.iota(out=bvi[:], pattern=[[-128, 64]], base=-128, channel_multiplier=1,
                                              allow_small_or_imprecise_dtypes=True), writes=['bvi'])
                for h in range(4):
                    S.op('dve', lambda e, h=h: e.tensor_scalar(out=bv[h][:], in0=bvi[:], scalar1=SLOPES[h], scalar2=None, op0=ALU.mult),
                         reads=['bvi'], writes=[('bv', h)])
                ttf = sbt(ph, "ttf", [128, 512], F32)
                S.op('pool', lambda e: e.iota(out=ttf[:], pattern=[[1, 512]], base=0, channel_multiplier=0,
                                              allow_small_or_imprecise_dtypes=True), writes=['ttf'])
                for j in range(4):
                    S.op('pool', lambda e, j=j: e.iota(out=dtmp[:], pattern=[[-1, 512]], base=128 * j, channel_multiplier=1,
                                                       allow_small_or_imprecise_dtypes=True), writes=['dtmp'])
                    for h in range(4):
                        S.op('dve', lambda e, h=h: e.tensor_scalar(out=dtmp2[:], in0=dtmp[:], scalar1=SLOPES[h], scalar2=None,
                                                                   op0=ALU.mult),
                             reads=['dtmp'], writes=['dtmp2'])
                        S.op('dve', lambda e, h=h: e.scalar_tensor_tensor(out=dtmp2[:], in0=dtmp[:], scalar=-SLOPES[h], in1=dtmp2[:],
                                                                          op0=ALU.mult, op1=ALU.min),
                             reads=['dtmp', 'dtmp2'], writes=['dtmp2'])
                        S.op('dve', lambda e, h=h: e.scalar_tensor_tensor(out=dtmp2[:], in0=ttf[:], scalar=SLOPES[h], in1=dtmp2[:],
                                                                          op0=ALU.mult, op1=ALU.add),
                             reads=['dtmp2', 'ttf'], writes=['dtmp2'])
                        S.op('pool', lambda e, h=h, j=j: e.affine_select(
                            out=fullb[h][j][:], in_=dtmp2[:], pattern=[[64, 8], [0, 64]], compare_op=ALU.is_ge, fill=NEGBIG,
                            base=63 - 128 * j, channel_multiplier=-1), reads=['dtmp2'], writes=[('fullb', h, j)])

                items = []
                sidx = 0
                for h in range(4):
                    for Q in range(NQ):
                        top = 4 * Q + 3
                        for i in range(0, top + 1):
                            items.append(dict(h=h, Q=Q, i=i, first=(i == 0), last=(i == top), sidx=sidx,
                                              diag=(i - 4 * Q if i >= 4 * Q else None), idx=len(items)))
                        sidx += 1

                vdv = vdS.rearrange("(i p) f -> p i f", p=128)

                def load_k(h):
                    for m in range(2):
                        S.dma('sp', kaug[h % 2][m][0:64, :], kdT[h * 128 + m * 64:h * 128 + (m + 1) * 64, :],
                              writes=[('kaug', h % 2, m)], semkey=('dkd', h % 2, m))

                def load_v(h):
                    S.dma('sp', vda[h % 2][:], vdv[:, :, h * 128:(h + 1) * 128], writes=[('vda', h % 2)], semkey=('dvd', h % 2))

                def load_q(h, Q, sidx):
                    for m in range(2):
                        S.dma('sp', qaug[h][sidx % 2][m][0:64, :], qdT[h * 128 + m * 64:h * 128 + (m + 1) * 64, Q * 512:(Q + 1) * 512],
                              writes=[('qaug', h, sidx % 2, m)], semkey=('dqd', sidx % 2, m))

                YB = [(banks[4], 'b4'), (banks[5], 'b5')]
                LB = [(banks[6], 'b6'), (banks[7], 'b7')]

                def stA(it):
                    h, Q, i, idx = it['h'], it['Q'], it['i'], it['idx']
                    if it['first']:
                        if Q == 0:
                            if h == 0:
                                load_k(0)
                                load_v(0)
                                load_q(0, 0, 0)
                            if h + 1 < 4:
                                load_k(h + 1)
                        nh, nQ = (h, Q + 1) if Q + 1 < NQ else (h + 1, 0)
                        if nh < 4:
                            load_q(nh, nQ, it['sidx'] + 1)
                    sb_ = it['sidx'] % 2
                    for m in range(2):
                        bn = (idx % 2) * 2 + m
                        bk, bkey = banks[bn], 'b%d' % bn
                        ka = kaug[h % 2][m]
                        qa = qaug[h][sb_][m]
                        S.op('pe', lambda e, bk=bk, ka=ka, qa=qa: e.matmul(bk[:], lhsT=ka[:, i * 128:(i + 1) * 128], rhs=qa[:], start=True, stop=True),
                             reads=[('kaug', h % 2, m), ('kaug_aug', h % 2, m), ('qaug', h, sb_, m), ('qaug_aug', h, sb_, m)], writes=[bkey])

                def stA2(it):
                    h, Q, i, idx = it['h'], it['Q'], it['i'], it['idx']
                    pi = idx % 2
                    zp = pp[pi]
                    bkeys = ['b%d' % (pi * 2), 'b%d' % (pi * 2 + 1)]
                    wt = w_t[idx % 3]
                    wkey = ('w', idx % 3)
                    if it['diag'] is None:
                        n_idx = 4 * Q - i - 1
                        S.op('act', lambda e: e.activation(out=wt[:], in_=zp[:], func=AF.Exp, bias=bv[h][:, n_idx:n_idx + 1], scale=1.0),
                             reads=bkeys + [('bv', h)], writes=[wkey])
                    else:
                        j = it['diag']
                        at = arg_t[pi]
                        for m in range(2):
                            S.op('dve', lambda e, m=m: e.tensor_tensor(out=at[:, m * 512:(m + 1) * 512], in0=zp[:, m * 512:(m + 1) * 512],
                                                                       in1=fullb[h][j][:], op=ALU.add),
                                 reads=[bkeys[m], ('fullb', h, j)], writes=[('arg', pi)])
                        S.op('act', lambda e: e.activation(out=wt[:], in_=at[:], func=AF.Exp), reads=[('arg', pi)], writes=[wkey])

                def stB(it):
                    h, Q, i, idx = it['h'], it['Q'], it['i'], it['idx']
                    if it['first'] and Q == 0 and h + 1 < 4:
                        load_v(h + 1)
                    wt = w_t[idx % 3]
                    wkey = ('w', idx % 3)
                    sl = it['sidx'] % 2
                    for m in range(2):
                        yk, ykey = YB[m]
                        S.op('pe', lambda e, yk=yk, m=m: e.matmul(yk, lhsT=vda[h % 2][:, i, :], rhs=wt[:, m * 512:(m + 1) * 512],
                                                               start=it['first'], stop=it['last']),
                             reads=[('vda', h % 2), wkey] + ([] if it['first'] else [ykey]), writes=[ykey])
                    lk, lkey = LB[0]
                    S.op('pe', lambda e, lk=lk: e.matmul(lk, lhsT=ones[:], rhs=wt[:, 0:512], start=it['first'], stop=it['last']),
                         reads=['ones', wkey] + ([] if it['first'] else [lkey]), writes=[lkey])
                    lk, lkey = LB[1]
                    if it['first']:
                        S.op('dve', lambda e, lk=lk: e.tensor_copy(out=lk, in_=wt[:, 512:1024]), reads=[wkey], writes=[lkey])
                    else:
                        S.op('dve', lambda e, lk=lk: e.tensor_tensor(out=lk, in0=lk, in1=wt[:, 512:1024], op=ALU.add),
                             reads=[wkey, lkey], writes=[lkey])
                    if it['last']:
                        S.op('act', lambda e, lk=lk: e.activation(out=wsum_t[1][:], in_=lk, func=AF.Copy), reads=[lkey], writes=[('wsum', 1)])
                        S.op('pe', lambda e, lk=lk: e.matmul(lk, lhsT=ones32[:], rhs=wsum_t[1][:], start=True, stop=True),
                             reads=['ones32', ('wsum', 1)], writes=[lkey])
                    if it['last']:
                        for m in range(2):
                            S.op('act', lambda e, m=m: e.activation(out=yc_t[m][:], in_=YB[m][0], func=AF.Copy), reads=[YB[m][1]], writes=[('yc', m)])
                            S.op('dve', lambda e, m=m: e.reciprocal(out=r_t[m][:], in_=LB[m][0]), reads=[LB[m][1]], writes=[('r', m)])
                        for m in range(2):
                            S.op('dve', lambda e, m=m: e.tensor_tensor(out=t_t[m][:], in0=yc_t[m][:], in1=r_t[m][:], op=ALU.mult),
                                 reads=[('yc', m), ('r', m)], writes=[('t', m)])
                        S.op('dve', lambda e: e.scalar_tensor_tensor(out=yo[sl][:], in0=t_t[1][:], scalar=neglam[:, 0:1], in1=t_t[0][:],
                                                                     op0=ALU.mult, op1=ALU.add),
                             reads=[('t', 0), ('t', 1), 'neglam'], writes=[('yo', sl)])
                        S.dma('sp', ydT[h * 128:(h + 1) * 128, Q * 512:(Q + 1) * 512], yo[sl][:], reads=[('yo', sl)], semkey=('dydo', sl))

                n = len(items)
                for s in range(n + 2):
                    if 0 <= s - 2 < n:
                        stB(items[s - 2])
                    if 0 <= s - 1 < n:
                        stA2(items[s - 1])
                    if s < n:
                        stA(items[s])
                S.barrier()

        def m3_phase():
            TT = 512
            NT = S_LEN // TT
            with ExitStack() as ph:
                wsb = sbt(ph, "m3wsb", [128, 4, D], BF16)
                wda = sbt(ph, "m3wda", [128, 4, D], BF16)
                wo = sbt(ph, "m3wo", [128, 8, D], BF16)
                wgt = sbt(ph, "m3wgt", [128, 8, 2048], BF16)
                xt = [sbt(ph, "m3x%d" % i, [128, 8, TT], F32) for i in range(2)]
                yat = [sbt(ph, "m3ya%d" % i, [128, 4, TT], BF16) for i in range(2)]
                ydt = [sbt(ph, "m3yd%d" % i, [128, 4, TT], BF16) for i in range(2)]
                u2 = [sbt(ph, "m3u%d" % i, [128, 8, TT], BF16) for i in range(2)]
                sq = sbt(ph, "m3sq", [128, 8, TT], BF16)
                mg = sbt(ph, "m3mg", [128, 8, TT], BF16)
                sqy = sbt(ph, "m3sqy", [128, 4, TT], BF16)
                ydn = [sbt(ph, "m3ydn%d" % i, [128, 4, TT], BF16) for i in range(2)]
                ln2 = [sbt(ph, "m3ln2%d" % i, [128, TT], F32) for i in range(2)]
                rs2 = [sbt(ph, "m3rs2%d" % i, [128, TT], F32) for i in range(2)]
                y = sbt(ph, "m3y", [128, 8, TT], F32)
                sg_t = [sbt(ph, "m3sg%d" % i, [128, TT], F32) for i in range(4)]
                mm_t = [sbt(ph, "m3mm%d" % i, [128, TT], F32) for i in range(4)]
                tmp = [sbt(ph, "m3tmp%d" % i, [128, TT], F32) for i in range(2)]
                ln_t = sbt(ph, "m3ln", [128, TT], F32)
                rstd = sbt(ph, "m3rstd", [128, TT], F32)
                load_w_bf16(wsb, w_bsb, 4, D, 'wsb')
                load_w_bf16(wda, w_bda, 4, D, 'wda')
                load_w_bf16(wgt, w_in, 8, 2048, 'wgt', col0=3072)
                load_w_bf16(wo, w_out, 8, D, 'wo')
                srcv = h1T.rearrange("(c p) t -> p c t", p=128)
                dstv = h2T.rearrange("(c p) t -> p c t", p=128)
                yav = yaT.rearrange("(c p) t -> p c t", p=128)
                ydv = ydT.rearrange("(c p) t -> p c t", p=128)

                def load(t):
                    s = t % 2
                    S.dma('sp', xt[s][:], srcv[:, :, t * TT:(t + 1) * TT], writes=[('x', s)], semkey=('dx', s))
                    S.dma('sp', yat[s][:], yav[:, :, t * TT:(t + 1) * TT], writes=[('ya', s)], semkey=('dya', s))
                    S.dma('sp', ydt[s][:], ydv[:, :, t * TT:(t + 1) * TT], writes=[('yd', s)], semkey=('dyd', s))

                def pre_sq(t):
                    s_ = t % 2
                    S.op('dve', lambda e: e.tensor_tensor(out=sq[:], in0=xt[s_][:], in1=xt[s_][:], op=ALU.mult), reads=[('x', s_)], writes=['sq'])
                    S.op('dve', lambda e: e.tensor_tensor(out=sqy[:], in0=ydt[s_][:], in1=ydt[s_][:], op=ALU.mult),
                         reads=[('yd', s_)], writes=['sqy'])

                def pre(t):
                    prenorm(xt[t % 2], ('x', t % 2), u2[t % 2], ('u', t % 2), sq, 1, TT, banks[7], 'b7', ln_t, rstd, tmp, do_sq=False)
                    s_ = t % 2
                    for k in range(4):
                        bk, bkey = banks[6 + k % 2], 'b%d' % (6 + k % 2)
                        S.op('pe', lambda e, bk=bk, k=k: e.matmul(bk, lhsT=ones[:], rhs=sqy[:, k, :], start=True, stop=True),
                             reads=['ones', 'sqy'], writes=[bkey])
                        S.op('act', lambda e, bk=bk, k=k: e.activation(out=ln2[k % 2][:], in_=bk, func=AF.Ln, bias=EPS, scale=1.0 / 128),
                             reads=[bkey], writes=[('ln2', k % 2)])
                        S.op('act', lambda e, k=k: e.activation(out=rs2[k % 2][:], in_=ln2[k % 2][:], func=AF.Exp, scale=-0.5),
                             reads=[('ln2', k % 2)], writes=[('rs2', k % 2)])
                        S.op('dve', lambda e, k=k: e.scalar_tensor_tensor(out=ydn[s_][:, k, :], in0=ydt[s_][:, k, :], scalar=sub8[:, 0:1],
                                                                          in1=rs2[k % 2][:], op0=ALU.mult, op1=ALU.mult),
                             reads=[('yd', s_), ('rs2', k % 2), 'sub8'], writes=[('ydn', s_)])

                load(0)
                pre_sq(0)
                pre(0)
                for t in range(NT):
                    s = t % 2
                    u = u2[s]
                    ukey = ('u', s)
                    if t + 1 < NT:
                        load(t + 1)
                    for c in range(8):
                        if c == 4 and t + 1 < NT:
                            pre_sq(t + 1)
                        cs_ = slice(c * 128, (c + 1) * 128)
                        b0 = (c % 2) * 4
                        bA, bB, bGa, bGd = banks[b0], banks[b0 + 1], banks[b0 + 2], banks[b0 + 3]
                        kA, kB, kGa, kGd = ['b%d' % (b0 + i) for i in range(4)]
                        r0 = (c % 2) * 2
                        for k in range(4):
                            S.op('pe', lambda e, k=k, cs_=cs_, bA=bA: e.matmul(bA[:], lhsT=wsb[:, k, cs_], rhs=yat[s][:, k, :], start=(k == 0), stop=(k == 3)),
                                 reads=['wsb', ('ya', s)], writes=[kA])
                        for k in range(4):
                            S.op('pe', lambda e, k=k, cs_=cs_, bB=bB: e.matmul(bB[:], lhsT=wda[:, k, cs_], rhs=ydn[s][:, k, :], start=(k == 0), stop=(k == 3)),
                                 reads=['wda', ('ydn', s)], writes=[kB])
                        for k in range(8):
                            S.op('pe', lambda e, k=k, c=c, bGa=bGa: e.matmul(bGa[:], lhsT=wgt[:, k, c * 128:(c + 1) * 128], rhs=u[:, k, :], start=(k == 0), stop=(k == 7)),
                                 reads=['wgt', ukey], writes=[kGa])
                        for k in range(8):
                            S.op('pe', lambda e, k=k, c=c, bGd=bGd: e.matmul(bGd[:], lhsT=wgt[:, k, 1024 + c * 128:1024 + (c + 1) * 128], rhs=u[:, k, :], start=(k == 0), stop=(k == 7)),
                                 reads=['wgt', ukey], writes=[kGd])
                        S.op('act', lambda e, bGa=bGa, r0=r0: e.activation(out=sg_t[r0][:], in_=bGa[:], func=AF.Sigmoid), reads=[kGa], writes=[('sg', r0)])
                        S.op('act', lambda e, bGd=bGd, r0=r0: e.activation(out=sg_t[r0 + 1][:], in_=bGd[:], func=AF.Sigmoid), reads=[kGd], writes=[('sg', r0 + 1)])
                        S.op('dve', lambda e, bA=bA, r0=r0: e.tensor_tensor(out=mm_t[r0][:], in0=bA[:], in1=sg_t[r0][:], op=ALU.mult),
                             reads=[kA, ('sg', r0)], writes=[('mm', r0)])
                        S.op('dve', lambda e, bB=bB, r0=r0: e.tensor_tensor(out=mm_t[r0 + 1][:], in0=bB[:], in1=sg_t[r0 + 1][:], op=ALU.mult),
                             reads=[kB, ('sg', r0 + 1)], writes=[('mm', r0 + 1)])
                        S.op('dve', lambda e, c=c, r0=r0: e.tensor_tensor(out=mg[:, c, :], in0=mm_t[r0][:], in1=mm_t[r0 + 1][:], op=ALU.add),
                             reads=[('mm', r0), ('mm', r0 + 1)], writes=['mg'])
                    if t + 1 < NT:
                        pre(t + 1)
                    for c in range(8):
                        bk = banks[c % 6]
                        bkey = 'b%d' % (c % 6)
                        for k in range(8):
                            S.op('pe', lambda e, bk=bk, k=k, c=c: e.matmul(bk[:], lhsT=wo[:, k, c * 128:(c + 1) * 128], rhs=mg[:, k, :], start=(k == 0), stop=(k == 7)),
                                 reads=['wo', 'mg'], writes=[bkey])
                        S.op('act', lambda e, bk=bk, c=c: e.activation(out=y[:, c, :], in_=bk[:], func=AF.Copy), reads=[bkey], writes=['y'])
                    postnorm_resid(y, xt[s], ('x', s), sq, 1, TT, banks[7], 'b7', ln_t, rstd, tmp)
                    S.dma('sp', dstv[:, :, t * TT:(t + 1) * TT], xt[s][:], reads=[('x', s)], semkey=('dxo', s))
                S.barrier()

        ffn_phase(xT, h1T, ffn_w[0], 0, "a")
        m1_phase()
        sb_phase()
        da_phase()
        m3_phase()
        ffn_phase(h2T, outT, ffn_w[1], 2, "b")
    return nc


_CACHE = {}


def _layout_inputs(inp, b):
    f = lambda a: np.ascontiguousarray(np.asarray(a, dtype=np.float32))
    d = {}
    d["xT"] = f(np.asarray(inp["x"])[b].T)
    d["cT"] = f(np.asarray(inp["c"])[b].reshape(8, 128).T)
    d["w_ada"] = f(inp["w_ada"][0])
    d["b_ada_r"] = f(np.asarray(inp["b_ada"])[0].reshape(72, 128).T)
    d["npre_r"] = f(np.asarray(inp["norm_pre"])[0].reshape(24, 128).T)
    d["npost_r"] = f(np.asarray(inp["norm_post"])[0].reshape(24, 128).T)
    d["f1_wg"] = f(inp["ffn1_w_gate"][0])
    d["f1_wu"] = f(inp["ffn1_w_up"][0])
    d["f1_wd"] = f(inp["ffn1_w_down"][0])
    d["f2_wg"] = f(inp["ffn2_w_gate"][0])
    d["f2_wu"] = f(inp["ffn2_w_up"][0])
    d["f2_wd"] = f(inp["ffn2_w_down"][0])
    d["w_in"] = f(inp["w_in"][0])
    lam = np.concatenate([np.asarray(inp["da_lambda_q1"])[0], np.asarray(inp["da_lambda_k1"])[0],
                          np.asarray(inp["da_lambda_q2"])[0], np.asarray(inp["da_lambda_k2"])[0]])
    d["lam_r"] = f(np.broadcast_to(lam[None, :], (128, 256)))
    d["subln_r"] = f(np.asarray(inp["da_subln"])[0].reshape(128, 1))
    d["w_bsb"] = f(inp["w_branch_sb"][0])
    d["w_bda"] = f(inp["w_branch_da"][0])
    d["w_out"] = f(inp["w_out"][0])
    return d


def kernel(**inputs):
    x = np.asarray(inputs["x"])
    B, S_LEN, _ = x.shape
    if S_LEN not in _CACHE:
        _CACHE[S_LEN] = build_program(S_LEN)
    nc = _CACHE[S_LEN]
    in_maps = [_layout_inputs(inputs, b) for b in range(B)]
    res = run_bass_kernel_spmd(nc, in_maps, core_ids=list(range(B)))
    out = np.empty((B, S_LEN, D), dtype=np.float32)
    for b in range(B):
        out[b] = np.asarray(res.results[b]["outT"]).T
    return out
```

```python
import math
import numpy as np
from contextlib import ExitStack
import concourse.bass as bass
import concourse.mybir as mybir
from concourse.bass_utils import run_bass_kernel_spmd

F32 = mybir.dt.float32
BF16 = mybir.dt.bfloat16
AF = mybir.ActivationFunctionType
ALU = mybir.AluOpType
AX = mybir.AxisListType

D = 1024
DFF = 2816
NF = DFF // 128
EPS = 1e-6
LAMBDA_INIT = 0.8 - 0.6 * math.exp(-0.3 * 0)
SLOPES = [2.0 ** (-8.0 * (h + 1) / 4) for h in range(4)]
NEGBIG = -30000.0


class Sched:
    EPOCH = 30000

    def __init__(self, nc, es):
        self.nc = nc
        self.es = es
        self.engs = {'pe': nc.tensor, 'act': nc.scalar, 'dve': nc.vector, 'pool': nc.gpsimd, 'sp': nc.sync}
        self.cnt = {k: 0 for k in self.engs}
        self.last = {}
        self.sems = {}
        self.semval = {}
        self.waited = {}
        self.res = {}
        self.nsem = 0
        self.dmakeys = set()

    def sem(self, key):
        if key not in self.sems:
            self.sems[key] = self.es.enter_context(self.nc.semaphore("s%d" % self.nsem))
            self.nsem += 1
            self.semval[key] = 0
        return self.sems[key]

    def _deps(self, eng, reads, writes):
        deps = {}

        def add(tok, kind):
            if tok is None:
                return
            k, v = tok
            if isinstance(k, tuple) and k[0] == 'E' and k[1] == eng:
                if eng == 'pe' or kind == 'war':
                    return
            if v > deps.get(k, 0):
                deps[k] = v
        for r in reads:
            st = self.res.get(r)
            if st:
                add(st[0], 'raw')
        for w in writes:
            st = self.res.get(w)
            if st:
                add(st[0], 'waw')
                for t in st[1].values():
                    add(t, 'war')
        for k, v in deps.items():
            if self.waited.get((eng, k), 0) < v:
                self.waited[(eng, k)] = v
                self.engs[eng].wait_ge(self.sems[k], v)

    def _commit(self, tok, reads, writes):
        for r in reads:
            st = self.res.setdefault(r, [None, {}])
            st[1][tok[0]] = tok
        for w in writes:
            self.res[w] = [tok, {}]

    def op(self, eng, fn, reads=(), writes=()):
        self._deps(eng, reads, writes)
        self.cnt[eng] += 1
        key = ('E', eng, self.cnt[eng] // self.EPOCH)
        s = self.sem(key)
        self.semval[key] += 1
        tok = (key, self.semval[key])
        fn(self.engs[eng]).then_inc(s, 1)
        self.last[eng] = tok
        self._commit(tok, reads, writes)
        return tok

    def dma(self, eng, out, in_, reads=(), writes=(), semkey=None, **kw):
        self._deps(eng, reads, writes)
        s = self.sem(semkey)
        self.dmakeys.add(semkey)
        self.semval[semkey] += 16
        tok = (semkey, self.semval[semkey])
        self.engs[eng].dma_start(out=out, in_=in_, **kw).then_inc(s, 16)
        self._commit(tok, reads, writes)
        return tok

    def barrier(self):
        for e in self.engs:
            for f, tok in self.last.items():
                if f == e:
                    continue
                k, v = tok
                if self.waited.get((e, k), 0) < v:
                    self.waited[(e, k)] = v
                    self.engs[e].wait_ge(self.sems[k], v)
            for k in self.dmakeys:
                v = self.semval[k]
                if v and self.waited.get((e, k), 0) < v:
                    self.waited[(e, k)] = v
                    self.engs[e].wait_ge(self.sems[k], v)
        self.res = {}


def build_program(S_LEN, debug=False):
    nc = bass.Bass("TRN2", target_bir_lowering=False)
    NQ = S_LEN // 512
    NKT = S_LEN // 128

    def din(name, shape, dt=F32):
        return nc.dram_tensor(name, shape, dt, kind="ExternalInput").ap()

    def dscr(name, shape, dt):
        return nc.dram_tensor(name, shape, dt, kind=("ExternalOutput" if debug else "Internal")).ap()

    xT = din("xT", [D, S_LEN])
    cT = din("cT", [128, 8])
    w_ada = din("w_ada", [D, 9 * D])
    b_ada_r = din("b_ada_r", [128, 72])
    npre_r = din("npre_r", [128, 24])
    npost_r = din("npost_r", [128, 24])
    ffn_w = []
    for i in (1, 2):
        ffn_w.append((din("f%d_wg" % i, [D, DFF]), din("f%d_wu" % i, [D, DFF]), din("f%d_wd" % i, [DFF, D])))
    w_in = din("w_in", [D, 5120])
    lam_r = din("lam_r", [128, 256])
    subln_r = din("subln_r", [128, 1])
    w_bsb = din("w_bsb", [512, D])
    w_bda = din("w_bda", [512, D])
    w_out = din("w_out", [D, D])
    outT = nc.dram_tensor("outT", [D, S_LEN], F32, kind="ExternalOutput").ap()

    h1T = dscr("h1T", [D, S_LEN], F32)
    h2T = dscr("h2T", [D, S_LEN], F32)
    qaT = dscr("qaT", [512, S_LEN], BF16)
    kaT = dscr("kaT", [512, S_LEN], BF16)
    vaS = dscr("vaS", [S_LEN, 512], BF16)
    qdT = dscr("qdT", [512, S_LEN], BF16)
    kdT = dscr("kdT", [512, S_LEN], BF16)
    vdS = dscr("vdS", [S_LEN, 512], BF16)
    yaT = dscr("yaT", [512, S_LEN], BF16)
    ydT = dscr("ydT", [512, S_LEN], BF16)

    with ExitStack() as es:
        S = Sched(nc, es)

        def sbt(st, name, shape, dt):
            return st.enter_context(nc.sbuf_tensor(name, shape, dt))

        pp = [es.enter_context(nc.psum_tensor("pp%d" % i, [128, 1024], F32)) for i in range(4)]
        banks = []
        for i in range(4):
            banks.append(pp[i][:, 0:512])
            banks.append(pp[i][:, 512:1024])

        ones = sbt(es, "ones", [128, 128], BF16)
        negones = sbt(es, "negones", [128, 128], BF16)
        negui = sbt(es, "negui", [128, 128], BF16)
        negI = sbt(es, "negI", [128, 128], BF16)
        modsb = sbt(es, "modsb", [128, 72], F32)
        Acoef = sbt(es, "Acoef", [128, 24], F32)
        Gcoef = sbt(es, "Gcoef", [128, 24], F32)
        neglam = sbt(es, "neglam", [128, 1], F32)
        sub8 = sbt(es, "sub8", [128, 1], F32)

        S.op('pool', lambda e: e.memset(ones[:], 1.0), writes=['ones'])
        S.op('pool', lambda e: e.memset(negones[:], -1.0), writes=['negones'])
        S.op('pool', lambda e: e.affine_select(out=negui[:], in_=negones[:], pattern=[[-1, 128]],
                                               compare_op=ALU.is_ge, fill=0.0, base=0, channel_multiplier=1),
             reads=['negones'], writes=['negui'])
        S.op('pool', lambda e: e.affine_select(out=negI[:], in_=negones[:], pattern=[[-1, 128]],
                                               compare_op=ALU.is_equal, fill=0.0, base=0, channel_multiplier=1),
             reads=['negones'], writes=['negI'])

        with ExitStack() as ph:
            cs = sbt(ph, "cs", [128, 8], F32)
            cs_in = sbt(ph, "cs_in", [128, 8], F32)
            bada = sbt(ph, "bada", [128, 72], F32)
            npre = sbt(ph, "npre", [128, 24], F32)
            npost = sbt(ph, "npost", [128, 24], F32)
            lamt = sbt(ph, "lamt", [128, 256], F32)
            lprod = sbt(ph, "lprod", [128, 128], F32)
            lsum = sbt(ph, "lsum", [128, 2], F32)
            lexp = sbt(ph, "lexp", [128, 2], F32)
            subt = sbt(ph, "subt", [128, 1], F32)
            GW = 1152
            wa = [sbt(ph, "wa%d" % i, [128, 8, GW], F32) for i in range(2)]
            S.dma('sp', cs_in[:], cT, writes=['cs_in'], semkey='d_c')
            S.dma('sp', bada[:], b_ada_r, writes=['bada'], semkey='d_b')
            S.dma('sp', npre[:], npre_r, writes=['npre'], semkey='d_np')
            S.dma('sp', npost[:], npost_r, writes=['npost'], semkey='d_npo')
            S.dma('sp', lamt[:], lam_r, writes=['lamt'], semkey='d_lam')
            S.dma('sp', subt[:], subln_r, writes=['subt'], semkey='d_sub')
            S.op('act', lambda e: e.activation(out=cs[:], in_=cs_in[:], func=AF.Silu), reads=['cs_in'], writes=['cs'])
            w_ada_v = w_ada.rearrange("(kc p) n -> p kc n", p=128)
            modps = banks[0]
            for g in range(8):
                b = g % 2
                S.dma('sp', wa[b][:], w_ada_v[:, :, g * GW:(g + 1) * GW], writes=[('wa', b)], semkey=('d_wa', b))
                for jj in range(9):
                    j = g * 9 + jj
                    for kc in range(8):
                        S.op('pe', lambda e, b=b, jj=jj, kc=kc, j=j: e.matmul(
                            modps[:, j:j + 1], lhsT=wa[b][:, kc, jj * 128:(jj + 1) * 128], rhs=cs[:, kc:kc + 1],
                            start=(kc == 0), stop=(kc == 7)),
                            reads=[('wa', b), 'cs'], writes=['modps'])
            S.op('dve', lambda e: e.tensor_tensor(out=modsb[:], in0=modps[:, 0:72], in1=bada[:], op=ALU.add),
                 reads=['modps', 'bada'], writes=['modsb'])
            for i in range(3):
                rw = 1.0 if i == 1 else 0.5
                S.op('dve', lambda e, i=i: e.scalar_tensor_tensor(
                    out=Acoef[:, i * 8:(i + 1) * 8], in0=modsb[:, (i * 3 + 1) * 8:(i * 3 + 2) * 8], scalar=1.0,
                    in1=npre[:, i * 8:(i + 1) * 8], op0=ALU.add, op1=ALU.mult),
                    reads=['modsb', 'npre'], writes=['Acoef'])
                S.op('dve', lambda e, i=i, rw=rw: e.scalar_tensor_tensor(
                    out=Gcoef[:, i * 8:(i + 1) * 8], in0=modsb[:, (i * 3 + 2) * 8:(i * 3 + 3) * 8], scalar=rw,
                    in1=npost[:, i * 8:(i + 1) * 8], op0=ALU.mult, op1=ALU.mult),
                    reads=['modsb', 'npost'], writes=['Gcoef'])
            S.op('dve', lambda e: e.tensor_tensor(out=lprod[:, 0:64], in0=lamt[:, 0:64], in1=lamt[:, 64:128], op=ALU.mult),
                 reads=['lamt'], writes=['lprod'])
            S.op('dve', lambda e: e.tensor_tensor(out=lprod[:, 64:128], in0=lamt[:, 128:192], in1=lamt[:, 192:256], op=ALU.mult),
                 reads=['lamt'], writes=['lprod2'])
            S.op('dve', lambda e: e.reduce_sum(out=lsum[:, 0:1], in_=lprod[:, 0:64], axis=AX.X),
                 reads=['lprod'], writes=['lsum'])
            S.op('dve', lambda e: e.reduce_sum(out=lsum[:, 1:2], in_=lprod[:, 64:128], axis=AX.X),
                 reads=['lprod2'], writes=['lsum2'])
            S.op('act', lambda e: e.activation(out=lexp[:], in_=lsum[:], func=AF.Exp), reads=['lsum', 'lsum2'], writes=['lexp'])
            S.op('dve', lambda e: e.scalar_tensor_tensor(out=neglam[:], in0=lexp[:, 1:2], scalar=-LAMBDA_INIT,
                                                         in1=lexp[:, 0:1], op0=ALU.add, op1=ALU.subtract),
                 reads=['lexp'], writes=['neglam'])
            S.op('dve', lambda e: e.tensor_scalar(out=sub8[:], in0=subt[:], scalar1=1.0 - LAMBDA_INIT, scalar2=None,
                                                  op0=ALU.mult),
                 reads=['subt'], writes=['sub8'])
            S.barrier()

        def load_w_bf16(dst, src, kcn, ncols, key, col0=0):
            v = src.rearrange("(kc p) n -> p kc n", p=128)
            c = 0
            while c < ncols:
                w = min(1024, ncols - c)
                S.dma('pool', dst[:, :, c:c + w], v[:, :, col0 + c:col0 + c + w], writes=[key], semkey=('dw', key))
                c += w

        def rstd_from(bank, TT, inv_n, ln_t, rstd, bkey, rkey):
            S.op('act', lambda e: e.activation(out=ln_t[:, 0:TT], in_=bank[:, 0:TT], func=AF.Ln, bias=EPS, scale=inv_n),
                 reads=[bkey], writes=[rkey + '_ln'])
            S.op('act', lambda e: e.activation(out=rstd[:, 0:TT], in_=ln_t[:, 0:TT], func=AF.Exp, scale=-0.5),
                 reads=[rkey + '_ln'], writes=[rkey])

        def prenorm(h, hkey, u, ukey, sq, i_sub, TT, bank, bkey, ln_t, rstd, tmp, do_sq=True):
            if do_sq:
                S.op('dve', lambda e: e.tensor_tensor(out=sq[:], in0=h[:], in1=h[:], op=ALU.mult), reads=[hkey], writes=['sq'])
            for c in range(8):
                S.op('pe', lambda e, c=c: e.matmul(bank[:, 0:TT], lhsT=ones[:], rhs=sq[:, c, :], start=(c == 0), stop=(c == 7)),
                     reads=['sq', 'ones'], writes=[bkey])
            rstd_from(bank, TT, 1.0 / D, ln_t, rstd, bkey, 'rstd')
            for c in range(8):
                col = i_sub * 8 + c
                S.op('dve', lambda e, c=c, col=col: e.scalar_tensor_tensor(
                    out=tmp[c % 2][:, 0:TT], in0=h[:, c, :], scalar=Acoef[:, col:col + 1], in1=rstd[:, 0:TT],
                    op0=ALU.mult, op1=ALU.mult), reads=[hkey, 'rstd', 'Acoef'], writes=[('tmp', c % 2)])
                scol = (i_sub * 3 + 0) * 8 + c
                S.op('act', lambda e, c=c, scol=scol: e.activation(
                    out=u[:, c, :], in_=tmp[c % 2][:, 0:TT], func=AF.Identity, bias=modsb[:, scol:scol + 1], scale=1.0),
                    reads=[('tmp', c % 2), 'modsb'], writes=[ukey])

        def postnorm_resid(y, h, hkey, sq, i_sub, TT, bank, bkey, ln_t, rstd, tmp):
            S.op('dve', lambda e: e.tensor_tensor(out=sq[:], in0=y[:], in1=y[:], op=ALU.mult), reads=['y'], writes=['sq'])
            for c in range(8):
                S.op('pe', lambda e, c=c: e.matmul(bank[:, 0:TT], lhsT=ones[:], rhs=sq[:, c, :], start=(c == 0), stop=(c == 7)),
                     reads=['sq', 'ones'], writes=[bkey])
            rstd_from(bank, TT, 1.0 / D, ln_t, rstd, bkey, 'rstd')
            for c in range(8):
                col = i_sub * 8 + c
                S.op('dve', lambda e, c=c, col=col: e.scalar_tensor_tensor(
                    out=tmp[c % 2][:, 0:TT], in0=y[:, c, :], scalar=Gcoef[:, col:col + 1], in1=rstd[:, 0:TT],
                    op0=ALU.mult, op1=ALU.mult), reads=['y', 'rstd', 'Gcoef'], writes=[('tmp', c % 2)])
                S.op('dve', lambda e, c=c: e.tensor_tensor(out=h[:, c, :], in0=tmp[c % 2][:, 0:TT], in1=h[:, c, :], op=ALU.add),
                     reads=[('tmp', c % 2), hkey], writes=[hkey])

        def ffn_phase(src, dst, wts, i_sub, tag):
            TT = 256
            NT = S_LEN // TT
            with ExitStack() as ph:
                wg = sbt(ph, "wg" + tag, [128, 8, DFF], BF16)
                wu = sbt(ph, "wu" + tag, [128, 8, DFF], BF16)
                wd = sbt(ph, "wd" + tag, [128, NF, D], BF16)
                xt = [sbt(ph, "xt%d%s" % (i, tag), [128, 8, TT], F32) for i in range(3)]
                u = [sbt(ph, "u%d%s" % (i, tag), [128, 8, TT], BF16) for i in range(2)]
                sq = sbt(ph, "sq" + tag, [128, 8, TT], BF16)
                act = sbt(ph, "act" + tag, [128, NF, TT], BF16)
                y = sbt(ph, "y" + tag, [128, 8, TT], F32)
                st = [sbt(ph, "st%d%s" % (i, tag), [128, TT], F32) for i in range(2)]
                tmp = [sbt(ph, "tmp%d%s" % (i, tag), [128, TT], F32) for i in range(2)]
                ln_t = sbt(ph, "ln" + tag, [128, TT], F32)
                rstd = sbt(ph, "rstd" + tag, [128, TT], F32)
                load_w_bf16(wg, wts[0], 8, DFF, 'wg')
                load_w_bf16(wu, wts[1], 8, DFF, 'wu')
                load_w_bf16(wd, wts[2], NF, D, 'wd')
                srcv = src.rearrange("(c p) t -> p c t", p=128)
                dstv = dst.rearrange("(c p) t -> p c t", p=128)

                def load(t):
                    s = t % 3
                    S.dma('sp', xt[s][:], srcv[:, :, t * TT:(t + 1) * TT], writes=[('x', s)], semkey=('dx', s))

                def pre(t):
                    prenorm(xt[t % 3], ('x', t % 3), u[t % 2], ('u', t % 2), sq, i_sub, TT, banks[7], 'b7', ln_t, rstd, tmp)

                load(0)
                if NT > 1:
                    load(1)
                pre(0)
                for t in range(NT):
                    s = t % 3
                    ut = u[t % 2]
                    ukey = ('u', t % 2)
                    if t + 2 < NT:
                        load(t + 2)
                    for f in range(NF):
                        bk = banks[f % 4]
                        bkey = 'b%d' % (f % 4)
                        for k in range(8):
                            S.op('pe', lambda e, bk=bk, f=f, k=k: e.matmul(
                                bk[:, 0:TT], lhsT=wg[:, k, f * 128:(f + 1) * 128], rhs=ut[:, k, :], start=(k == 0), stop=(k == 7)),
                                reads=['wg', ukey], writes=[bkey])
                        for k in range(8):
                            S.op('pe', lambda e, bk=bk, f=f, k=k: e.matmul(
                                bk[:, 256:256 + TT], lhsT=wu[:, k, f * 128:(f + 1) * 128], rhs=ut[:, k, :], start=(k == 0), stop=(k == 7)),
                                reads=['wu', ukey], writes=[bkey])
                        S.op('act', lambda e, bk=bk, f=f: e.activation(out=st[f % 2][:], in_=bk[:, 0:TT], func=AF.Silu),
                             reads=[bkey], writes=[('st', f % 2)])
                        S.op('dve', lambda e, bk=bk, f=f: e.tensor_tensor(out=act[:, f, :], in0=st[f % 2][:], in1=bk[:, 256:256 + TT], op=ALU.mult),
                             reads=[bkey, ('st', f % 2)], writes=['act'])
                    if t + 1 < NT:
                        pre(t + 1)
                    for c in range(8):
                        bk = banks[4 + c % 3]
                        bkey = 'b%d' % (4 + c % 3)
                        for f in range(NF):
                            S.op('pe', lambda e, bk=bk, f=f, c=c: e.matmul(
                                bk[:, 0:TT], lhsT=wd[:, f, c * 128:(c + 1) * 128], rhs=act[:, f, :], start=(f == 0), stop=(f == NF - 1)),
                                reads=['wd', 'act'], writes=[bkey])
                        S.op('act', lambda e, bk=bk, c=c: e.activation(out=y[:, c, :], in_=bk[:, 0:TT], func=AF.Copy),
                             reads=[bkey], writes=['y'])
                    postnorm_resid(y, xt[s], ('x', s), sq, i_sub, TT, banks[7], 'b7', ln_t, rstd, tmp)
                    S.dma('sp', dstv[:, :, t * TT:(t + 1) * TT], xt[s][:], reads=[('x', s)], semkey=('dxo', s))
                S.barrier()

        def m1_phase():
            TT = 512
            NT = S_LEN // TT
            with ExitStack() as ph:
                win = sbt(ph, "win", [128, 8, 3072], BF16)
                xt = [sbt(ph, "m1x%d" % i, [128, 8, TT], F32) for i in range(2)]
                u2 = [sbt(ph, "m1u%d" % i, [128, 8, TT], BF16) for i in range(2)]
                sq = sbt(ph, "m1sq", [128, 8, TT], BF16)
                tmp = [sbt(ph, "m1tmp%d" % i, [128, TT], F32) for i in range(2)]
                ln_t = sbt(ph, "m1ln", [128, TT], F32)
                rstd = sbt(ph, "m1rstd", [128, TT], F32)
                stg = {}
                for nm in ('qa', 'ka', 'qd', 'kd', 'va', 'vd'):
                    stg[nm] = [sbt(ph, "stg_%s%d" % (nm, i), [128, 4, 512], BF16) for i in range(2)]
                load_w_bf16(win, w_in, 8, 3072, 'win')
                srcv = h1T.rearrange("(c p) t -> p c t", p=128)
                fm = [('qa', 0, 0.125, qaT), ('ka', 512, 1.0, kaT), ('qd', 1536, 0.125, qdT), ('kd', 2048, 1.0, kdT)]
                tm = [('va', 1024, vaS), ('vd', 2560, vdS)]
                S.dma('sp', xt[0][:], srcv[:, :, 0:TT], writes=[('x', 0)], semkey=('dx', 0))
                bi = 0
                for t in range(NT):
                    s = t % 2
                    if t + 1 < NT:
                        S.dma('sp', xt[1 - s][:], srcv[:, :, (t + 1) * TT:(t + 2) * TT], writes=[('x', 1 - s)], semkey=('dx', 1 - s))
                    if t == 0:
                        prenorm(xt[0], ('x', 0), u2[0], ('u', 0), sq, 1, TT, banks[7], 'b7', ln_t, rstd, tmp)
                    u = u2[s]
                    ukey = ('u', s)
                    for (nm, c0, scl, dram) in fm:
                        sg = stg[nm][t % 2]
                        skey = ('stg', nm, t % 2)
                        for j in range(4):
                            bk = banks[bi % 6]
                            bkey = 'b%d' % (bi % 6)
                            bi += 1
                            for k in range(8):
                                S.op('pe', lambda e, bk=bk, k=k, j=j, c0=c0: e.matmul(
                                    bk[:], lhsT=win[:, k, c0 + j * 128:c0 + (j + 1) * 128], rhs=u[:, k, :], start=(k == 0), stop=(k == 7)),
                                    reads=['win', ukey], writes=[bkey])
                            S.op('act', lambda e, bk=bk, j=j, sg=sg, scl=scl: e.activation(out=sg[:, j, :], in_=bk[:], func=AF.Copy, scale=scl),
                                 reads=[bkey], writes=[skey])
                        S.dma('sp', dram.rearrange("(c p) t -> p c t", p=128)[:, :, t * TT:(t + 1) * TT], sg[:],
                              reads=[skey], semkey=('dstg', nm, t % 2))
                    if t + 1 < NT:
                        prenorm(xt[1 - s], ('x', 1 - s), u2[1 - s], ('u', 1 - s), sq, 1, TT, banks[7], 'b7', ln_t, rstd, tmp)
                    for (nm, c0, dram) in tm:
                        sg = stg[nm][t % 2]
                        skey = ('stg', nm, t % 2)
                        for j in range(4):
                            bk = banks[bi % 6]
                            bkey = 'b%d' % (bi % 6)
                            bi += 1
                            for k in range(8):
                                S.op('pe', lambda e, bk=bk, k=k, j=j, c0=c0: e.matmul(
                                    bk[:], lhsT=u[:, k, j * 128:(j + 1) * 128], rhs=win[:, k, c0:c0 + 512], start=(k == 0), stop=(k == 7)),
                                    reads=['win', ukey], writes=[bkey])
                            S.op('dve', lambda e, bk=bk, j=j, sg=sg: e.tensor_copy(out=sg[:, j, :], in_=bk[:]),
                                 reads=[bkey], writes=[skey])
                        S.dma('sp', dram[t * TT:(t + 1) * TT, :].rearrange("(s p) f -> p s f", p=128), sg[:],
                              reads=[skey], semkey=('dstg', nm, t % 2))
                S.barrier()

        def sb_phase():
            with ExitStack() as ph:
                vsb = sbt(ph, "vsb", [128, NKT, 512], BF16)
                kT = [sbt(ph, "kT%d" % i, [128, S_LEN], BF16) for i in range(2)]
                qT = [sbt(ph, "qT%d" % i, [128, 512], BF16) for i in range(2)]
                onesw = sbt(ph, "onesw", [128, 512], BF16)
                msb = [sbt(ph, "msb%d" % j, [128, 512], BF16) for j in range(4)]
                e_t = [sbt(ph, "e_t%d" % i, [128, 1024], F32) for i in range(2)]
                sp_t = [sbt(ph, "sp_t%d" % i, [128, 1024], BF16) for i in range(3)]
                w_t = [sbt(ph, "w_t%d" % i, [128, 1024], BF16) for i in range(3)]
                cbf = [sbt(ph, "cbf%d" % i, [128, 512], BF16) for i in range(3)]
                ystg = [sbt(ph, "ystg%d" % i, [128, 512], BF16) for i in range(2)]
                S.op('pool', lambda e: e.memset(onesw[:], 1.0), writes=['onesw'])
                for b_ in range(2):
                    S.op('pool', lambda e, b_=b_: e.memset(kT[b_][64:128, :], 0.0), writes=[('kTz', b_)])
                    S.op('pool', lambda e, b_=b_: e.memset(qT[b_][64:128, :], 0.0), writes=[('qTz', b_)])
                for j in range(4):
                    S.op('pool', lambda e, j=j: e.affine_select(out=msb[j][:], in_=onesw[:], pattern=[[1, 512]],
                                                                 compare_op=ALU.is_ge, fill=0.0, base=-128 * j - 1,
                                                                 channel_multiplier=-1),
                         reads=['onesw'], writes=[('msb', j)])
                S.dma('sp', vsb[:], vaS.rearrange("(i p) f -> p i f", p=128), writes=['vsb'], semkey='d_vsb')
                pairs = []
                sidx = 0
                for h in range(8):
                    for Q in range(NQ):
                        top = 4 * Q + 3
                        for ia in range(top, -1, -2):
                            ib = ia - 1
                            pairs.append(dict(h=h, Q=Q, ia=ia, ib=ib, first=(ia == top), last=(ib == 0), sidx=sidx,
                                              ja=(ia - 4 * Q if ia >= 4 * Q else None),
                                              jb=(ib - 4 * Q if ib >= 4 * Q else None), p=len(pairs)))
                        sidx += 1
                YK, YKEY = banks[6], 'b6'
                CK, CKEY = banks[7], 'b7'

                def load_k(h):
                    S.dma('sp', kT[h % 2][0:64, :], kaT[h * 64:(h + 1) * 64, :], writes=[('kT', h % 2)], semkey=('dk', h % 2))

                def load_q(h, Q, sidx):
                    S.dma('sp', qT[sidx % 2][0:64, :], qaT[h * 64:(h + 1) * 64, Q * 512:(Q + 1) * 512],
                          writes=[('qT', sidx % 2)], semkey=('dq', sidx % 2))

                def P1(pr):
                    h, Q, p = pr['h'], pr['Q'], pr['p']
                    if pr['first']:
                        if Q == 0:
                            if h == 0:
                                load_k(0)
                                load_q(0, 0, 0)
                            if h + 1 < 8:
                                load_k(h + 1)
                        nh, nQ = (h, Q + 1) if Q + 1 < NQ else (h + 1, 0)
                        if nh < 8:
                            load_q(nh, nQ, pr['sidx'] + 1)
                    zp = pp[p % 3]
                    zkey = ('zp', p % 3)
                    kt = kT[h % 2]
                    qt = qT[pr['sidx'] % 2]
                    rd = [('kT', h % 2), ('qT', pr['sidx'] % 2), ('kTz', h % 2), ('qTz', pr['sidx'] % 2)]
                    for half, i in ((0, pr['ia']), (1, pr['ib'])):
                        S.op('pe', lambda e, half=half, i=i: e.matmul(zp[:, half * 512:(half + 1) * 512], lhsT=kt[:, i * 128:(i + 1) * 128],
                                                                      rhs=qt[:], start=True, stop=False),
                             reads=rd, writes=[zkey])

                def A1(pr):
                    p = pr['p']
                    zp = pp[p % 3]
                    zkey = ('zp', p % 3)
                    et = e_t[p % 2]
                    spt = sp_t[p % 3]
                    S.op('act', lambda e: e.activation(out=et[:], in_=zp[:], func=AF.Exp), reads=[zkey], writes=[('e', p % 2)])
                    S.op('act', lambda e: e.activation(out=spt[:], in_=et[:], func=AF.Ln, bias=1.0, scale=1.0),
                         reads=[('e', p % 2)], writes=[('sp', p % 3)])
                    for half, j in ((0, pr['ja']), (1, pr['jb'])):
                        if j is not None:
                            S.op('pool', lambda e, half=half, j=j: e.tensor_tensor(
                                out=spt[:, half * 512:(half + 1) * 512], in0=spt[:, half * 512:(half + 1) * 512], in1=msb[j][:], op=ALU.mult),
                                reads=[('sp', p % 3), ('msb', j)], writes=[('sp', p % 3)])

                def P2(pr):
                    p = pr['p']
                    zp = pp[p % 3]
                    zkey = ('zp', p % 3)
                    spt = sp_t[p % 3]
                    spk = ('sp', p % 3)
                    za, zb_ = zp[:, 0:512], zp[:, 512:1024]
                    spa, spb = spt[:, 0:512], spt[:, 512:1024]
                    pc = (p - 1) % 3
                    S.op('pe', lambda e: e.matmul(za, lhsT=negui[:], rhs=spa, start=False, stop=True, skip_group_check=True),
                         reads=['negui', spk, zkey], writes=[zkey])
                    if not pr['first']:
                        S.op('pe', lambda e: e.matmul(za, lhsT=negI[:], rhs=cbf[pc][:], start=False, stop=True, skip_group_check=True),
                             reads=['negI', ('cbf', pc), zkey], writes=[zkey])
                    S.op('pe', lambda e: e.matmul(zb_, lhsT=negui[:], rhs=spb, start=False, stop=True, skip_group_check=True),
                         reads=['negui', spk, zkey], writes=[zkey])
                    S.op('pe', lambda e: e.matmul(zb_, lhsT=negones[:], rhs=spa, start=False, stop=True, skip_group_check=True),
                         reads=['negones', spk, zkey], writes=[zkey])
                    if not pr['first']:
                        S.op('pe', lambda e: e.matmul(zb_, lhsT=negI[:], rhs=cbf[pc][:], start=False, stop=True, skip_group_check=True),
                             reads=['negI', ('cbf', pc), zkey], writes=[zkey])
                    if not pr['last']:
                        S.op('pe', lambda e: e.matmul(CK, lhsT=ones[:], rhs=spa, start=pr['first'], stop=False, skip_group_check=True),
                             reads=['ones', spk] + ([] if pr['first'] else [CKEY]), writes=[CKEY])
                        S.op('pe', lambda e: e.matmul(CK, lhsT=ones[:], rhs=spb, start=False, stop=True, skip_group_check=True),
                             reads=['ones', spk, CKEY], writes=[CKEY])
                        S.op('dve', lambda e: e.tensor_copy(out=cbf[p % 3][:], in_=CK), reads=[CKEY], writes=[('cbf', p % 3)])

                def A2(pr):
                    p = pr['p']
                    zp = pp[p % 3]
                    zkey = ('zp', p % 3)
                    wt = w_t[p % 3]
                    S.op('act', lambda e: e.activation(out=wt[:], in_=zp[:], func=AF.Exp), reads=[zkey], writes=[('w', p % 3)])
                    for half, j in ((0, pr['ja']), (1, pr['jb'])):
                        if j is not None:
                            S.op('pool', lambda e, half=half, j=j: e.tensor_tensor(
                                out=wt[:, half * 512:(half + 1) * 512], in0=wt[:, half * 512:(half + 1) * 512], in1=msb[j][:], op=ALU.mult),
                                reads=[('w', p % 3), ('msb', j)], writes=[('w', p % 3)])

                def P3(pr):
                    h, Q, p = pr['h'], pr['Q'], pr['p']
                    wt = w_t[p % 3]
                    hp = h // 2
                    ho = (h % 2) * 64
                    for half, i in ((0, pr['ia']), (1, pr['ib'])):
                        st_ = pr['first'] and half == 0
                        S.op('pe', lambda e, half=half, i=i, st_=st_: e.matmul(
                            YK, lhsT=vsb[:, i, hp * 128:(hp + 1) * 128], rhs=wt[:, half * 512:(half + 1) * 512],
                            start=st_, stop=(pr['last'] and half == 1), skip_group_check=True),
                            reads=['vsb', ('w', p % 3)] + ([] if st_ else [YKEY]), writes=[YKEY])
                    if pr['last']:
                        sl = pr['sidx'] % 2
                        S.op('dve', lambda e: e.tensor_copy(out=ystg[sl][ho:ho + 64, :], in_=YK[ho:ho + 64, :]), reads=[YKEY], writes=[('ystg', sl)])
                        S.dma('sp', yaT[h * 64:(h + 1) * 64, Q * 512:(Q + 1) * 512], ystg[sl][ho:ho + 64, :], reads=[('ystg', sl)], semkey=('dyo', sl))

                n = len(pairs)
                P1(pairs[0])
                for s in range(n + 2):
                    if 0 <= s - 1 < n:
                        P2(pairs[s - 1])
                    if s < n:
                        A1(pairs[s])
                    if 0 <= s - 1 < n:
                        A2(pairs[s - 1])
                    if s + 1 < n:
                        P1(pairs[s + 1])
                    if 0 <= s - 2 < n:
                        P3(pairs[s - 2])
                S.barrier()

        def da_phase():
            with ExitStack() as ph:
                vda = [sbt(ph, "vda%d" % i, [128, NKT, 128], BF16) for i in range(2)]
                kaug = [[sbt(ph, "kaug%d_%d" % (b, m), [128, S_LEN], BF16) for m in range(2)] for b in range(2)]
                qaug = [[[sbt(ph, "qaug%d_%d_%d" % (h, b, m), [128, 512], BF16) for m in range(2)] for b in range(2)] for h in range(4)]
                fullb = [[sbt(ph, "fb%d_%d" % (h, j), [128, 512], F32) for j in range(4)] for h in range(4)]
                bvi = sbt(ph, "bvi", [128, 64], F32)
                bv = [sbt(ph, "bv%d" % h, [128, 64], F32) for h in range(4)]
                augf = sbt(ph, "augf", [128, 512], F32)
                dtmp = sbt(ph, "dtmp", [128, 512], F32)
                dtmp2 = sbt(ph, "dtmp2", [128, 512], F32)
                arg_t = [sbt(ph, "arg%d" % i, [128, 1024], F32) for i in range(2)]
                w_t = [sbt(ph, "dw%d" % i, [128, 1024], BF16) for i in range(3)]
                wsum_t = [sbt(ph, "wsum%d" % m, [128, 512], F32) for m in range(2)]
                ones32 = sbt(ph, "ones32", [128, 128], F32)
                S.op('pool', lambda e: e.memset(ones32[:], 1.0), writes=['ones32'])
                r_t = [sbt(ph, "dr%d" % i, [128, 512], F32) for i in range(2)]
                t_t = [sbt(ph, "dt%d" % i, [128, 512], F32) for i in range(2)]
                yc_t = [sbt(ph, "dyc%d" % i, [128, 512], F32) for i in range(2)]
                yo = [sbt(ph, "dyo%d" % i, [128, 512], BF16) for i in range(2)]
                S.op('pool', lambda e: e.iota(out=augf[64:96, :], pattern=[[2, 256], [0, 2]], base=0, channel_multiplier=0,
                                              allow_small_or_imprecise_dtypes=True), writes=['augfa'])
                S.op('pool', lambda e: e.iota(out=augf[96:128, :], pattern=[[0, 256], [1, 2]], base=0, channel_multiplier=0,
                                              allow_small_or_imprecise_dtypes=True), writes=['augfb'])
                for h in range(4):
                    for b in range(2):
                        for m in range(2):
                            S.op('dve', lambda e, h=h, b=b, m=m: e.tensor_scalar(
                                out=qaug[h][b][m][64:128, :], in0=augf[64:128, :], scalar1=-SLOPES[h], scalar2=None, op0=ALU.mult),
                                reads=['augfa', 'augfb'], writes=[('qaug_aug', h, b, m)])
                for b in range(2):
                    for m in range(2):
                        ka = kaug[b][m]
                        S.op('pool', lambda e, ka=ka: e.memset(ka[64:128, :], 0.0), writes=[('kaug_aug', b, m)])
                        S.op('pool', lambda e, ka=ka: e.memset(ka[64:65, :], 1.0), reads=[('kaug_aug', b, m)], writes=[('kaug_aug', b, m)])
                        S.op('pool', lambda e, ka=ka: e.memset(ka[96:97, :], 1.0), reads=[('kaug_aug', b, m)], writes=[('kaug_aug', b, m)])
                S.op('pool', lambda e: e.iota(out=bvi[:], pattern=[[-128, 64]], base=-128, channel_multiplier=1,
                                              allow_small_or_imprecise_dtypes=True), writes=['bvi'])
                for h in range(4):
                    S.op('dve', lambda e, h=h: e.tensor_scalar(out=bv[h][:], in0=bvi[:], scalar1=SLOPES[h], scalar2=None, op0=ALU.mult),
                         reads=['bvi'], writes=[('bv', h)])
                ttf = sbt(ph, "ttf", [128, 512], F32)
                S.op('pool', lambda e: e.iota(out=ttf[:], pattern=[[1, 512]], base=0, channel_multiplier=0,
                                              allow_small_or_imprecise_dtypes=True), writes=['ttf'])
                for j in range(4):
                    S.op('pool', lambda e, j=j: e.iota(out=dtmp[:], pattern=[[-1, 512]], base=128 * j, channel_multiplier=1,
                                                       allow_small_or_imprecise_dtypes=True), writes=['dtmp'])
                    for h in range(4):
                        S.op('dve', lambda e, h=h: e.tensor_scalar(out=dtmp2[:], in0=dtmp[:], scalar1=SLOPES[h], scalar2=None,
                                                                   op0=ALU.mult),
                             reads=['dtmp'], writes=['dtmp2'])
                        S.op('dve', lambda e, h=h: e.scalar_tensor_tensor(out=dtmp2[:], in0=dtmp[:], scalar=-SLOPES[h], in1=dtmp2[:],
                                                                          op0=ALU.mult, op1=ALU.min),
                             reads=['dtmp', 'dtmp2'], writes=['dtmp2'])
                        S.op('dve', lambda e, h=h: e.scalar_tensor_tensor(out=dtmp2[:], in0=ttf[:], scalar=SLOPES[h], in1=dtmp2[:],
                                                                          op0=ALU.mult, op1=ALU.add),
                             reads=['dtmp2', 'ttf'], writes=['dtmp2'])
                        S.op('pool', lambda e, h=h, j=j: e.affine_select(
                            out=fullb[h][j][:], in_=dtmp2[:], pattern=[[64, 8], [0, 64]], compare_op=ALU.is_ge, fill=NEGBIG,
                            base=63 - 128 * j, channel_multiplier=-1), reads=['dtmp2'], writes=[('fullb', h, j)])

                items = []
                sidx = 0
                for h in range(4):
                    for Q in range(NQ):
                        top = 4 * Q + 3
                        for i in range(0, top + 1):
                            items.append(dict(h=h, Q=Q, i=i, first=(i == 0), last=(i == top), sidx=sidx,
                                              diag=(i - 4 * Q if i >= 4 * Q else None), idx=len(items)))
                        sidx += 1

                vdv = vdS.rearrange("(i p) f -> p i f", p=128)

                def load_k(h):
                    for m in range(2):
                        S.dma('sp', kaug[h % 2][m][0:64, :], kdT[h * 128 + m * 64:h * 128 + (m + 1) * 64, :],
                              writes=[('kaug', h % 2, m)], semkey=('dkd', h % 2, m))

                def load_v(h):
                    S.dma('sp', vda[h % 2][:], vdv[:, :, h * 128:(h + 1) * 128], writes=[('vda', h % 2)], semkey=('dvd', h % 2))

                def load_q(h, Q, sidx):
                    for m in range(2):
                        S.dma('sp', qaug[h][sidx % 2][m][0:64, :], qdT[h * 128 + m * 64:h * 128 + (m + 1) * 64, Q * 512:(Q + 1) * 512],
                              writes=[('qaug', h, sidx % 2, m)], semkey=('dqd', sidx % 2, m))

                YB = [(banks[4], 'b4'), (banks[5], 'b5')]
                LB = [(banks[6], 'b6'), (banks[7], 'b7')]

                def stA(it):
                    h, Q, i, idx = it['h'], it['Q'], it['i'], it['idx']
                    if it['first']:
                        if Q == 0:
                            if h == 0:
                                load_k(0)
                                load_v(0)
                                load_q(0, 0, 0)
                            if h + 1 < 4:
                                load_k(h + 1)
                        nh, nQ = (h, Q + 1) if Q + 1 < NQ else (h + 1, 0)
                        if nh < 4:
                            load_q(nh, nQ, it['sidx'] + 1)
                    sb_ = it['sidx'] % 2
                    for m in range(2):
                        bn = (idx % 2) * 2 + m
                        bk, bkey = banks[bn], 'b%d' % bn
                        ka = kaug[h % 2][m]
                        qa = qaug[h][sb_][m]
                        S.op('pe', lambda e, bk=bk, ka=ka, qa=qa: e.matmul(bk[:], lhsT=ka[:, i * 128:(i + 1) * 128], rhs=qa[:], start=True, stop=True),
                             reads=[('kaug', h % 2, m), ('kaug_aug', h % 2, m), ('qaug', h, sb_, m), ('qaug_aug', h, sb_, m)], writes=[bkey])

                def stA2(it):
                    h, Q, i, idx = it['h'], it['Q'], it['i'], it['idx']
                    pi = idx % 2
                    zp = pp[pi]
                    bkeys = ['b%d' % (pi * 2), 'b%d' % (pi * 2 + 1)]
                    wt = w_t[idx % 3]
                    wkey = ('w', idx % 3)
                    if it['diag'] is None:
                        n_idx = 4 * Q - i - 1
                        S.op('act', lambda e: e.activation(out=wt[:], in_=zp[:], func=AF.Exp, bias=bv[h][:, n_idx:n_idx + 1], scale=1.0),
                             reads=bkeys + [('bv', h)], writes=[wkey])
                    else:
                        j = it['diag']
                        at = arg_t[pi]
                        for m in range(2):
                            S.op('dve', lambda e, m=m: e.tensor_tensor(out=at[:, m * 512:(m + 1) * 512], in0=zp[:, m * 512:(m + 1) * 512],
                                                                       in1=fullb[h][j][:], op=ALU.add),
                                 reads=[bkeys[m], ('fullb', h, j)], writes=[('arg', pi)])
                        S.op('act', lambda e: e.activation(out=wt[:], in_=at[:], func=AF.Exp), reads=[('arg', pi)], writes=[wkey])

                def stB(it):
                    h, Q, i, idx = it['h'], it['Q'], it['i'], it['idx']
                    if it['first'] and Q == 0 and h + 1 < 4:
                        load_v(h + 1)
                    wt = w_t[idx % 3]
                    wkey = ('w', idx % 3)
                    sl = it['sidx'] % 2
                    for m in range(2):
                        yk, ykey = YB[m]
                        S.op('pe', lambda e, yk=yk, m=m: e.matmul(yk, lhsT=vda[h % 2][:, i, :], rhs=wt[:, m * 512:(m + 1) * 512],
                                                               start=it['first'], stop=it['last']),
                             reads=[('vda', h % 2), wkey] + ([] if it['first'] else [ykey]), writes=[ykey])
                    for m in range(2):
                        lk, lkey = LB[m]
                        S.op('pe', lambda e, lk=lk, m=m: e.matmul(lk, lhsT=ones[:], rhs=wt[:, m * 512:(m + 1) * 512], start=it['first'], stop=it['last']),
                             reads=['ones', wkey] + ([] if it['first'] else [lkey]), writes=[lkey])
                    if it['last']:
                        for m in range(2):
                            S.op('act', lambda e, m=m: e.activation(out=yc_t[m][:], in_=YB[m][0], func=AF.Copy), reads=[YB[m][1]], writes=[('yc', m)])
                            S.op('dve', lambda e, m=m: e.reciprocal(out=r_t[m][:], in_=LB[m][0]), reads=[LB[m][1]], writes=[('r', m)])
                        for m in range(2):
                            S.op('dve', lambda e, m=m: e.tensor_tensor(out=t_t[m][:], in0=yc_t[m][:], in1=r_t[m][:], op=ALU.mult),
                                 reads=[('yc', m), ('r', m)], writes=[('t', m)])
                        S.op('dve', lambda e: e.scalar_tensor_tensor(out=yo[sl][:], in0=t_t[1][:], scalar=neglam[:, 0:1], in1=t_t[0][:],
                                                                     op0=ALU.mult, op1=ALU.add),
                             reads=[('t', 0), ('t', 1), 'neglam'], writes=[('yo', sl)])
                        S.dma('sp', ydT[h * 128:(h + 1) * 128, Q * 512:(Q + 1) * 512], yo[sl][:], reads=[('yo', sl)], semkey=('dydo', sl))

                n = len(items)
                for s in range(n + 2):
                    if 0 <= s - 2 < n:
                        stB(items[s - 2])
                    if 0 <= s - 1 < n:
                        stA2(items[s - 1])
                    if s < n:
                        stA(items[s])
                S.barrier()

        def m3_phase():
            TT = 512
            NT = S_LEN // TT
            with ExitStack() as ph:
                wsb = sbt(ph, "m3wsb", [128, 4, D], BF16)
                wda = sbt(ph, "m3wda", [128, 4, D], BF16)
                wo = sbt(ph, "m3wo", [128, 8, D], BF16)
                wgt = sbt(ph, "m3wgt", [128, 8, 2048], BF16)
                xt = [sbt(ph, "m3x%d" % i, [128, 8, TT], F32) for i in range(2)]
                yat = [sbt(ph, "m3ya%d" % i, [128, 4, TT], BF16) for i in range(2)]
                ydt = [sbt(ph, "m3yd%d" % i, [128, 4, TT], BF16) for i in range(2)]
                u2 = [sbt(ph, "m3u%d" % i, [128, 8, TT], BF16) for i in range(2)]
                sq = sbt(ph, "m3sq", [128, 8, TT], BF16)
                mg = sbt(ph, "m3mg", [128, 8, TT], BF16)
                sqy = sbt(ph, "m3sqy", [128, 4, TT], BF16)
                ydn = [sbt(ph, "m3ydn%d" % i, [128, 4, TT], BF16) for i in range(2)]
                ln2 = [sbt(ph, "m3ln2%d" % i, [128, TT], F32) for i in range(2)]
                rs2 = [sbt(ph, "m3rs2%d" % i, [128, TT], F32) for i in range(2)]
                y = sbt(ph, "m3y", [128, 8, TT], F32)
                sg_t = [sbt(ph, "m3sg%d" % i, [128, TT], F32) for i in range(4)]
                mm_t = [sbt(ph, "m3mm%d" % i, [128, TT], F32) for i in range(4)]
                tmp = [sbt(ph, "m3tmp%d" % i, [128, TT], F32) for i in range(2)]
                ln_t = sbt(ph, "m3ln", [128, TT], F32)
                rstd = sbt(ph, "m3rstd", [128, TT], F32)
                load_w_bf16(wsb, w_bsb, 4, D, 'wsb')
                load_w_bf16(wda, w_bda, 4, D, 'wda')
                load_w_bf16(wgt, w_in, 8, 2048, 'wgt', col0=3072)
                load_w_bf16(wo, w_out, 8, D, 'wo')
                srcv = h1T.rearrange("(c p) t -> p c t", p=128)
                dstv = h2T.rearrange("(c p) t -> p c t", p=128)
                yav = yaT.rearrange("(c p) t -> p c t", p=128)
                ydv = ydT.rearrange("(c p) t -> p c t", p=128)

                def load(t):
                    s = t % 2
                    S.dma('sp', xt[s][:], srcv[:, :, t * TT:(t + 1) * TT], writes=[('x', s)], semkey=('dx', s))
                    S.dma('sp', yat[s][:], yav[:, :, t * TT:(t + 1) * TT], writes=[('ya', s)], semkey=('dya', s))
                    S.dma('sp', ydt[s][:], ydv[:, :, t * TT:(t + 1) * TT], writes=[('yd', s)], semkey=('dyd', s))

                def pre_sq(t):
                    s_ = t % 2
                    S.op('dve', lambda e: e.tensor_tensor(out=sq[:], in0=xt[s_][:], in1=xt[s_][:], op=ALU.mult), reads=[('x', s_)], writes=['sq'])
                    S.op('dve', lambda e: e.tensor_tensor(out=sqy[:], in0=ydt[s_][:], in1=ydt[s_][:], op=ALU.mult),
                         reads=[('yd', s_)], writes=['sqy'])

                def pre(t):
                    prenorm(xt[t % 2], ('x', t % 2), u2[t % 2], ('u', t % 2), sq, 1, TT, banks[7], 'b7', ln_t, rstd, tmp, do_sq=False)
                    s_ = t % 2
                    for k in range(4):
                        bk, bkey = banks[6 + k % 2], 'b%d' % (6 + k % 2)
                        S.op('pe', lambda e, bk=bk, k=k: e.matmul(bk, lhsT=ones[:], rhs=sqy[:, k, :], start=True, stop=True),
                             reads=['ones', 'sqy'], writes=[bkey])
                        S.op('act', lambda e, bk=bk, k=k: e.activation(out=ln2[k % 2][:], in_=bk, func=AF.Ln, bias=EPS, scale=1.0 / 128),
                             reads=[bkey], writes=[('ln2', k % 2)])
                        S.op('act', lambda e, k=k: e.activation(out=rs2[k % 2][:], in_=ln2[k % 2][:], func=AF.Exp, scale=-0.5),
                             reads=[('ln2', k % 2)], writes=[('rs2', k % 2)])
                        S.op('dve', lambda e, k=k: e.scalar_tensor_tensor(out=ydn[s_][:, k, :], in0=ydt[s_][:, k, :], scalar=sub8[:, 0:1],
                                                                          in1=rs2[k % 2][:], op0=ALU.mult, op1=ALU.mult),
                             reads=[('yd', s_), ('rs2', k % 2), 'sub8'], writes=[('ydn', s_)])

                load(0)
                pre_sq(0)
                pre(0)
                for t in range(NT):
                    s = t % 2
                    u = u2[s]
                    ukey = ('u', s)
                    if t + 1 < NT:
                        load(t + 1)
                    for c in range(8):
                        if c == 4 and t + 1 < NT:
                            pre_sq(t + 1)
                        cs_ = slice(c * 128, (c + 1) * 128)
                        b0 = (c % 2) * 4
                        bA, bB, bGa, bGd = banks[b0], banks[b0 + 1], banks[b0 + 2], banks[b0 + 3]
                        kA, kB, kGa, kGd = ['b%d' % (b0 + i) for i in range(4)]
                        r0 = (c % 2) * 2
                        for k in range(4):
                            S.op('pe', lambda e, k=k, cs_=cs_, bA=bA: e.matmul(bA[:], lhsT=wsb[:, k, cs_], rhs=yat[s][:, k, :], start=(k == 0), stop=(k == 3)),
                                 reads=['wsb', ('ya', s)], writes=[kA])
                        for k in range(4):
                            S.op('pe', lambda e, k=k, cs_=cs_, bB=bB: e.matmul(bB[:], lhsT=wda[:, k, cs_], rhs=ydn[s][:, k, :], start=(k == 0), stop=(k == 3)),
                                 reads=['wda', ('ydn', s)], writes=[kB])
                        for k in range(8):
                            S.op('pe', lambda e, k=k, c=c, bGa=bGa: e.matmul(bGa[:], lhsT=wgt[:, k, c * 128:(c + 1) * 128], rhs=u[:, k, :], start=(k == 0), stop=(k == 7)),
                                 reads=['wgt', ukey], writes=[kGa])
                        for k in range(8):
                            S.op('pe', lambda e, k=k, c=c, bGd=bGd: e.matmul(bGd[:], lhsT=wgt[:, k, 1024 + c * 128:1024 + (c + 1) * 128], rhs=u[:, k, :], start=(k == 0), stop=(k == 7)),
                                 reads=['wgt', ukey], writes=[kGd])
                        S.op('act', lambda e, bGa=bGa, r0=r0: e.activation(out=sg_t[r0][:], in_=bGa[:], func=AF.Sigmoid), reads=[kGa], writes=[('sg', r0)])
                        S.op('act', lambda e, bGd=bGd, r0=r0: e.activation(out=sg_t[r0 + 1][:], in_=bGd[:], func=AF.Sigmoid), reads=[kGd], writes=[('sg', r0 + 1)])
                        S.op('dve', lambda e, bA=bA, r0=r0: e.tensor_tensor(out=mm_t[r0][:], in0=bA[:], in1=sg_t[r0][:], op=ALU.mult),
                             reads=[kA, ('sg', r0)], writes=[('mm', r0)])
                        S.op('dve', lambda e, bB=bB, r0=r0: e.tensor_tensor(out=mm_t[r0 + 1][:], in0=bB[:], in1=sg_t[r0 + 1][:], op=ALU.mult),
                             reads=[kB, ('sg', r0 + 1)], writes=[('mm', r0 + 1)])
                        S.op('dve', lambda e, c=c, r0=r0: e.tensor_tensor(out=mg[:, c, :], in0=mm_t[r0][:], in1=mm_t[r0 + 1][:], op=ALU.add),
                             reads=[('mm', r0), ('mm', r0 + 1)], writes=['mg'])
                    if t + 1 < NT:
                        pre(t + 1)
                    for c in range(8):
                        bk = banks[c % 6]
                        bkey = 'b%d' % (c % 6)
                        for k in range(8):
                            S.op('pe', lambda e, bk=bk, k=k, c=c: e.matmul(bk[:], lhsT=wo[:, k, c * 128:(c + 1) * 128], rhs=mg[:, k, :], start=(k == 0), stop=(k == 7)),
                                 reads=['wo', 'mg'], writes=[bkey])
                        S.op('act', lambda e, bk=bk, c=c: e.activation(out=y[:, c, :], in_=bk[:], func=AF.Copy), reads=[bkey], writes=['y'])
                    postnorm_resid(y, xt[s], ('x', s), sq, 1, TT, banks[7], 'b7', ln_t, rstd, tmp)
                    S.dma('sp', dstv[:, :, t * TT:(t + 1) * TT], xt[s][:], reads=[('x', s)], semkey=('dxo', s))
                S.barrier()

        ffn_phase(xT, h1T, ffn_w[0], 0, "a")
        m1_phase()
        sb_phase()
        da_phase()
        m3_phase()
        ffn_phase(h2T, outT, ffn_w[1], 2, "b")
    return nc


_CACHE = {}


def _layout_inputs(inp, b):
    f = lambda a: np.ascontiguousarray(np.asarray(a, dtype=np.float32))
    d = {}
    d["xT"] = f(np.asarray(inp["x"])[b].T)
    d["cT"] = f(np.asarray(inp["c"])[b].reshape(8, 128).T)
    d["w_ada"] = f(inp["w_ada"][0])
    d["b_ada_r"] = f(np.asarray(inp["b_ada"])[0].reshape(72, 128).T)
    d["npre_r"] = f(np.asarray(inp["norm_pre"])[0].reshape(24, 128).T)
    d["npost_r"] = f(np.asarray(inp["norm_post"])[0].reshape(24, 128).T)
    d["f1_wg"] = f(inp["ffn1_w_gate"][0])
    d["f1_wu"] = f(inp["ffn1_w_up"][0])
    d["f1_wd"] = f(inp["ffn1_w_down"][0])
    d["f2_wg"] = f(inp["ffn2_w_gate"][0])
    d["f2_wu"] = f(inp["ffn2_w_up"][0])
    d["f2_wd"] = f(inp["ffn2_w_down"][0])
    d["w_in"] = f(inp["w_in"][0])
    lam = np.concatenate([np.asarray(inp["da_lambda_q1"])[0], np.asarray(inp["da_lambda_k1"])[0],
                          np.asarray(inp["da_lambda_q2"])[0], np.asarray(inp["da_lambda_k2"])[0]])
    d["lam_r"] = f(np.broadcast_to(lam[None, :], (128, 256)))
    d["subln_r"] = f(np.asarray(inp["da_subln"])[0].reshape(128, 1))
    d["w_bsb"] = f(inp["w_branch_sb"][0])
    d["w_bda"] = f(inp["w_branch_da"][0])
    d["w_out"] = f(inp["w_out"][0])
    return d


def kernel(**inputs):
    x = np.asarray(inputs["x"])
    B, S_LEN, _ = x.shape
    if S_LEN not in _CACHE:
        _CACHE[S_LEN] = build_program(S_LEN)
    nc = _CACHE[S_LEN]
    in_maps = [_layout_inputs(inputs, b) for b in range(B)]
    res = run_bass_kernel_spmd(nc, in_maps, core_ids=list(range(B)))
    out = np.empty((B, S_LEN, D), dtype=np.float32)
    for b in range(B):
        out[b] = np.asarray(res.results[b]["outT"]).T
    return out
```

```python
import math
import numpy as np
from contextlib import ExitStack
import concourse.bass as bass
import concourse.mybir as mybir
from concourse.bass_utils import run_bass_kernel_spmd

F32 = mybir.dt.float32
BF16 = mybir.dt.bfloat16
AF = mybir.ActivationFunctionType
ALU = mybir.AluOpType
AX = mybir.AxisListType

D = 1024
DFF = 2816
NF = DFF // 128
EPS = 1e-6
LAMBDA_INIT = 0.8 - 0.6 * math.exp(-0.3 * 0)
SLOPES = [2.0 ** (-8.0 * (h + 1) / 4) for h in range(4)]
NEGBIG = -30000.0


class Sched:
    EPOCH = 30000

    def __init__(self, nc, es):
        self.nc = nc
        self.es = es
        self.engs = {'pe': nc.tensor, 'act': nc.scalar, 'dve': nc.vector, 'pool': nc.gpsimd, 'sp': nc.sync}
        self.cnt = {k: 0 for k in self.engs}
        self.last = {}
        self.sems = {}
        self.semval = {}
        self.waited = {}
        self.res = {}
        self.nsem = 0
        self.dmakeys = set()

    def sem(self, key):
        if key not in self.sems:
            self.sems[key] = self.es.enter_context(self.nc.semaphore("s%d" % self.nsem))
            self.nsem += 1
            self.semval[key] = 0
        return self.sems[key]

    def _deps(self, eng, reads, writes):
        deps = {}

        def add(tok, kind):
            if tok is None:
                return
            k, v = tok
            if isinstance(k, tuple) and k[0] == 'E' and k[1] == eng:
                if eng == 'pe' or kind == 'war':
                    return
            if v > deps.get(k, 0):
                deps[k] = v
        for r in reads:
            st = self.res.get(r)
            if st:
                add(st[0], 'raw')
        for w in writes:
            st = self.res.get(w)
            if st:
                add(st[0], 'waw')
                for t in st[1].values():
                    add(t, 'war')
        for k, v in deps.items():
            if self.waited.get((eng, k), 0) < v:
                self.waited[(eng, k)] = v
                self.engs[eng].wait_ge(self.sems[k], v)

    def _commit(self, tok, reads, writes):
        for r in reads:
            st = self.res.setdefault(r, [None, {}])
            st[1][tok[0]] = tok
        for w in writes:
            self.res[w] = [tok, {}]

    def op(self, eng, fn, reads=(), writes=()):
        self._deps(eng, reads, writes)
        self.cnt[eng] += 1
        key = ('E', eng, self.cnt[eng] // self.EPOCH)
        s = self.sem(key)
        self.semval[key] += 1
        tok = (key, self.semval[key])
        fn(self.engs[eng]).then_inc(s, 1)
        self.last[eng] = tok
        self._commit(tok, reads, writes)
        return tok

    def dma(self, eng, out, in_, reads=(), writes=(), semkey=None, **kw):
        self._deps(eng, reads, writes)
        s = self.sem(semkey)
        self.dmakeys.add(semkey)
        self.semval[semkey] += 16
        tok = (semkey, self.semval[semkey])
        self.engs[eng].dma_start(out=out, in_=in_, **kw).then_inc(s, 16)
        self._commit(tok, reads, writes)
        return tok

    def barrier(self):
        for e in self.engs:
            for f, tok in self.last.items():
                if f == e:
                    continue
                k, v = tok
                if self.waited.get((e, k), 0) < v:
                    self.waited[(e, k)] = v
                    self.engs[e].wait_ge(self.sems[k], v)
            for k in self.dmakeys:
                v = self.semval[k]
                if v and self.waited.get((e, k), 0) < v:
                    self.waited[(e, k)] = v
                    self.engs[e].wait_ge(self.sems[k], v)
        self.res = {}


def build_program(S_LEN, debug=False):
    nc = bass.Bass("TRN2", target_bir_lowering=False)
    NQ = S_LEN // 512
    NKT = S_LEN // 128

    def din(name, shape, dt=F32):
        return nc.dram_tensor(name, shape, dt, kind="ExternalInput").ap()

    def dscr(name, shape, dt):
        return nc.dram_tensor(name, shape, dt, kind=("ExternalOutput" if debug else "Internal")).ap()

    xT = din("xT", [D, S_LEN])
    cT = din("cT", [128, 8])
    w_ada = din("w_ada", [D, 9 * D])
    b_ada_r = din("b_ada_r", [128, 72])
    npre_r = din("npre_r", [128, 24])
    npost_r = din("npost_r", [128, 24])
    ffn_w = []
    for i in (1, 2):
        ffn_w.append((din("f%d_wg" % i, [D, DFF]), din("f%d_wu" % i, [D, DFF]), din("f%d_wd" % i, [DFF, D])))
    w_in = din("w_in", [D, 5120])
    lam_r = din("lam_r", [128, 256])
    subln_r = din("subln_r", [128, 1])
    w_bsb = din("w_bsb", [512, D])
    w_bda = din("w_bda", [512, D])
    w_out = din("w_out", [D, D])
    outT = nc.dram_tensor("outT", [D, S_LEN], F32, kind="ExternalOutput").ap()

    h1T = dscr("h1T", [D, S_LEN], F32)
    h2T = dscr("h2T", [D, S_LEN], F32)
    qaT = dscr("qaT", [512, S_LEN], BF16)
    kaT = dscr("kaT", [512, S_LEN], BF16)
    vaS = dscr("vaS", [S_LEN, 512], BF16)
    qdT = dscr("qdT", [512, S_LEN], BF16)
    kdT = dscr("kdT", [512, S_LEN], BF16)
    vdS = dscr("vdS", [S_LEN, 512], BF16)
    yaT = dscr("yaT", [512, S_LEN], BF16)
    ydT = dscr("ydT", [512, S_LEN], BF16)

    with ExitStack() as es:
        S = Sched(nc, es)

        def sbt(st, name, shape, dt):
            return st.enter_context(nc.sbuf_tensor(name, shape, dt))

        pp = [es.enter_context(nc.psum_tensor("pp%d" % i, [128, 1024], F32)) for i in range(4)]
        banks = []
        for i in range(4):
            banks.append(pp[i][:, 0:512])
            banks.append(pp[i][:, 512:1024])

        ones = sbt(es, "ones", [128, 128], BF16)
        negones = sbt(es, "negones", [128, 128], BF16)
        negui = sbt(es, "negui", [128, 128], BF16)
        negI = sbt(es, "negI", [128, 128], BF16)
        modsb = sbt(es, "modsb", [128, 72], F32)
        Acoef = sbt(es, "Acoef", [128, 24], F32)
        Gcoef = sbt(es, "Gcoef", [128, 24], F32)
        neglam = sbt(es, "neglam", [128, 1], F32)
        sub8 = sbt(es, "sub8", [128, 1], F32)

        S.op('pool', lambda e: e.memset(ones[:], 1.0), writes=['ones'])
        S.op('pool', lambda e: e.memset(negones[:], -1.0), writes=['negones'])
        S.op('pool', lambda e: e.affine_select(out=negui[:], in_=negones[:], pattern=[[-1, 128]],
                                               compare_op=ALU.is_ge, fill=0.0, base=0, channel_multiplier=1),
             reads=['negones'], writes=['negui'])
        S.op('pool', lambda e: e.affine_select(out=negI[:], in_=negones[:], pattern=[[-1, 128]],
                                               compare_op=ALU.is_equal, fill=0.0, base=0, channel_multiplier=1),
             reads=['negones'], writes=['negI'])

        with ExitStack() as ph:
            cs = sbt(ph, "cs", [128, 8], F32)
            cs_in = sbt(ph, "cs_in", [128, 8], F32)
            bada = sbt(ph, "bada", [128, 72], F32)
            npre = sbt(ph, "npre", [128, 24], F32)
            npost = sbt(ph, "npost", [128, 24], F32)
            lamt = sbt(ph, "lamt", [128, 256], F32)
            lprod = sbt(ph, "lprod", [128, 128], F32)
            lsum = sbt(ph, "lsum", [128, 2], F32)
            lexp = sbt(ph, "lexp", [128, 2], F32)
            subt = sbt(ph, "subt", [128, 1], F32)
            GW = 1152
            wa = [sbt(ph, "wa%d" % i, [128, 8, GW], F32) for i in range(2)]
            S.dma('sp', cs_in[:], cT, writes=['cs_in'], semkey='d_c')
            S.dma('sp', bada[:], b_ada_r, writes=['bada'], semkey='d_b')
            S.dma('sp', npre[:], npre_r, writes=['npre'], semkey='d_np')
            S.dma('sp', npost[:], npost_r, writes=['npost'], semkey='d_npo')
            S.dma('sp', lamt[:], lam_r, writes=['lamt'], semkey='d_lam')
            S.dma('sp', subt[:], subln_r, writes=['subt'], semkey='d_sub')
            S.op('act', lambda e: e.activation(out=cs[:], in_=cs_in[:], func=AF.Silu), reads=['cs_in'], writes=['cs'])
            w_ada_v = w_ada.rearrange("(kc p) n -> p kc n", p=128)
            modps = banks[0]
            for g in range(8):
                b = g % 2
                S.dma('sp', wa[b][:], w_ada_v[:, :, g * GW:(g + 1) * GW], writes=[('wa', b)], semkey=('d_wa', b))
                for jj in range(9):
                    j = g * 9 + jj
                    for kc in range(8):
                        S.op('pe', lambda e, b=b, jj=jj, kc=kc, j=j: e.matmul(
                            modps[:, j:j + 1], lhsT=wa[b][:, kc, jj * 128:(jj + 1) * 128], rhs=cs[:, kc:kc + 1],
                            start=(kc == 0), stop=(kc == 7)),
                            reads=[('wa', b), 'cs'], writes=['modps'])
            S.op('dve', lambda e: e.tensor_tensor(out=modsb[:], in0=modps[:, 0:72], in1=bada[:], op=ALU.add),
                 reads=['modps', 'bada'], writes=['modsb'])
            for i in range(3):
                rw = 1.0 if i == 1 else 0.5
                S.op('dve', lambda e, i=i: e.scalar_tensor_tensor(
                    out=Acoef[:, i * 8:(i + 1) * 8], in0=modsb[:, (i * 3 + 1) * 8:(i * 3 + 2) * 8], scalar=1.0,
                    in1=npre[:, i * 8:(i + 1) * 8], op0=ALU.add, op1=ALU.mult),
                    reads=['modsb', 'npre'], writes=['Acoef'])
                S.op('dve', lambda e, i=i, rw=rw: e.scalar_tensor_tensor(
                    out=Gcoef[:, i * 8:(i + 1) * 8], in0=modsb[:, (i * 3 + 2) * 8:(i * 3 + 3) * 8], scalar=rw,
                    in1=npost[:, i * 8:(i + 1) * 8], op0=ALU.mult, op1=ALU.mult),
                    reads=['modsb', 'npost'], writes=['Gcoef'])
            S.op('dve', lambda e: e.tensor_tensor(out=lprod[:, 0:64], in0=lamt[:, 0:64], in1=lamt[:, 64:128], op=ALU.mult),
                 reads=['lamt'], writes=['lprod'])
            S.op('dve', lambda e: e.tensor_tensor(out=lprod[:, 64:128], in0=lamt[:, 128:192], in1=lamt[:, 192:256], op=ALU.mult),
                 reads=['lamt'], writes=['lprod2'])
            S.op('dve', lambda e: e.reduce_sum(out=lsum[:, 0:1], in_=lprod[:, 0:64], axis=AX.X),
                 reads=['lprod'], writes=['lsum'])
            S.op('dve', lambda e: e.reduce_sum(out=lsum[:, 1:2], in_=lprod[:, 64:128], axis=AX.X),
                 reads=['lprod2'], writes=['lsum2'])
            S.op('act', lambda e: e.activation(out=lexp[:], in_=lsum[:], func=AF.Exp), reads=['lsum', 'lsum2'], writes=['lexp'])
            S.op('dve', lambda e: e.scalar_tensor_tensor(out=neglam[:], in0=lexp[:, 1:2], scalar=-LAMBDA_INIT,
                                                         in1=lexp[:, 0:1], op0=ALU.add, op1=ALU.subtract),
                 reads=['lexp'], writes=['neglam'])
            S.op('dve', lambda e: e.tensor_scalar(out=sub8[:], in0=subt[:], scalar1=1.0 - LAMBDA_INIT, scalar2=None,
                                                  op0=ALU.mult),
                 reads=['subt'], writes=['sub8'])
            S.barrier()

        def load_w_bf16(dst, src, kcn, ncols, key, col0=0):
            v = src.rearrange("(kc p) n -> p kc n", p=128)
            c = 0
            while c < ncols:
                w = min(1024, ncols - c)
                S.dma('pool', dst[:, :, c:c + w], v[:, :, col0 + c:col0 + c + w], writes=[key], semkey=('dw', key))
                c += w

        def rstd_from(bank, TT, inv_n, ln_t, rstd, bkey, rkey):
            S.op('act', lambda e: e.activation(out=ln_t[:, 0:TT], in_=bank[:, 0:TT], func=AF.Ln, bias=EPS, scale=inv_n),
                 reads=[bkey], writes=[rkey + '_ln'])
            S.op('act', lambda e: e.activation(out=rstd[:, 0:TT], in_=ln_t[:, 0:TT], func=AF.Exp, scale=-0.5),
                 reads=[rkey + '_ln'], writes=[rkey])

        def prenorm(h, hkey, u, ukey, sq, i_sub, TT, bank, bkey, ln_t, rstd, tmp, do_sq=True):
            if do_sq:
                S.op('dve', lambda e: e.tensor_tensor(out=sq[:], in0=h[:], in1=h[:], op=ALU.mult), reads=[hkey], writes=['sq'])
            for c in range(8):
                S.op('pe', lambda e, c=c: e.matmul(bank[:, 0:TT], lhsT=ones[:], rhs=sq[:, c, :], start=(c == 0), stop=(c == 7)),
                     reads=['sq', 'ones'], writes=[bkey])
            rstd_from(bank, TT, 1.0 / D, ln_t, rstd, bkey, 'rstd')
            for c in range(8):
                col = i_sub * 8 + c
                S.op('dve', lambda e, c=c, col=col: e.scalar_tensor_tensor(
                    out=tmp[c % 2][:, 0:TT], in0=h[:, c, :], scalar=Acoef[:, col:col + 1], in1=rstd[:, 0:TT],
                    op0=ALU.mult, op1=ALU.mult), reads=[hkey, 'rstd', 'Acoef'], writes=[('tmp', c % 2)])
                scol = (i_sub * 3 + 0) * 8 + c
                S.op('act', lambda e, c=c, scol=scol: e.activation(
                    out=u[:, c, :], in_=tmp[c % 2][:, 0:TT], func=AF.Identity, bias=modsb[:, scol:scol + 1], scale=1.0),
                    reads=[('tmp', c % 2), 'modsb'], writes=[ukey])

        def postnorm_resid(y, h, hkey, sq, i_sub, TT, bank, bkey, ln_t, rstd, tmp):
            S.op('dve', lambda e: e.tensor_tensor(out=sq[:], in0=y[:], in1=y[:], op=ALU.mult), reads=['y'], writes=['sq'])
            for c in range(8):
                S.op('pe', lambda e, c=c: e.matmul(bank[:, 0:TT], lhsT=ones[:], rhs=sq[:, c, :], start=(c == 0), stop=(c == 7)),
                     reads=['sq', 'ones'], writes=[bkey])
            rstd_from(bank, TT, 1.0 / D, ln_t, rstd, bkey, 'rstd')
            for c in range(8):
                col = i_sub * 8 + c
                S.op('dve', lambda e, c=c, col=col: e.scalar_tensor_tensor(
                    out=tmp[c % 2][:, 0:TT], in0=y[:, c, :], scalar=Gcoef[:, col:col + 1], in1=rstd[:, 0:TT],
                    op0=ALU.mult, op1=ALU.mult), reads=['y', 'rstd', 'Gcoef'], writes=[('tmp', c % 2)])
                S.op('dve', lambda e, c=c: e.tensor_tensor(out=h[:, c, :], in0=tmp[c % 2][:, 0:TT], in1=h[:, c, :], op=ALU.add),
                     reads=[('tmp', c % 2), hkey], writes=[hkey])

        def ffn_phase(src, dst, wts, i_sub, tag):
            TT = 256
            NT = S_LEN // TT
            with ExitStack() as ph:
                wg = sbt(ph, "wg" + tag, [128, 8, DFF], BF16)
                wu = sbt(ph, "wu" + tag, [128, 8, DFF], BF16)
                wd = sbt(ph, "wd" + tag, [128, NF, D], BF16)
                xt = [sbt(ph, "xt%d%s" % (i, tag), [128, 8, TT], F32) for i in range(3)]
                u = [sbt(ph, "u%d%s" % (i, tag), [128, 8, TT], BF16) for i in range(2)]
                sq = sbt(ph, "sq" + tag, [128, 8, TT], BF16)
                act = sbt(ph, "act" + tag, [128, NF, TT], BF16)
                y = sbt(ph, "y" + tag, [128, 8, TT], F32)
                st = [sbt(ph, "st%d%s" % (i, tag), [128, TT], F32) for i in range(2)]
                tmp = [sbt(ph, "tmp%d%s" % (i, tag), [128, TT], F32) for i in range(2)]
                ln_t = sbt(ph, "ln" + tag, [128, TT], F32)
                rstd = sbt(ph, "rstd" + tag, [128, TT], F32)
                load_w_bf16(wg, wts[0], 8, DFF, 'wg')
                load_w_bf16(wu, wts[1], 8, DFF, 'wu')
                load_w_bf16(wd, wts[2], NF, D, 'wd')
                srcv = src.rearrange("(c p) t -> p c t", p=128)
                dstv = dst.rearrange("(c p) t -> p c t", p=128)

                def load(t):
                    s = t % 3
                    S.dma('sp', xt[s][:], srcv[:, :, t * TT:(t + 1) * TT], writes=[('x', s)], semkey=('dx', s))

                def pre(t):
                    prenorm(xt[t % 3], ('x', t % 3), u[t % 2], ('u', t % 2), sq, i_sub, TT, banks[7], 'b7', ln_t, rstd, tmp)

                load(0)
                if NT > 1:
                    load(1)
                pre(0)
                for t in range(NT):
                    s = t % 3
                    ut = u[t % 2]
                    ukey = ('u', t % 2)
                    if t + 2 < NT:
                        load(t + 2)
                    for f in range(NF):
                        bk = banks[f % 4]
                        bkey = 'b%d' % (f % 4)
                        for k in range(8):
                            S.op('pe', lambda e, bk=bk, f=f, k=k: e.matmul(
                                bk[:, 0:TT], lhsT=wg[:, k, f * 128:(f + 1) * 128], rhs=ut[:, k, :], start=(k == 0), stop=(k == 7)),
                                reads=['wg', ukey], writes=[bkey])
                        for k in range(8):
                            S.op('pe', lambda e, bk=bk, f=f, k=k: e.matmul(
                                bk[:, 256:256 + TT], lhsT=wu[:, k, f * 128:(f + 1) * 128], rhs=ut[:, k, :], start=(k == 0), stop=(k == 7)),
                                reads=['wu', ukey], writes=[bkey])
                        S.op('act', lambda e, bk=bk, f=f: e.activation(out=st[f % 2][:], in_=bk[:, 0:TT], func=AF.Silu),
                             reads=[bkey], writes=[('st', f % 2)])
                        S.op('dve', lambda e, bk=bk, f=f: e.tensor_tensor(out=act[:, f, :], in0=st[f % 2][:], in1=bk[:, 256:256 + TT], op=ALU.mult),
                             reads=[bkey, ('st', f % 2)], writes=['act'])
                    if t + 1 < NT:
                        pre(t + 1)
                    for c in range(8):
                        bk = banks[4 + c % 3]
                        bkey = 'b%d' % (4 + c % 3)
                        for f in range(NF):
                            S.op('pe', lambda e, bk=bk, f=f, c=c: e.matmul(
                                bk[:, 0:TT], lhsT=wd[:, f, c * 128:(c + 1) * 128], rhs=act[:, f, :], start=(f == 0), stop=(f == NF - 1)),
                                reads=['wd', 'act'], writes=[bkey])
                        S.op('act', lambda e, bk=bk, c=c: e.activation(out=y[:, c, :], in_=bk[:, 0:TT], func=AF.Copy),
                             reads=[bkey], writes=['y'])
                    postnorm_resid(y, xt[s], ('x', s), sq, i_sub, TT, banks[7], 'b7', ln_t, rstd, tmp)
                    S.dma('sp', dstv[:, :, t * TT:(t + 1) * TT], xt[s][:], reads=[('x', s)], semkey=('dxo', s))
                S.barrier()

        def m1_phase():
            TT = 512
            NT = S_LEN // TT
            with ExitStack() as ph:
                win = sbt(ph, "win", [128, 8, 3072], BF16)
                xt = [sbt(ph, "m1x%d" % i, [128, 8, TT], F32) for i in range(2)]
                u2 = [sbt(ph, "m1u%d" % i, [128, 8, TT], BF16) for i in range(2)]
                sq = sbt(ph, "m1sq", [128, 8, TT], BF16)
                tmp = [sbt(ph, "m1tmp%d" % i, [128, TT], F32) for i in range(2)]
                ln_t = sbt(ph, "m1ln", [128, TT], F32)
                rstd = sbt(ph, "m1rstd", [128, TT], F32)
                stg = {}
                for nm in ('qa', 'ka', 'qd', 'kd', 'va', 'vd'):
                    stg[nm] = [sbt(ph, "stg_%s%d" % (nm, i), [128, 4, 512], BF16) for i in range(2)]
                load_w_bf16(win, w_in, 8, 3072, 'win')
                srcv = h1T.rearrange("(c p) t -> p c t", p=128)
                fm = [('qa', 0, 0.125, qaT), ('ka', 512, 1.0, kaT), ('qd', 1536, 0.125, qdT), ('kd', 2048, 1.0, kdT)]
                tm = [('va', 1024, vaS), ('vd', 2560, vdS)]
                S.dma('sp', xt[0][:], srcv[:, :, 0:TT], writes=[('x', 0)], semkey=('dx', 0))
                bi = 0
                for t in range(NT):
                    s = t % 2
                    if t + 1 < NT:
                        S.dma('sp', xt[1 - s][:], srcv[:, :, (t + 1) * TT:(t + 2) * TT], writes=[('x', 1 - s)], semkey=('dx', 1 - s))
                    if t == 0:
                        prenorm(xt[0], ('x', 0), u2[0], ('u', 0), sq, 1, TT, banks[7], 'b7', ln_t, rstd, tmp)
                    u = u2[s]
                    ukey = ('u', s)
                    for (nm, c0, scl, dram) in fm:
                        sg = stg[nm][t % 2]
                        skey = ('stg', nm, t % 2)
                        for j in range(4):
                            bk = banks[bi % 6]
                            bkey = 'b%d' % (bi % 6)
                            bi += 1
                            for k in range(8):
                                S.op('pe', lambda e, bk=bk, k=k, j=j, c0=c0: e.matmul(
                                    bk[:], lhsT=win[:, k, c0 + j * 128:c0 + (j + 1) * 128], rhs=u[:, k, :], start=(k == 0), stop=(k == 7)),
                                    reads=['win', ukey], writes=[bkey])
                            S.op('act', lambda e, bk=bk, j=j, sg=sg, scl=scl: e.activation(out=sg[:, j, :], in_=bk[:], func=AF.Copy, scale=scl),
                                 reads=[bkey], writes=[skey])
                        S.dma('sp', dram.rearrange("(c p) t -> p c t", p=128)[:, :, t * TT:(t + 1) * TT], sg[:],
                              reads=[skey], semkey=('dstg', nm, t % 2))
                    if t + 1 < NT:
                        prenorm(xt[1 - s], ('x', 1 - s), u2[1 - s], ('u', 1 - s), sq, 1, TT, banks[7], 'b7', ln_t, rstd, tmp)
                    for (nm, c0, dram) in tm:
                        sg = stg[nm][t % 2]
                        skey = ('stg', nm, t % 2)
                        for j in range(4):
                            bk = banks[bi % 6]
                            bkey = 'b%d' % (bi % 6)
                            bi += 1
                            for k in range(8):
                                S.op('pe', lambda e, bk=bk, k=k, j=j, c0=c0: e.matmul(
                                    bk[:], lhsT=u[:, k, j * 128:(j + 1) * 128], rhs=win[:, k, c0:c0 + 512], start=(k == 0), stop=(k == 7)),
                                    reads=['win', ukey], writes=[bkey])
                            S.op('dve', lambda e, bk=bk, j=j, sg=sg: e.tensor_copy(out=sg[:, j, :], in_=bk[:]),
                                 reads=[bkey], writes=[skey])
                        S.dma('sp', dram[t * TT:(t + 1) * TT, :].rearrange("(s p) f -> p s f", p=128), sg[:],
                              reads=[skey], semkey=('dstg', nm, t % 2))
                S.barrier()

        def sb_phase():
            with ExitStack() as ph:
                vsb = sbt(ph, "vsb", [128, NKT, 512], BF16)
                kT = [sbt(ph, "kT%d" % i, [128, S_LEN], BF16) for i in range(2)]
                qT = [sbt(ph, "qT%d" % i, [128, 512], BF16) for i in range(2)]
                onesw = sbt(ph, "onesw", [128, 512], BF16)
                msb = [sbt(ph, "msb%d" % j, [128, 512], BF16) for j in range(4)]
                e_t = [sbt(ph, "e_t%d" % i, [128, 1024], F32) for i in range(2)]
                sp_t = [sbt(ph, "sp_t%d" % i, [128, 1024], BF16) for i in range(3)]
                w_t = [sbt(ph, "w_t%d" % i, [128, 1024], BF16) for i in range(3)]
                cbf = [sbt(ph, "cbf%d" % i, [128, 512], BF16) for i in range(3)]
                ystg = [sbt(ph, "ystg%d" % i, [128, 512], BF16) for i in range(2)]
                S.op('pool', lambda e: e.memset(onesw[:], 1.0), writes=['onesw'])
                for b_ in range(2):
                    S.op('pool', lambda e, b_=b_: e.memset(kT[b_][64:128, :], 0.0), writes=[('kTz', b_)])
                    S.op('pool', lambda e, b_=b_: e.memset(qT[b_][64:128, :], 0.0), writes=[('qTz', b_)])
                for j in range(4):
                    S.op('pool', lambda e, j=j: e.affine_select(out=msb[j][:], in_=onesw[:], pattern=[[1, 512]],
                                                                 compare_op=ALU.is_ge, fill=0.0, base=-128 * j - 1,
                                                                 channel_multiplier=-1),
                         reads=['onesw'], writes=[('msb', j)])
                S.dma('sp', vsb[:], vaS.rearrange("(i p) f -> p i f", p=128), writes=['vsb'], semkey='d_vsb')
                pairs = []
                sidx = 0
                for h in range(8):
                    for Q in range(NQ):
                        top = 4 * Q + 3
                        for ia in range(top, -1, -2):
                            ib = ia - 1
                            pairs.append(dict(h=h, Q=Q, ia=ia, ib=ib, first=(ia == top), last=(ib == 0), sidx=sidx,
                                              ja=(ia - 4 * Q if ia >= 4 * Q else None),
                                              jb=(ib - 4 * Q if ib >= 4 * Q else None), p=len(pairs)))
                        sidx += 1
                YK, YKEY = banks[6], 'b6'
                CK, CKEY = banks[7], 'b7'

                def load_k(h):
                    S.dma('sp', kT[h % 2][0:64, :], kaT[h * 64:(h + 1) * 64, :], writes=[('kT', h % 2)], semkey=('dk', h % 2))

                def load_q(h, Q, sidx):
                    S.dma('sp', qT[sidx % 2][0:64, :], qaT[h * 64:(h + 1) * 64, Q * 512:(Q + 1) * 512],
                          writes=[('qT', sidx % 2)], semkey=('dq', sidx % 2))

                def P1(pr):
                    h, Q, p = pr['h'], pr['Q'], pr['p']
                    if pr['first']:
                        if Q == 0:
                            if h == 0:
                                load_k(0)
                                load_q(0, 0, 0)
                            if h + 1 < 8:
                                load_k(h + 1)
                        nh, nQ = (h, Q + 1) if Q + 1 < NQ else (h + 1, 0)
                        if nh < 8:
                            load_q(nh, nQ, pr['sidx'] + 1)
                    zp = pp[p % 3]
                    zkey = ('zp', p % 3)
                    kt = kT[h % 2]
                    qt = qT[pr['sidx'] % 2]
                    rd = [('kT', h % 2), ('qT', pr['sidx'] % 2), ('kTz', h % 2), ('qTz', pr['sidx'] % 2)]
                    for half, i in ((0, pr['ia']), (1, pr['ib'])):
                        S.op('pe', lambda e, half=half, i=i: e.matmul(zp[:, half * 512:(half + 1) * 512], lhsT=kt[:, i * 128:(i + 1) * 128],
                                                                      rhs=qt[:], start=True, stop=False),
                             reads=rd, writes=[zkey])

                def A1(pr):
                    p = pr['p']
                    zp = pp[p % 3]
                    zkey = ('zp', p % 3)
                    et = e_t[p % 2]
                    spt = sp_t[p % 3]
                    S.op('act', lambda e: e.activation(out=et[:], in_=zp[:], func=AF.Exp), reads=[zkey], writes=[('e', p % 2)])
                    S.op('act', lambda e: e.activation(out=spt[:], in_=et[:], func=AF.Ln, bias=1.0, scale=1.0),
                         reads=[('e', p % 2)], writes=[('sp', p % 3)])
                    for half, j in ((0, pr['ja']), (1, pr['jb'])):
                        if j is not None:
                            S.op('pool', lambda e, half=half, j=j: e.tensor_tensor(
                                out=spt[:, half * 512:(half + 1) * 512], in0=spt[:, half * 512:(half + 1) * 512], in1=msb[j][:], op=ALU.mult),
                                reads=[('sp', p % 3), ('msb', j)], writes=[('sp', p % 3)])

                def P2(pr):
                    p = pr['p']
                    zp = pp[p % 3]
                    zkey = ('zp', p % 3)
                    spt = sp_t[p % 3]
                    spk = ('sp', p % 3)
                    za, zb_ = zp[:, 0:512], zp[:, 512:1024]
                    spa, spb = spt[:, 0:512], spt[:, 512:1024]
                    pc = (p - 1) % 3
                    S.op('pe', lambda e: e.matmul(za, lhsT=negui[:], rhs=spa, start=False, stop=True, skip_group_check=True),
                         reads=['negui', spk, zkey], writes=[zkey])
                    if not pr['first']:
                        S.op('pe', lambda e: e.matmul(za, lhsT=negI[:], rhs=cbf[pc][:], start=False, stop=True, skip_group_check=True),
                             reads=['negI', ('cbf', pc), zkey], writes=[zkey])
                    S.op('pe', lambda e: e.matmul(zb_, lhsT=negui[:], rhs=spb, start=False, stop=True, skip_group_check=True),
                         reads=['negui', spk, zkey], writes=[zkey])
                    S.op('pe', lambda e: e.matmul(zb_, lhsT=negones[:], rhs=spa, start=False, stop=True, skip_group_check=True),
                         reads=['negones', spk, zkey], writes=[zkey])
                    if not pr['first']:
                        S.op('pe', lambda e: e.matmul(zb_, lhsT=negI[:], rhs=cbf[pc][:], start=False, stop=True, skip_group_check=True),
                             reads=['negI', ('cbf', pc), zkey], writes=[zkey])
                    if not pr['last']:
                        S.op('pe', lambda e: e.matmul(CK, lhsT=ones[:], rhs=spa, start=pr['first'], stop=False, skip_group_check=True),
                             reads=['ones', spk] + ([] if pr['first'] else [CKEY]), writes=[CKEY])
                        S.op('pe', lambda e: e.matmul(CK, lhsT=ones[:], rhs=spb, start=False, stop=True, skip_group_check=True),
                             reads=['ones', spk, CKEY], writes=[CKEY])
                        S.op('dve', lambda e: e.tensor_copy(out=cbf[p % 3][:], in_=CK), reads=[CKEY], writes=[('cbf', p % 3)])

                def A2(pr):
                    p = pr['p']
                    zp = pp[p % 3]
                    zkey = ('zp', p % 3)
                    wt = w_t[p % 3]
                    S.op('act', lambda e: e.activation(out=wt[:], in_=zp[:], func=AF.Exp), reads=[zkey], writes=[('w', p % 3)])
                    for half, j in ((0, pr['ja']), (1, pr['jb'])):
                        if j is not None:
                            S.op('pool', lambda e, half=half, j=j: e.tensor_tensor(
                                out=wt[:, half * 512:(half + 1) * 512], in0=wt[:, half * 512:(half + 1) * 512], in1=msb[j][:], op=ALU.mult),
                                reads=[('w', p % 3), ('msb', j)], writes=[('w', p % 3)])

                def P3(pr):
                    h, Q, p = pr['h'], pr['Q'], pr['p']
                    wt = w_t[p % 3]
                    hp = h // 2
                    ho = (h % 2) * 64
                    for half, i in ((0, pr['ia']), (1, pr['ib'])):
                        st_ = pr['first'] and half == 0
                        S.op('pe', lambda e, half=half, i=i, st_=st_: e.matmul(
                            YK, lhsT=vsb[:, i, hp * 128:(hp + 1) * 128], rhs=wt[:, half * 512:(half + 1) * 512],
                            start=st_, stop=(pr['last'] and half == 1), skip_group_check=True),
                            reads=['vsb', ('w', p % 3)] + ([] if st_ else [YKEY]), writes=[YKEY])
                    if pr['last']:
                        sl = pr['sidx'] % 2
                        S.op('dve', lambda e: e.tensor_copy(out=ystg[sl][ho:ho + 64, :], in_=YK[ho:ho + 64, :]), reads=[YKEY], writes=[('ystg', sl)])
                        S.dma('sp', yaT[h * 64:(h + 1) * 64, Q * 512:(Q + 1) * 512], ystg[sl][ho:ho + 64, :], reads=[('ystg', sl)], semkey=('dyo', sl))

                n = len(pairs)
                P1(pairs[0])
                for s in range(n + 2):
                    if 0 <= s - 1 < n:
                        P2(pairs[s - 1])
                    if s < n:
                        A1(pairs[s])
                    if 0 <= s - 1 < n:
                        A2(pairs[s - 1])
                    if s + 1 < n:
                        P1(pairs[s + 1])
                    if 0 <= s - 2 < n:
                        P3(pairs[s - 2])
                S.barrier()

        def da_phase():
            with ExitStack() as ph:
                vda = [sbt(ph, "vda%d" % i, [128, NKT, 128], BF16) for i in range(2)]
                kaug = [[sbt(ph, "kaug%d_%d" % (b, m), [128, S_LEN], BF16) for m in range(2)] for b in range(2)]
                qaug = [[[sbt(ph, "qaug%d_%d_%d" % (h, b, m), [128, 512], BF16) for m in range(2)] for b in range(2)] for h in range(4)]
                fullb = [[sbt(ph, "fb%d_%d" % (h, j), [128, 512], F32) for j in range(4)] for h in range(4)]
                bvi = sbt(ph, "bvi", [128, 64], F32)
                bv = [sbt(ph, "bv%d" % h, [128, 64], F32) for h in range(4)]
                augf = sbt(ph, "augf", [128, 512], F32)
                dtmp = sbt(ph, "dtmp", [128, 512], F32)
                dtmp2 = sbt(ph, "dtmp2", [128, 512], F32)
                arg_t = [sbt(ph, "arg%d" % i, [128, 1024], F32) for i in range(2)]
                w_t = [sbt(ph, "dw%d" % i, [128, 1024], BF16) for i in range(3)]
                wsum_t = [sbt(ph, "wsum%d" % m, [128, 512], F32) for m in range(2)]
                ones32 = sbt(ph, "ones32", [128, 128], F32)
                S.op('pool', lambda e: e.memset(ones32[:], 1.0), writes=['ones32'])
                r_t = [sbt(ph, "dr%d" % i, [128, 512], F32) for i in range(2)]
                t_t = [sbt(ph, "dt%d" % i, [128, 512], F32) for i in range(2)]
                yc_t = [sbt(ph, "dyc%d" % i, [128, 512], F32) for i in range(2)]
                yo = [sbt(ph, "dyo%d" % i, [128, 512], BF16) for i in range(2)]
                S.op('pool', lambda e: e.iota(out=augf[64:96, :], pattern=[[2, 256], [0, 2]], base=0, channel_multiplier=0,
                                              allow_small_or_imprecise_dtypes=True), writes=['augfa'])
                S.op('pool', lambda e: e.iota(out=augf[96:128, :], pattern=[[0, 256], [1, 2]], base=0, channel_multiplier=0,
                                              allow_small_or_imprecise_dtypes=True), writes=['augfb'])
                for h in range(4):
                    for b in range(2):
                        for m in range(2):
                            S.op('dve', lambda e, h=h, b=b, m=m: e.tensor_scalar(
                                out=qaug[h][b][m][64:128, :], in0=augf[64:128, :], scalar1=-SLOPES[h], scalar2=None, op0=ALU.mult),
                                reads=['augfa', 'augfb'], writes=[('qaug_aug', h, b, m)])
                for b in range(2):
                    for m in range(2):
                        ka = kaug[b][m]
                        S.op('pool', lambda e, ka=ka: e.memset(ka[64:128, :], 0.0), writes=[('kaug_aug', b, m)])
                        S.op('pool', lambda e, ka=ka: e.memset(ka[64:65, :], 1.0), reads=[('kaug_aug', b, m)], writes=[('kaug_aug', b, m)])
                        S.op('pool', lambda e, ka=ka: e.memset(ka[96:97, :], 1.0), reads=[('kaug_aug', b, m)], writes=[('kaug_aug', b, m)])
                S.op('pool', lambda e: e.iota(out=bvi[:], pattern=[[-128, 64]], base=-128, channel_multiplier=1,
                                              allow_small_or_imprecise_dtypes=True), writes=['bvi'])
                for h in range(4):
                    S.op('dve', lambda e, h=h: e.tensor_scalar(out=bv[h][:], in0=bvi[:], scalar1=SLOPES[h], scalar2=None, op0=ALU.mult),
                         reads=['bvi'], writes=[('bv', h)])
                ttf = sbt(ph, "ttf", [128, 512], F32)
                S.op('pool', lambda e: e.iota(out=ttf[:], pattern=[[1, 512]], base=0, channel_multiplier=0,
                                              allow_small_or_imprecise_dtypes=True), writes=['ttf'])
                for j in range(4):
                    S.op('pool', lambda e, j=j: e.iota(out=dtmp[:], pattern=[[-1, 512]], base=128 * j, channel_multiplier=1,
                                                       allow_small_or_imprecise_dtypes=True), writes=['dtmp'])
                    for h in range(4):
                        S.op('dve', lambda e, h=h: e.tensor_scalar(out=dtmp2[:], in0=dtmp[:], scalar1=SLOPES[h], scalar2=None,
                                                                   op0=ALU.mult),
                             reads=['dtmp'], writes=['dtmp2'])
                        S.op('dve', lambda e, h=h: e.scalar_tensor_tensor(out=dtmp2[:], in0=dtmp[:], scalar=-SLOPES[h], in1=dtmp2[:],
                                                                          op0=ALU.mult, op1=ALU.min),
                             reads=['dtmp', 'dtmp2'], writes=['dtmp2'])
                        S.op('dve', lambda e, h=h: e.scalar_tensor_tensor(out=dtmp2[:], in0=ttf[:], scalar=SLOPES[h], in1=dtmp2[:],
                                                                          op0=ALU.mult, op1=ALU.add),
                             reads=['dtmp2', 'ttf'], writes=['dtmp2'])
                        S.op('pool', lambda e, h=h, j=j: e.affine_select(
                            out=fullb[h][j][:], in_=dtmp2[:], pattern=[[64, 8], [0, 64]], compare_op=ALU.is_ge, fill=NEGBIG,
                            base=63 - 128 * j, channel_multiplier=-1), reads=['dtmp2'], writes=[('fullb', h, j)])

                items = []
                sidx = 0
                for h in range(4):
                    for Q in range(NQ):
                        top = 4 * Q + 3
                        for i in range(0, top + 1):
                            items.append(dict(h=h, Q=Q, i=i, first=(i == 0), last=(i == top), sidx=sidx,
                                              diag=(i - 4 * Q if i >= 4 * Q else None), idx=len(items)))
                        sidx += 1

                vdv = vdS.rearrange("(i p) f -> p i f", p=128)

                def load_k(h):
                    for m in range(2):
                        S.dma('sp', kaug[h % 2][m][0:64, :], kdT[h * 128 + m * 64:h * 128 + (m + 1) * 64, :],
                              writes=[('kaug', h % 2, m)], semkey=('dkd', h % 2, m))

                def load_v(h):
                    S.dma('sp', vda[h % 2][:], vdv[:, :, h * 128:(h + 1) * 128], writes=[('vda', h % 2)], semkey=('dvd', h % 2))

                def load_q(h, Q, sidx):
                    for m in range(2):
                        S.dma('sp', qaug[h][sidx % 2][m][0:64, :], qdT[h * 128 + m * 64:h * 128 + (m + 1) * 64, Q * 512:(Q + 1) * 512],
                              writes=[('qaug', h, sidx % 2, m)], semkey=('dqd', sidx % 2, m))

                YB = [(banks[4], 'b4'), (banks[5], 'b5')]
                LB = [(banks[6], 'b6'), (banks[7], 'b7')]

                def stA(it):
                    h, Q, i, idx = it['h'], it['Q'], it['i'], it['idx']
                    if it['first']:
                        if Q == 0:
                            if h == 0:
                                load_k(0)
                                load_v(0)
                                load_q(0, 0, 0)
                            if h + 1 < 4:
                                load_k(h + 1)
                        nh, nQ = (h, Q + 1) if Q + 1 < NQ else (h + 1, 0)
                        if nh < 4:
                            load_q(nh, nQ, it['sidx'] + 1)
                    sb_ = it['sidx'] % 2
                    for m in range(2):
                        bn = (idx % 2) * 2 + m
                        bk, bkey = banks[bn], 'b%d' % bn
                        ka = kaug[h % 2][m]
                        qa = qaug[h][sb_][m]
                        S.op('pe', lambda e, bk=bk, ka=ka, qa=qa: e.matmul(bk[:], lhsT=ka[:, i * 128:(i + 1) * 128], rhs=qa[:], start=True, stop=True),
                             reads=[('kaug', h % 2, m), ('kaug_aug', h % 2, m), ('qaug', h, sb_, m), ('qaug_aug', h, sb_, m)], writes=[bkey])

                def stA2(it):
                    h, Q, i, idx = it['h'], it['Q'], it['i'], it['idx']
                    pi = idx % 2
                    zp = pp[pi]
                    bkeys = ['b%d' % (pi * 2), 'b%d' % (pi * 2 + 1)]
                    wt = w_t[idx % 3]
                    wkey = ('w', idx % 3)
                    if it['diag'] is None:
                        n_idx = 4 * Q - i - 1
                        S.op('act', lambda e: e.activation(out=wt[:], in_=zp[:], func=AF.Exp, bias=bv[h][:, n_idx:n_idx + 1], scale=1.0),
                             reads=bkeys + [('bv', h)], writes=[wkey])
                    else:
                        j = it['diag']
                        at = arg_t[pi]
                        for m in range(2):
                            S.op('dve', lambda e, m=m: e.tensor_tensor(out=at[:, m * 512:(m + 1) * 512], in0=zp[:, m * 512:(m + 1) * 512],
                                                                       in1=fullb[h][j][:], op=ALU.add),
                                 reads=[bkeys[m], ('fullb', h, j)], writes=[('arg', pi)])
                        S.op('act', lambda e: e.activation(out=wt[:], in_=at[:], func=AF.Exp), reads=[('arg', pi)], writes=[wkey])

                def stB(it):
                    h, Q, i, idx = it['h'], it['Q'], it['i'], it['idx']
                    if it['first'] and Q == 0 and h + 1 < 4:
                        load_v(h + 1)
                    wt = w_t[idx % 3]
                    wkey = ('w', idx % 3)
                    sl = it['sidx'] % 2
                    for m in range(2):
                        yk, ykey = YB[m]
                        S.op('pe', lambda e, yk=yk, m=m: e.matmul(yk, lhsT=vda[h % 2][:, i, :], rhs=wt[:, m * 512:(m + 1) * 512],
                                                               start=it['first'], stop=it['last']),
                             reads=[('vda', h % 2), wkey] + ([] if it['first'] else [ykey]), writes=[ykey])
                    for m in range(2):
                        lk, lkey = LB[m]
                        S.op('pe', lambda e, lk=lk, m=m: e.matmul(lk, lhsT=ones[:], rhs=wt[:, m * 512:(m + 1) * 512], start=it['first'], stop=it['last']),
                             reads=['ones', wkey] + ([] if it['first'] else [lkey]), writes=[lkey])
                    if it['last']:
                        for m in range(2):
                            S.op('act', lambda e, m=m: e.activation(out=yc_t[m][:], in_=YB[m][0], func=AF.Copy), reads=[YB[m][1]], writes=[('yc', m)])
                            S.op('dve', lambda e, m=m: e.tensor_copy(out=wsum_t[m][:], in_=LB[m][0]), reads=[LB[m][1]], writes=[('wsum', m)])
                        for m in range(2):
                            S.op('dve', lambda e, m=m: e.reciprocal(out=r_t[m][:], in_=wsum_t[m][:]), reads=[('wsum', m)], writes=[('r', m)])
                        for m in range(2):
                            S.op('dve', lambda e, m=m: e.tensor_tensor(out=t_t[m][:], in0=yc_t[m][:], in1=r_t[m][:], op=ALU.mult),
                                 reads=[('yc', m), ('r', m)], writes=[('t', m)])
                        S.op('dve', lambda e: e.scalar_tensor_tensor(out=yo[sl][:], in0=t_t[1][:], scalar=neglam[:, 0:1], in1=t_t[0][:],
                                                                     op0=ALU.mult, op1=ALU.add),
                             reads=[('t', 0), ('t', 1), 'neglam'], writes=[('yo', sl)])
                        S.dma('sp', ydT[h * 128:(h + 1) * 128, Q * 512:(Q + 1) * 512], yo[sl][:], reads=[('yo', sl)], semkey=('dydo', sl))

                n = len(items)
                for s in range(n + 2):
                    if 0 <= s - 2 < n:
                        stB(items[s - 2])
                    if 0 <= s - 1 < n:
                        stA2(items[s - 1])
                    if s < n:
                        stA(items[s])
                S.barrier()

        def m3_phase():
            TT = 512
            NT = S_LEN // TT
            with ExitStack() as ph:
                wsb = sbt(ph, "m3wsb", [128, 4, D], BF16)
                wda = sbt(ph, "m3wda", [128, 4, D], BF16)
                wo = sbt(ph, "m3wo", [128, 8, D], BF16)
                wgt = sbt(ph, "m3wgt", [128, 8, 2048], BF16)
                xt = [sbt(ph, "m3x%d" % i, [128, 8, TT], F32) for i in range(2)]
                yat = [sbt(ph, "m3ya%d" % i, [128, 4, TT], BF16) for i in range(2)]
                ydt = [sbt(ph, "m3yd%d" % i, [128, 4, TT], BF16) for i in range(2)]
                u2 = [sbt(ph, "m3u%d" % i, [128, 8, TT], BF16) for i in range(2)]
                sq = sbt(ph, "m3sq", [128, 8, TT], BF16)
                mg = sbt(ph, "m3mg", [128, 8, TT], BF16)
                sqy = sbt(ph, "m3sqy", [128, 4, TT], BF16)
                ydn = [sbt(ph, "m3ydn%d" % i, [128, 4, TT], BF16) for i in range(2)]
                ln2 = [sbt(ph, "m3ln2%d" % i, [128, TT], F32) for i in range(2)]
                rs2 = [sbt(ph, "m3rs2%d" % i, [128, TT], F32) for i in range(2)]
                y = sbt(ph, "m3y", [128, 8, TT], F32)
                sg_t = [sbt(ph, "m3sg%d" % i, [128, TT], F32) for i in range(4)]
                mm_t = [sbt(ph, "m3mm%d" % i, [128, TT], F32) for i in range(4)]
                tmp = [sbt(ph, "m3tmp%d" % i, [128, TT], F32) for i in range(2)]
                ln_t = sbt(ph, "m3ln", [128, TT], F32)
                rstd = sbt(ph, "m3rstd", [128, TT], F32)
                load_w_bf16(wsb, w_bsb, 4, D, 'wsb')
                load_w_bf16(wda, w_bda, 4, D, 'wda')
                load_w_bf16(wgt, w_in, 8, 2048, 'wgt', col0=3072)
                load_w_bf16(wo, w_out, 8, D, 'wo')
                srcv = h1T.rearrange("(c p) t -> p c t", p=128)
                dstv = h2T.rearrange("(c p) t -> p c t", p=128)
                yav = yaT.rearrange("(c p) t -> p c t", p=128)
                ydv = ydT.rearrange("(c p) t -> p c t", p=128)

                def load(t):
                    s = t % 2
                    S.dma('sp', xt[s][:], srcv[:, :, t * TT:(t + 1) * TT], writes=[('x', s)], semkey=('dx', s))
                    S.dma('sp', yat[s][:], yav[:, :, t * TT:(t + 1) * TT], writes=[('ya', s)], semkey=('dya', s))
                    S.dma('sp', ydt[s][:], ydv[:, :, t * TT:(t + 1) * TT], writes=[('yd', s)], semkey=('dyd', s))

                def pre_sq(t):
                    s_ = t % 2
                    S.op('dve', lambda e: e.tensor_tensor(out=sq[:], in0=xt[s_][:], in1=xt[s_][:], op=ALU.mult), reads=[('x', s_)], writes=['sq'])
                    S.op('dve', lambda e: e.tensor_tensor(out=sqy[:], in0=ydt[s_][:], in1=ydt[s_][:], op=ALU.mult),
                         reads=[('yd', s_)], writes=['sqy'])

                def pre(t):
                    prenorm(xt[t % 2], ('x', t % 2), u2[t % 2], ('u', t % 2), sq, 1, TT, banks[7], 'b7', ln_t, rstd, tmp, do_sq=False)
                    s_ = t % 2
                    for k in range(4):
                        bk, bkey = banks[6 + k % 2], 'b%d' % (6 + k % 2)
                        S.op('pe', lambda e, bk=bk, k=k: e.matmul(bk, lhsT=ones[:], rhs=sqy[:, k, :], start=True, stop=True),
                             reads=['ones', 'sqy'], writes=[bkey])
                        S.op('act', lambda e, bk=bk, k=k: e.activation(out=ln2[k % 2][:], in_=bk, func=AF.Ln, bias=EPS, scale=1.0 / 128),
                             reads=[bkey], writes=[('ln2', k % 2)])
                        S.op('act', lambda e, k=k: e.activation(out=rs2[k % 2][:], in_=ln2[k % 2][:], func=AF.Exp, scale=-0.5),
                             reads=[('ln2', k % 2)], writes=[('rs2', k % 2)])
                        S.op('dve', lambda e, k=k: e.scalar_tensor_tensor(out=ydn[s_][:, k, :], in0=ydt[s_][:, k, :], scalar=sub8[:, 0:1],
                                                                          in1=rs2[k % 2][:], op0=ALU.mult, op1=ALU.mult),
                             reads=[('yd', s_), ('rs2', k % 2), 'sub8'], writes=[('ydn', s_)])

                load(0)
                pre_sq(0)
                pre(0)
                for t in range(NT):
                    s = t % 2
                    u = u2[s]
                    ukey = ('u', s)
                    if t + 1 < NT:
                        load(t + 1)
                    for c in range(8):
                        if c == 4 and t + 1 < NT:
                            pre_sq(t + 1)
                        cs_ = slice(c * 128, (c + 1) * 128)
                        b0 = (c % 2) * 4
                        bA, bB, bGa, bGd = banks[b0], banks[b0 + 1], banks[b0 + 2], banks[b0 + 3]
                        kA, kB, kGa, kGd = ['b%d' % (b0 + i) for i in range(4)]
                        r0 = (c % 2) * 2
                        for k in range(4):
                            S.op('pe', lambda e, k=k, cs_=cs_, bA=bA: e.matmul(bA[:], lhsT=wsb[:, k, cs_], rhs=yat[s][:, k, :], start=(k == 0), stop=(k == 3)),
                                 reads=['wsb', ('ya', s)], writes=[kA])
                        for k in range(4):
                            S.op('pe', lambda e, k=k, cs_=cs_, bB=bB: e.matmul(bB[:], lhsT=wda[:, k, cs_], rhs=ydn[s][:, k, :], start=(k == 0), stop=(k == 3)),
                                 reads=['wda', ('ydn', s)], writes=[kB])
                        for k in range(8):
                            S.op('pe', lambda e, k=k, c=c, bGa=bGa: e.matmul(bGa[:], lhsT=wgt[:, k, c * 128:(c + 1) * 128], rhs=u[:, k, :], start=(k == 0), stop=(k == 7)),
                                 reads=['wgt', ukey], writes=[kGa])
                        for k in range(8):
                            S.op('pe', lambda e, k=k, c=c, bGd=bGd: e.matmul(bGd[:], lhsT=wgt[:, k, 1024 + c * 128:1024 + (c + 1) * 128], rhs=u[:, k, :], start=(k == 0), stop=(k == 7)),
                                 reads=['wgt', ukey], writes=[kGd])
                        S.op('act', lambda e, bGa=bGa, r0=r0: e.activation(out=sg_t[r0][:], in_=bGa[:], func=AF.Sigmoid), reads=[kGa], writes=[('sg', r0)])
                        S.op('act', lambda e, bGd=bGd, r0=r0: e.activation(out=sg_t[r0 + 1][:], in_=bGd[:], func=AF.Sigmoid), reads=[kGd], writes=[('sg', r0 + 1)])
                        S.op('dve', lambda e, bA=bA, r0=r0: e.tensor_tensor(out=mm_t[r0][:], in0=bA[:], in1=sg_t[r0][:], op=ALU.mult),
                             reads=[kA, ('sg', r0)], writes=[('mm', r0)])
                        S.op('dve', lambda e, bB=bB, r0=r0: e.tensor_tensor(out=mm_t[r0 + 1][:], in0=bB[:], in1=sg_t[r0 + 1][:], op=ALU.mult),
                             reads=[kB, ('sg', r0 + 1)], writes=[('mm', r0 + 1)])
                        S.op('dve', lambda e, c=c, r0=r0: e.tensor_tensor(out=mg[:, c, :], in0=mm_t[r0][:], in1=mm_t[r0 + 1][:], op=ALU.add),
                             reads=[('mm', r0), ('mm', r0 + 1)], writes=['mg'])
                    if t + 1 < NT:
                        pre(t + 1)
                    for c in range(8):
                        bk = banks[c % 6]
                        bkey = 'b%d' % (c % 6)
                        for k in range(8):
                            S.op('pe', lambda e, bk=bk, k=k, c=c: e.matmul(bk[:], lhsT=wo[:, k, c * 128:(c + 1) * 128], rhs=mg[:, k, :], start=(k == 0), stop=(k == 7)),
                                 reads=['wo', 'mg'], writes=[bkey])
                        S.op('act', lambda e, bk=bk, c=c: e.activation(out=y[:, c, :], in_=bk[:], func=AF.Copy), reads=[bkey], writes=['y'])
                    postnorm_resid(y, xt[s], ('x', s), sq, 1, TT, banks[7], 'b7', ln_t, rstd, tmp)
                    S.dma('sp', dstv[:, :, t * TT:(t + 1) * TT], xt[s][:], reads=[('x', s)], semkey=('dxo', s))
                S.barrier()

        ffn_phase(xT, h1T, ffn_w[0], 0, "a")
        m1_phase()
        sb_phase()
        da_phase()
        m3_phase()
        ffn_phase(h2T, outT, ffn_w[1], 2, "b")
    return nc


_CACHE = {}


def _layout_inputs(inp, b):
    f = lambda a: np.ascontiguousarray(np.asarray(a, dtype=np.float32))
    d = {}
    d["xT"] = f(np.asarray(inp["x"])[b].T)
    d["cT"] = f(np.asarray(inp["c"])[b].reshape(8, 128).T)
    d["w_ada"] = f(inp["w_ada"][0])
    d["b_ada_r"] = f(np.asarray(inp["b_ada"])[0].reshape(72, 128).T)
    d["npre_r"] = f(np.asarray(inp["norm_pre"])[0].reshape(24, 128).T)
    d["npost_r"] = f(np.asarray(inp["norm_post"])[0].reshape(24, 128).T)
    d["f1_wg"] = f(inp["ffn1_w_gate"][0])
    d["f1_wu"] = f(inp["ffn1_w_up"][0])
    d["f1_wd"] = f(inp["ffn1_w_down"][0])
    d["f2_wg"] = f(inp["ffn2_w_gate"][0])
    d["f2_wu"] = f(inp["ffn2_w_up"][0])
    d["f2_wd"] = f(inp["ffn2_w_down"][0])
    d["w_in"] = f(inp["w_in"][0])
    lam = np.concatenate([np.asarray(inp["da_lambda_q1"])[0], np.asarray(inp["da_lambda_k1"])[0],
                          np.asarray(inp["da_lambda_q2"])[0], np.asarray(inp["da_lambda_k2"])[0]])
    d["lam_r"] = f(np.broadcast_to(lam[None, :], (128, 256)))
    d["subln_r"] = f(np.asarray(inp["da_subln"])[0].reshape(128, 1))
    d["w_bsb"] = f(inp["w_branch_sb"][0])
    d["w_bda"] = f(inp["w_branch_da"][0])
    d["w_out"] = f(inp["w_out"][0])
    return d


def kernel(**inputs):
    x = np.asarray(inputs["x"])
    B, S_LEN, _ = x.shape
    if S_LEN not in _CACHE:
        _CACHE[S_LEN] = build_program(S_LEN)
    nc = _CACHE[S_LEN]
    in_maps = [_layout_inputs(inputs, b) for b in range(B)]
    res = run_bass_kernel_spmd(nc, in_maps, core_ids=list(range(B)))
    out = np.empty((B, S_LEN, D), dtype=np.float32)
    for b in range(B):
        out[b] = np.asarray(res.results[b]["outT"]).T
    return out
```

```python
import math
import numpy as np
from contextlib import ExitStack
import concourse.bass as bass
import concourse.mybir as mybir
from concourse.bass_utils import run_bass_kernel_spmd

F32 = mybir.dt.float32
BF16 = mybir.dt.bfloat16
AF = mybir.ActivationFunctionType
ALU = mybir.AluOpType
AX = mybir.AxisListType

D = 1024
DFF = 2816
NF = DFF // 128
EPS = 1e-6
LAMBDA_INIT = 0.8 - 0.6 * math.exp(-0.3 * 0)
SLOPES = [2.0 ** (-8.0 * (h + 1) / 4) for h in range(4)]
NEGBIG = -30000.0


class Sched:
    EPOCH = 30000

    def __init__(self, nc, es):
        self.nc = nc
        self.es = es
        self.engs = {'pe': nc.tensor, 'act': nc.scalar, 'dve': nc.vector, 'pool': nc.gpsimd, 'sp': nc.sync}
        self.cnt = {k: 0 for k in self.engs}
        self.last = {}
        self.sems = {}
        self.semval = {}
        self.waited = {}
        self.res = {}
        self.nsem = 0
        self.dmakeys = set()

    def sem(self, key):
        if key not in self.sems:
            self.sems[key] = self.es.enter_context(self.nc.semaphore("s%d" % self.nsem))
            self.nsem += 1
            self.semval[key] = 0
        return self.sems[key]

    def _deps(self, eng, reads, writes):
        deps = {}

        def add(tok, kind):
            if tok is None:
                return
            k, v = tok
            if isinstance(k, tuple) and k[0] == 'E' and k[1] == eng:
                if eng == 'pe' or kind == 'war':
                    return
            if v > deps.get(k, 0):
                deps[k] = v
        for r in reads:
            st = self.res.get(r)
            if st:
                add(st[0], 'raw')
        for w in writes:
            st = self.res.get(w)
            if st:
                add(st[0], 'waw')
                for t in st[1].values():
                    add(t, 'war')
        for k, v in deps.items():
            if self.waited.get((eng, k), 0) < v:
                self.waited[(eng, k)] = v
                self.engs[eng].wait_ge(self.sems[k], v)

    def _commit(self, tok, reads, writes):
        for r in reads:
            st = self.res.setdefault(r, [None, {}])
            st[1][tok[0]] = tok
        for w in writes:
            self.res[w] = [tok, {}]

    def op(self, eng, fn, reads=(), writes=()):
        self._deps(eng, reads, writes)
        self.cnt[eng] += 1
        key = ('E', eng, self.cnt[eng] // self.EPOCH)
        s = self.sem(key)
        self.semval[key] += 1
        tok = (key, self.semval[key])
        fn(self.engs[eng]).then_inc(s, 1)
        self.last[eng] = tok
        self._commit(tok, reads, writes)
        return tok

    def dma(self, eng, out, in_, reads=(), writes=(), semkey=None, **kw):
        self._deps(eng, reads, writes)
        s = self.sem(semkey)
        self.dmakeys.add(semkey)
        self.semval[semkey] += 16
        tok = (semkey, self.semval[semkey])
        self.engs[eng].dma_start(out=out, in_=in_, **kw).then_inc(s, 16)
        self._commit(tok, reads, writes)
        return tok

    def barrier(self):
        for e in self.engs:
            for f, tok in self.last.items():
                if f == e:
                    continue
                k, v = tok
                if self.waited.get((e, k), 0) < v:
                    self.waited[(e, k)] = v
                    self.engs[e].wait_ge(self.sems[k], v)
            for k in self.dmakeys:
                v = self.semval[k]
                if v and self.waited.get((e, k), 0) < v:
                    self.waited[(e, k)] = v
                    self.engs[e].wait_ge(self.sems[k], v)
        self.res = {}


def build_program(S_LEN, debug=False):
    nc = bass.Bass("TRN2", target_bir_lowering=False)
    NQ = S_LEN // 512
    NKT = S_LEN // 128

    def din(name, shape, dt=F32):
        return nc.dram_tensor(name, shape, dt, kind="ExternalInput").ap()

    def dscr(name, shape, dt):
        return nc.dram_tensor(name, shape, dt, kind=("ExternalOutput" if debug else "Internal")).ap()

    xT = din("xT", [D, S_LEN])
    cT = din("cT", [128, 8])
    w_ada = din("w_ada", [D, 9 * D])
    b_ada_r = din("b_ada_r", [128, 72])
    npre_r = din("npre_r", [128, 24])
    npost_r = din("npost_r", [128, 24])
    ffn_w = []
    for i in (1, 2):
        ffn_w.append((din("f%d_wg" % i, [D, DFF]), din("f%d_wu" % i, [D, DFF]), din("f%d_wd" % i, [DFF, D])))
    w_in = din("w_in", [D, 5120])
    lam_r = din("lam_r", [128, 256])
    subln_r = din("subln_r", [128, 1])
    w_bsb = din("w_bsb", [512, D])
    w_bda = din("w_bda", [512, D])
    w_out = din("w_out", [D, D])
    outT = nc.dram_tensor("outT", [D, S_LEN], F32, kind="ExternalOutput").ap()

    h1T = dscr("h1T", [D, S_LEN], F32)
    h2T = dscr("h2T", [D, S_LEN], F32)
    qaT = dscr("qaT", [512, S_LEN], BF16)
    kaT = dscr("kaT", [512, S_LEN], BF16)
    vaS = dscr("vaS", [S_LEN, 512], BF16)
    qdT = dscr("qdT", [512, S_LEN], BF16)
    kdT = dscr("kdT", [512, S_LEN], BF16)
    vdS = dscr("vdS", [S_LEN, 512], BF16)
    yaT = dscr("yaT", [512, S_LEN], BF16)
    ydT = dscr("ydT", [512, S_LEN], BF16)

    with ExitStack() as es:
        S = Sched(nc, es)

        def sbt(st, name, shape, dt):
            return st.enter_context(nc.sbuf_tensor(name, shape, dt))

        pp = [es.enter_context(nc.psum_tensor("pp%d" % i, [128, 1024], F32)) for i in range(4)]
        banks = []
        for i in range(4):
            banks.append(pp[i][:, 0:512])
            banks.append(pp[i][:, 512:1024])

        ones = sbt(es, "ones", [128, 128], BF16)
        negones = sbt(es, "negones", [128, 128], BF16)
        negui = sbt(es, "negui", [128, 128], BF16)
        negI = sbt(es, "negI", [128, 128], BF16)
        modsb = sbt(es, "modsb", [128, 72], F32)
        Acoef = sbt(es, "Acoef", [128, 24], F32)
        Gcoef = sbt(es, "Gcoef", [128, 24], F32)
        neglam = sbt(es, "neglam", [128, 1], F32)
        sub8 = sbt(es, "sub8", [128, 1], F32)

        S.op('pool', lambda e: e.memset(ones[:], 1.0), writes=['ones'])
        S.op('pool', lambda e: e.memset(negones[:], -1.0), writes=['negones'])
        S.op('pool', lambda e: e.affine_select(out=negui[:], in_=negones[:], pattern=[[-1, 128]],
                                               compare_op=ALU.is_ge, fill=0.0, base=0, channel_multiplier=1),
             reads=['negones'], writes=['negui'])
        S.op('pool', lambda e: e.affine_select(out=negI[:], in_=negones[:], pattern=[[-1, 128]],
                                               compare_op=ALU.is_equal, fill=0.0, base=0, channel_multiplier=1),
             reads=['negones'], writes=['negI'])

        with ExitStack() as ph:
            cs = sbt(ph, "cs", [128, 8], F32)
            cs_in = sbt(ph, "cs_in", [128, 8], F32)
            bada = sbt(ph, "bada", [128, 72], F32)
            npre = sbt(ph, "npre", [128, 24], F32)
            npost = sbt(ph, "npost", [128, 24], F32)
            lamt = sbt(ph, "lamt", [128, 256], F32)
            lprod = sbt(ph, "lprod", [128, 128], F32)
            lsum = sbt(ph, "lsum", [128, 2], F32)
            lexp = sbt(ph, "lexp", [128, 2], F32)
            subt = sbt(ph, "subt", [128, 1], F32)
            GW = 1152
            wa = [sbt(ph, "wa%d" % i, [128, 8, GW], F32) for i in range(2)]
            S.dma('sp', cs_in[:], cT, writes=['cs_in'], semkey='d_c')
            S.dma('sp', bada[:], b_ada_r, writes=['bada'], semkey='d_b')
            S.dma('sp', npre[:], npre_r, writes=['npre'], semkey='d_np')
            S.dma('sp', npost[:], npost_r, writes=['npost'], semkey='d_npo')
            S.dma('sp', lamt[:], lam_r, writes=['lamt'], semkey='d_lam')
            S.dma('sp', subt[:], subln_r, writes=['subt'], semkey='d_sub')
            S.op('act', lambda e: e.activation(out=cs[:], in_=cs_in[:], func=AF.Silu), reads=['cs_in'], writes=['cs'])
            w_ada_v = w_ada.rearrange("(kc p) n -> p kc n", p=128)
            modps = banks[0]
            for g in range(8):
                b = g % 2
                S.dma('sp', wa[b][:], w_ada_v[:, :, g * GW:(g + 1) * GW], writes=[('wa', b)], semkey=('d_wa', b))
                for jj in range(9):
                    j = g * 9 + jj
                    for kc in range(8):
                        S.op('pe', lambda e, b=b, jj=jj, kc=kc, j=j: e.matmul(
                            modps[:, j:j + 1], lhsT=wa[b][:, kc, jj * 128:(jj + 1) * 128], rhs=cs[:, kc:kc + 1],
                            start=(kc == 0), stop=(kc == 7)),
                            reads=[('wa', b), 'cs'], writes=['modps'])
            S.op('dve', lambda e: e.tensor_tensor(out=modsb[:], in0=modps[:, 0:72], in1=bada[:], op=ALU.add),
                 reads=['modps', 'bada'], writes=['modsb'])
            for i in range(3):
                rw = 1.0 if i == 1 else 0.5
                S.op('dve', lambda e, i=i: e.scalar_tensor_tensor(
                    out=Acoef[:, i * 8:(i + 1) * 8], in0=modsb[:, (i * 3 + 1) * 8:(i * 3 + 2) * 8], scalar=1.0,
                    in1=npre[:, i * 8:(i + 1) * 8], op0=ALU.add, op1=ALU.mult),
                    reads=['modsb', 'npre'], writes=['Acoef'])
                S.op('dve', lambda e, i=i, rw=rw: e.scalar_tensor_tensor(
                    out=Gcoef[:, i * 8:(i + 1) * 8], in0=modsb[:, (i * 3 + 2) * 8:(i * 3 + 3) * 8], scalar=rw,
                    in1=npost[:, i * 8:(i + 1) * 8], op0=ALU.mult, op1=ALU.mult),
                    reads=['modsb', 'npost'], writes=['Gcoef'])
            S.op('dve', lambda e: e.tensor_tensor(out=lprod[:, 0:64], in0=lamt[:, 0:64], in1=lamt[:, 64:128], op=ALU.mult),
                 reads=['lamt'], writes=['lprod'])
            S.op('dve', lambda e: e.tensor_tensor(out=lprod[:, 64:128], in0=lamt[:, 128:192], in1=lamt[:, 192:256], op=ALU.mult),
                 reads=['lamt'], writes=['lprod2'])
            S.op('dve', lambda e: e.reduce_sum(out=lsum[:, 0:1], in_=lprod[:, 0:64], axis=AX.X),
                 reads=['lprod'], writes=['lsum'])
            S.op('dve', lambda e: e.reduce_sum(out=lsum[:, 1:2], in_=lprod[:, 64:128], axis=AX.X),
                 reads=['lprod2'], writes=['lsum2'])
            S.op('act', lambda e: e.activation(out=lexp[:], in_=lsum[:], func=AF.Exp), reads=['lsum', 'lsum2'], writes=['lexp'])
            S.op('dve', lambda e: e.scalar_tensor_tensor(out=neglam[:], in0=lexp[:, 1:2], scalar=-LAMBDA_INIT,
                                                         in1=lexp[:, 0:1], op0=ALU.add, op1=ALU.subtract),
                 reads=['lexp'], writes=['neglam'])
            S.op('dve', lambda e: e.tensor_scalar(out=sub8[:], in0=subt[:], scalar1=1.0 - LAMBDA_INIT, scalar2=None,
                                                  op0=ALU.mult),
                 reads=['subt'], writes=['sub8'])
            S.barrier()

        def load_w_bf16(dst, src, kcn, ncols, key, col0=0):
            v = src.rearrange("(kc p) n -> p kc n", p=128)
            c = 0
            while c < ncols:
                w = min(1024, ncols - c)
                S.dma('pool', dst[:, :, c:c + w], v[:, :, col0 + c:col0 + c + w], writes=[key], semkey=('dw', key))
                c += w

        def rstd_from(bank, TT, inv_n, ln_t, rstd, bkey, rkey):
            S.op('act', lambda e: e.activation(out=ln_t[:, 0:TT], in_=bank[:, 0:TT], func=AF.Ln, bias=EPS, scale=inv_n),
                 reads=[bkey], writes=[rkey + '_ln'])
            S.op('act', lambda e: e.activation(out=rstd[:, 0:TT], in_=ln_t[:, 0:TT], func=AF.Exp, scale=-0.5),
                 reads=[rkey + '_ln'], writes=[rkey])

        def prenorm(h, hkey, u, ukey, sq, i_sub, TT, bank, bkey, ln_t, rstd, tmp, do_sq=True):
            if do_sq:
                S.op('dve', lambda e: e.tensor_tensor(out=sq[:], in0=h[:], in1=h[:], op=ALU.mult), reads=[hkey], writes=['sq'])
            for c in range(8):
                S.op('pe', lambda e, c=c: e.matmul(bank[:, 0:TT], lhsT=ones[:], rhs=sq[:, c, :], start=(c == 0), stop=(c == 7)),
                     reads=['sq', 'ones'], writes=[bkey])
            rstd_from(bank, TT, 1.0 / D, ln_t, rstd, bkey, 'rstd')
            for c in range(8):
                col = i_sub * 8 + c
                S.op('dve', lambda e, c=c, col=col: e.scalar_tensor_tensor(
                    out=tmp[c % 2][:, 0:TT], in0=h[:, c, :], scalar=Acoef[:, col:col + 1], in1=rstd[:, 0:TT],
                    op0=ALU.mult, op1=ALU.mult), reads=[hkey, 'rstd', 'Acoef'], writes=[('tmp', c % 2)])
                scol = (i_sub * 3 + 0) * 8 + c
                S.op('act', lambda e, c=c, scol=scol: e.activation(
                    out=u[:, c, :], in_=tmp[c % 2][:, 0:TT], func=AF.Identity, bias=modsb[:, scol:scol + 1], scale=1.0),
                    reads=[('tmp', c % 2), 'modsb'], writes=[ukey])

        def sq_chunk(y, sq, c):
            S.op('dve', lambda e: e.tensor_tensor(out=sq[:, c, :], in0=y[:, c, :], in1=y[:, c, :], op=ALU.mult),
                 reads=[('y', c)], writes=['sq'])

        def postnorm_resid(y, h, hkey, sq, i_sub, TT, bank, bkey, ln_t, rstd, tmp):
            for c in range(8):
                S.op('pe', lambda e, c=c: e.matmul(bank[:, 0:TT], lhsT=ones[:], rhs=sq[:, c, :], start=(c == 0), stop=(c == 7)),
                     reads=['sq', 'ones'], writes=[bkey])
            rstd_from(bank, TT, 1.0 / D, ln_t, rstd, bkey, 'rstd')
            for c in range(8):
                col = i_sub * 8 + c
                S.op('dve', lambda e, c=c, col=col: e.scalar_tensor_tensor(
                    out=tmp[c % 2][:, 0:TT], in0=y[:, c, :], scalar=Gcoef[:, col:col + 1], in1=rstd[:, 0:TT],
                    op0=ALU.mult, op1=ALU.mult), reads=[('y', c), 'rstd', 'Gcoef'], writes=[('tmp', c % 2)])
                S.op('dve', lambda e, c=c: e.tensor_tensor(out=h[:, c, :], in0=tmp[c % 2][:, 0:TT], in1=h[:, c, :], op=ALU.add),
                     reads=[('tmp', c % 2), hkey], writes=[hkey])

        def ffn_phase(src, dst, wts, i_sub, tag):
            TT = 256
            NT = S_LEN // TT
            with ExitStack() as ph:
                wg = sbt(ph, "wg" + tag, [128, 8, DFF], BF16)
                wu = sbt(ph, "wu" + tag, [128, 8, DFF], BF16)
                wd = sbt(ph, "wd" + tag, [128, NF, D], BF16)
                xt = [sbt(ph, "xt%d%s" % (i, tag), [128, 8, TT], F32) for i in range(3)]
                u = [sbt(ph, "u%d%s" % (i, tag), [128, 8, TT], BF16) for i in range(2)]
                sq = sbt(ph, "sq" + tag, [128, 8, TT], BF16)
                act = sbt(ph, "act" + tag, [128, NF, TT], BF16)
                y = sbt(ph, "y" + tag, [128, 8, TT], F32)
                st = [sbt(ph, "st%d%s" % (i, tag), [128, TT], F32) for i in range(2)]
                tmp = [sbt(ph, "tmp%d%s" % (i, tag), [128, TT], F32) for i in range(2)]
                ln_t = sbt(ph, "ln" + tag, [128, TT], F32)
                rstd = sbt(ph, "rstd" + tag, [128, TT], F32)
                load_w_bf16(wg, wts[0], 8, DFF, 'wg')
                load_w_bf16(wu, wts[1], 8, DFF, 'wu')
                load_w_bf16(wd, wts[2], NF, D, 'wd')
                srcv = src.rearrange("(c p) t -> p c t", p=128)
                dstv = dst.rearrange("(c p) t -> p c t", p=128)

                def load(t):
                    s = t % 3
                    S.dma('sp', xt[s][:], srcv[:, :, t * TT:(t + 1) * TT], writes=[('x', s)], semkey=('dx', s))

                def pre(t):
                    prenorm(xt[t % 3], ('x', t % 3), u[t % 2], ('u', t % 2), sq, i_sub, TT, banks[7], 'b7', ln_t, rstd, tmp)

                load(0)
                if NT > 1:
                    load(1)
                pre(0)
                for t in range(NT):
                    s = t % 3
                    ut = u[t % 2]
                    ukey = ('u', t % 2)
                    if t + 2 < NT:
                        load(t + 2)
                    for f in range(NF):
                        bk = banks[f % 4]
                        bkey = 'b%d' % (f % 4)
                        for k in range(8):
                            S.op('pe', lambda e, bk=bk, f=f, k=k: e.matmul(
                                bk[:, 0:TT], lhsT=wg[:, k, f * 128:(f + 1) * 128], rhs=ut[:, k, :], start=(k == 0), stop=(k == 7)),
                                reads=['wg', ukey], writes=[bkey])
                        for k in range(8):
                            S.op('pe', lambda e, bk=bk, f=f, k=k: e.matmul(
                                bk[:, 256:256 + TT], lhsT=wu[:, k, f * 128:(f + 1) * 128], rhs=ut[:, k, :], start=(k == 0), stop=(k == 7)),
                                reads=['wu', ukey], writes=[bkey])
                        S.op('act', lambda e, bk=bk, f=f: e.activation(out=st[f % 2][:], in_=bk[:, 0:TT], func=AF.Silu),
                             reads=[bkey], writes=[('st', f % 2)])
                        S.op('dve', lambda e, bk=bk, f=f: e.tensor_tensor(out=act[:, f, :], in0=st[f % 2][:], in1=bk[:, 256:256 + TT], op=ALU.mult),
                             reads=[bkey, ('st', f % 2)], writes=['act'])
                    if t + 1 < NT:
                        pre(t + 1)
                    for c in range(8):
                        bk = banks[4 + c % 3]
                        bkey = 'b%d' % (4 + c % 3)
                        for f in range(NF):
                            S.op('pe', lambda e, bk=bk, f=f, c=c: e.matmul(
                                bk[:, 0:TT], lhsT=wd[:, f, c * 128:(c + 1) * 128], rhs=act[:, f, :], start=(f == 0), stop=(f == NF - 1)),
                                reads=['wd', 'act'], writes=[bkey])
                        S.op('act', lambda e, bk=bk, c=c: e.activation(out=y[:, c, :], in_=bk[:, 0:TT], func=AF.Copy),
                             reads=[bkey], writes=[('y', c)])
                        sq_chunk(y, sq, c)
                    postnorm_resid(y, xt[s], ('x', s), sq, i_sub, TT, banks[7], 'b7', ln_t, rstd, tmp)
                    S.dma('sp', dstv[:, :, t * TT:(t + 1) * TT], xt[s][:], reads=[('x', s)], semkey=('dxo', s))
                S.barrier()

        def m1_phase():
            TT = 512
            NT = S_LEN // TT
            with ExitStack() as ph:
                win = sbt(ph, "win", [128, 8, 3072], BF16)
                xt = [sbt(ph, "m1x%d" % i, [128, 8, TT], F32) for i in range(2)]
                u2 = [sbt(ph, "m1u%d" % i, [128, 8, TT], BF16) for i in range(2)]
                sq = sbt(ph, "m1sq", [128, 8, TT], BF16)
                tmp = [sbt(ph, "m1tmp%d" % i, [128, TT], F32) for i in range(2)]
                ln_t = sbt(ph, "m1ln", [128, TT], F32)
                rstd = sbt(ph, "m1rstd", [128, TT], F32)
                stg = {}
                for nm in ('qa', 'ka', 'qd', 'kd', 'va', 'vd'):
                    stg[nm] = [sbt(ph, "stg_%s%d" % (nm, i), [128, 4, 512], BF16) for i in range(2)]
                load_w_bf16(win, w_in, 8, 3072, 'win')
                srcv = h1T.rearrange("(c p) t -> p c t", p=128)
                fm = [('qa', 0, 0.125, qaT), ('ka', 512, 1.0, kaT), ('qd', 1536, 0.125, qdT), ('kd', 2048, 1.0, kdT)]
                tm = [('va', 1024, vaS), ('vd', 2560, vdS)]
                S.dma('sp', xt[0][:], srcv[:, :, 0:TT], writes=[('x', 0)], semkey=('dx', 0))
                bi = 0
                for t in range(NT):
                    s = t % 2
                    if t + 1 < NT:
                        S.dma('sp', xt[1 - s][:], srcv[:, :, (t + 1) * TT:(t + 2) * TT], writes=[('x', 1 - s)], semkey=('dx', 1 - s))
                    if t == 0:
                        prenorm(xt[0], ('x', 0), u2[0], ('u', 0), sq, 1, TT, banks[7], 'b7', ln_t, rstd, tmp)
                    u = u2[s]
                    ukey = ('u', s)
                    for (nm, c0, scl, dram) in fm:
                        sg = stg[nm][t % 2]
                        skey = ('stg', nm, t % 2)
                        for j in range(4):
                            bk = banks[bi % 6]
                            bkey = 'b%d' % (bi % 6)
                            bi += 1
                            for k in range(8):
                                S.op('pe', lambda e, bk=bk, k=k, j=j, c0=c0: e.matmul(
                                    bk[:], lhsT=win[:, k, c0 + j * 128:c0 + (j + 1) * 128], rhs=u[:, k, :], start=(k == 0), stop=(k == 7)),
                                    reads=['win', ukey], writes=[bkey])
                            S.op('act', lambda e, bk=bk, j=j, sg=sg, scl=scl: e.activation(out=sg[:, j, :], in_=bk[:], func=AF.Copy, scale=scl),
                                 reads=[bkey], writes=[skey])
                        S.dma('sp', dram.rearrange("(c p) t -> p c t", p=128)[:, :, t * TT:(t + 1) * TT], sg[:],
                              reads=[skey], semkey=('dstg', nm, t % 2))
                    if t + 1 < NT:
                        prenorm(xt[1 - s], ('x', 1 - s), u2[1 - s], ('u', 1 - s), sq, 1, TT, banks[7], 'b7', ln_t, rstd, tmp)
                    for (nm, c0, dram) in tm:
                        sg = stg[nm][t % 2]
                        skey = ('stg', nm, t % 2)
                        for j in range(4):
                            bk = banks[bi % 6]
                            bkey = 'b%d' % (bi % 6)
                            bi += 1
                            for k in range(8):
                                S.op('pe', lambda e, bk=bk, k=k, j=j, c0=c0: e.matmul(
                                    bk[:], lhsT=u[:, k, j * 128:(j + 1) * 128], rhs=win[:, k, c0:c0 + 512], start=(k == 0), stop=(k == 7)),
                                    reads=['win', ukey], writes=[bkey])
                            S.op('dve', lambda e, bk=bk, j=j, sg=sg: e.tensor_copy(out=sg[:, j, :], in_=bk[:]),
                                 reads=[bkey], writes=[skey])
                        S.dma('sp', dram[t * TT:(t + 1) * TT, :].rearrange("(s p) f -> p s f", p=128), sg[:],
                              reads=[skey], semkey=('dstg', nm, t % 2))
                S.barrier()

        def sb_phase():
            with ExitStack() as ph:
                vsb = sbt(ph, "vsb", [128, NKT, 512], BF16)
                kT = [sbt(ph, "kT%d" % i, [128, S_LEN], BF16) for i in range(2)]
                qT = [sbt(ph, "qT%d" % i, [128, 512], BF16) for i in range(2)]
                onesw = sbt(ph, "onesw", [128, 512], BF16)
                msb = [sbt(ph, "msb%d" % j, [128, 512], BF16) for j in range(4)]
                e_t = [sbt(ph, "e_t%d" % i, [128, 1024], F32) for i in range(2)]
                sp_t = [sbt(ph, "sp_t%d" % i, [128, 1024], BF16) for i in range(3)]
                w_t = [sbt(ph, "w_t%d" % i, [128, 1024], BF16) for i in range(3)]
                cbf = [sbt(ph, "cbf%d" % i, [128, 512], BF16) for i in range(3)]
                ystg = [sbt(ph, "ystg%d" % i, [128, 512], BF16) for i in range(2)]
                S.op('pool', lambda e: e.memset(onesw[:], 1.0), writes=['onesw'])
                for b_ in range(2):
                    S.op('pool', lambda e, b_=b_: e.memset(kT[b_][64:128, :], 0.0), writes=[('kTz', b_)])
                    S.op('pool', lambda e, b_=b_: e.memset(qT[b_][64:128, :], 0.0), writes=[('qTz', b_)])
                for j in range(4):
                    S.op('pool', lambda e, j=j: e.affine_select(out=msb[j][:], in_=onesw[:], pattern=[[1, 512]],
                                                                 compare_op=ALU.is_ge, fill=0.0, base=-128 * j - 1,
                                                                 channel_multiplier=-1),
                         reads=['onesw'], writes=[('msb', j)])
                S.dma('sp', vsb[:], vaS.rearrange("(i p) f -> p i f", p=128), writes=['vsb'], semkey='d_vsb')
                pairs = []
                sidx = 0
                for h in range(8):
                    for Q in range(NQ):
                        top = 4 * Q + 3
                        for ia in range(top, -1, -2):
                            ib = ia - 1
                            pairs.append(dict(h=h, Q=Q, ia=ia, ib=ib, first=(ia == top), last=(ib == 0), sidx=sidx,
                                              ja=(ia - 4 * Q if ia >= 4 * Q else None),
                                              jb=(ib - 4 * Q if ib >= 4 * Q else None), p=len(pairs)))
                        sidx += 1
                YK, YKEY = banks[6], 'b6'
                CK, CKEY = banks[7], 'b7'

                def load_k(h):
                    S.dma('sp', kT[h % 2][0:64, :], kaT[h * 64:(h + 1) * 64, :], writes=[('kT', h % 2)], semkey=('dk', h % 2))

                def load_q(h, Q, sidx):
                    S.dma('sp', qT[sidx % 2][0:64, :], qaT[h * 64:(h + 1) * 64, Q * 512:(Q + 1) * 512],
                          writes=[('qT', sidx % 2)], semkey=('dq', sidx % 2))

                def P1(pr):
                    h, Q, p = pr['h'], pr['Q'], pr['p']
                    if pr['first']:
                        if Q == 0:
                            if h == 0:
                                load_k(0)
                                load_q(0, 0, 0)
                            if h + 1 < 8:
                                load_k(h + 1)
                        nh, nQ = (h, Q + 1) if Q + 1 < NQ else (h + 1, 0)
                        if nh < 8:
                            load_q(nh, nQ, pr['sidx'] + 1)
                    zp = pp[p % 3]
                    zkey = ('zp', p % 3)
                    kt = kT[h % 2]
                    qt = qT[pr['sidx'] % 2]
                    rd = [('kT', h % 2), ('qT', pr['sidx'] % 2), ('kTz', h % 2), ('qTz', pr['sidx'] % 2)]
                    for half, i in ((0, pr['ia']), (1, pr['ib'])):
                        S.op('pe', lambda e, half=half, i=i: e.matmul(zp[:, half * 512:(half + 1) * 512], lhsT=kt[:, i * 128:(i + 1) * 128],
                                                                      rhs=qt[:], start=True, stop=False),
                             reads=rd, writes=[zkey])

                def A1(pr):
                    p = pr['p']
                    zp = pp[p % 3]
                    zkey = ('zp', p % 3)
                    et = e_t[p % 2]
                    spt = sp_t[p % 3]
                    S.op('act', lambda e: e.activation(out=et[:], in_=zp[:], func=AF.Exp), reads=[zkey], writes=[('e', p % 2)])
                    S.op('act', lambda e: e.activation(out=spt[:], in_=et[:], func=AF.Ln, bias=1.0, scale=1.0),
                         reads=[('e', p % 2)], writes=[('sp', p % 3)])
                    for half, j in ((0, pr['ja']), (1, pr['jb'])):
                        if j is not None:
                            S.op('pool', lambda e, half=half, j=j: e.tensor_tensor(
                                out=spt[:, half * 512:(half + 1) * 512], in0=spt[:, half * 512:(half + 1) * 512], in1=msb[j][:], op=ALU.mult),
                                reads=[('sp', p % 3), ('msb', j)], writes=[('sp', p % 3)])

                def P2(pr):
                    p = pr['p']
                    zp = pp[p % 3]
                    zkey = ('zp', p % 3)
                    spt = sp_t[p % 3]
                    spk = ('sp', p % 3)
                    za, zb_ = zp[:, 0:512], zp[:, 512:1024]
                    spa, spb = spt[:, 0:512], spt[:, 512:1024]
                    pc = (p - 1) % 3
                    S.op('pe', lambda e: e.matmul(za, lhsT=negui[:], rhs=spa, start=False, stop=True, skip_group_check=True),
                         reads=['negui', spk, zkey], writes=[zkey])
                    if not pr['first']:
                        S.op('pe', lambda e: e.matmul(za, lhsT=negI[:], rhs=cbf[pc][:], start=False, stop=True, skip_group_check=True),
                             reads=['negI', ('cbf', pc), zkey], writes=[zkey])
                    S.op('pe', lambda e: e.matmul(zb_, lhsT=negui[:], rhs=spb, start=False, stop=True, skip_group_check=True),
                         reads=['negui', spk, zkey], writes=[zkey])
                    S.op('pe', lambda e: e.matmul(zb_, lhsT=negones[:], rhs=spa, start=False, stop=True, skip_group_check=True),
                         reads=['negones', spk, zkey], writes=[zkey])
                    if not pr['first']:
                        S.op('pe', lambda e: e.matmul(zb_, lhsT=negI[:], rhs=cbf[pc][:], start=False, stop=True, skip_group_check=True),
                             reads=['negI', ('cbf', pc), zkey], writes=[zkey])
                    if not pr['last']:
                        S.op('pe', lambda e: e.matmul(CK, lhsT=ones[:], rhs=spa, start=pr['first'], stop=False, skip_group_check=True),
                             reads=['ones', spk] + ([] if pr['first'] else [CKEY]), writes=[CKEY])
                        S.op('pe', lambda e: e.matmul(CK, lhsT=ones[:], rhs=spb, start=False, stop=True, skip_group_check=True),
                             reads=['ones', spk, CKEY], writes=[CKEY])
                        S.op('dve', lambda e: e.tensor_copy(out=cbf[p % 3][:], in_=CK), reads=[CKEY], writes=[('cbf', p % 3)])

                def A2(pr):
                    p = pr['p']
                    zp = pp[p % 3]
                    zkey = ('zp', p % 3)
                    wt = w_t[p % 3]
                    S.op('act', lambda e: e.activation(out=wt[:], in_=zp[:], func=AF.Exp), reads=[zkey], writes=[('w', p % 3)])
                    for half, j in ((0, pr['ja']), (1, pr['jb'])):
                        if j is not None:
                            S.op('pool', lambda e, half=half, j=j: e.tensor_tensor(
                                out=wt[:, half * 512:(half + 1) * 512], in0=wt[:, half * 512:(half + 1) * 512], in1=msb[j][:], op=ALU.mult),
                                reads=[('w', p % 3), ('msb', j)], writes=[('w', p % 3)])

                def P3(pr):
                    h, Q, p = pr['h'], pr['Q'], pr['p']
                    wt = w_t[p % 3]
                    hp = h // 2
                    ho = (h % 2) * 64
                    for half, i in ((0, pr['ia']), (1, pr['ib'])):
                        st_ = pr['first'] and half == 0
                        S.op('pe', lambda e, half=half, i=i, st_=st_: e.matmul(
                            YK, lhsT=vsb[:, i, hp * 128:(hp + 1) * 128], rhs=wt[:, half * 512:(half + 1) * 512],
                            start=st_, stop=(pr['last'] and half == 1), skip_group_check=True),
                            reads=['vsb', ('w', p % 3)] + ([] if st_ else [YKEY]), writes=[YKEY])
                    if pr['last']:
                        sl = pr['sidx'] % 2
                        S.op('dve', lambda e: e.tensor_copy(out=ystg[sl][ho:ho + 64, :], in_=YK[ho:ho + 64, :]), reads=[YKEY], writes=[('ystg', sl)])
                        S.dma('sp', yaT[h * 64:(h + 1) * 64, Q * 512:(Q + 1) * 512], ystg[sl][ho:ho + 64, :], reads=[('ystg', sl)], semkey=('dyo', sl))

                n = len(pairs)
                P1(pairs[0])
                for s in range(n + 2):
                    if 0 <= s - 1 < n:
                        P2(pairs[s - 1])
                    if s < n:
                        A1(pairs[s])
                    if 0 <= s - 1 < n:
                        A2(pairs[s - 1])
                    if s + 1 < n:
                        P1(pairs[s + 1])
                    if 0 <= s - 2 < n:
                        P3(pairs[s - 2])
                S.barrier()

        def da_phase():
            with ExitStack() as ph:
                vda = [sbt(ph, "vda%d" % i, [128, NKT, 128], BF16) for i in range(2)]
                kaug = [[sbt(ph, "kaug%d_%d" % (b, m), [128, S_LEN], BF16) for m in range(2)] for b in range(2)]
                qaug = [[[sbt(ph, "qaug%d_%d_%d" % (h, b, m), [128, 512], BF16) for m in range(2)] for b in range(2)] for h in range(4)]
                fullb = [[sbt(ph, "fb%d_%d" % (h, j), [128, 512], F32) for j in range(4)] for h in range(4)]
                bvi = sbt(ph, "bvi", [128, 64], F32)
                bv = [sbt(ph, "bv%d" % h, [128, 64], F32) for h in range(4)]
                augf = sbt(ph, "augf", [128, 512], F32)
                dtmp = sbt(ph, "dtmp", [128, 512], F32)
                dtmp2 = sbt(ph, "dtmp2", [128, 512], F32)
                arg_t = [sbt(ph, "arg%d" % i, [128, 1024], F32) for i in range(2)]
                w_t = [sbt(ph, "dw%d" % i, [128, 1024], BF16) for i in range(3)]
                wsum_t = [sbt(ph, "wsum%d" % m, [128, 512], F32) for m in range(2)]
                ones32 = sbt(ph, "ones32", [128, 128], F32)
                S.op('pool', lambda e: e.memset(ones32[:], 1.0), writes=['ones32'])
                r_t = [sbt(ph, "dr%d" % i, [128, 512], F32) for i in range(2)]
                t_t = [sbt(ph, "dt%d" % i, [128, 512], F32) for i in range(2)]
                yc_t = [sbt(ph, "dyc%d" % i, [128, 512], F32) for i in range(2)]
                yo = [sbt(ph, "dyo%d" % i, [128, 512], BF16) for i in range(2)]
                S.op('pool', lambda e: e.iota(out=augf[64:96, :], pattern=[[2, 256], [0, 2]], base=0, channel_multiplier=0,
                                              allow_small_or_imprecise_dtypes=True), writes=['augfa'])
                S.op('pool', lambda e: e.iota(out=augf[96:128, :], pattern=[[0, 256], [1, 2]], base=0, channel_multiplier=0,
                                              allow_small_or_imprecise_dtypes=True), writes=['augfb'])
                for h in range(4):
                    for b in range(2):
                        for m in range(2):
                            S.op('dve', lambda e, h=h, b=b, m=m: e.tensor_scalar(
                                out=qaug[h][b][m][64:128, :], in0=augf[64:128, :], scalar1=-SLOPES[h], scalar2=None, op0=ALU.mult),
                                reads=['augfa', 'augfb'], writes=[('qaug_aug', h, b, m)])
                for b in range(2):
                    for m in range(2):
                        ka = kaug[b][m]
                        S.op('pool', lambda e, ka=ka: e.memset(ka[64:128, :], 0.0), writes=[('kaug_aug', b, m)])
                        S.op('pool', lambda e, ka=ka: e.memset(ka[64:65, :], 1.0), reads=[('kaug_aug', b, m)], writes=[('kaug_aug', b, m)])
                        S.op('pool', lambda e, ka=ka: e.memset(ka[96:97, :], 1.0), reads=[('kaug_aug', b, m)], writes=[('kaug_aug', b, m)])
                S.op('pool', lambda e: e.iota(out=bvi[:], pattern=[[-128, 64]], base=-128, channel_multiplier=1,
                                              allow_small_or_imprecise_dtypes=True), writes=['bvi'])
                for h in range(4):
                    S.op('dve', lambda e, h=h: e.tensor_scalar(out=bv[h][:], in0=bvi[:], scalar1=SLOPES[h], scalar2=None, op0=ALU.mult),
                         reads=['bvi'], writes=[('bv', h)])
                ttf = sbt(ph, "ttf", [128, 512], F32)
                S.op('pool', lambda e: e.iota(out=ttf[:], pattern=[[1, 512]], base=0, channel_multiplier=0,
                                              allow_small_or_imprecise_dtypes=True), writes=['ttf'])
                for j in range(4):
                    S.op('pool', lambda e, j=j: e.iota(out=dtmp[:], pattern=[[-1, 512]], base=128 * j, channel_multiplier=1,
                                                       allow_small_or_imprecise_dtypes=True), writes=['dtmp'])
                    for h in range(4):
                        S.op('dve', lambda e, h=h: e.tensor_scalar(out=dtmp2[:], in0=dtmp[:], scalar1=SLOPES[h], scalar2=None,
                                                                   op0=ALU.mult),
                             reads=['dtmp'], writes=['dtmp2'])
                        S.op('dve', lambda e, h=h: e.scalar_tensor_tensor(out=dtmp2[:], in0=dtmp[:], scalar=-SLOPES[h], in1=dtmp2[:],
                                                                          op0=ALU.mult, op1=ALU.min),
                             reads=['dtmp', 'dtmp2'], writes=['dtmp2'])
                        S.op('dve', lambda e, h=h: e.scalar_tensor_tensor(out=dtmp2[:], in0=ttf[:], scalar=SLOPES[h], in1=dtmp2[:],
                                                                          op0=ALU.mult, op1=ALU.add),
                             reads=['dtmp2', 'ttf'], writes=['dtmp2'])
                        S.op('pool', lambda e, h=h, j=j: e.affine_select(
                            out=fullb[h][j][:], in_=dtmp2[:], pattern=[[64, 8], [0, 64]], compare_op=ALU.is_ge, fill=NEGBIG,
                            base=63 - 128 * j, channel_multiplier=-1), reads=['dtmp2'], writes=[('fullb', h, j)])

                items = []
                sidx = 0
                for h in range(4):
                    for Q in range(NQ):
                        top = 4 * Q + 3
                        for i in range(0, top + 1):
                            items.append(dict(h=h, Q=Q, i=i, first=(i == 0), last=(i == top), sidx=sidx,
                                              diag=(i - 4 * Q if i >= 4 * Q else None), idx=len(items)))
                        sidx += 1

                vdv = vdS.rearrange("(i p) f -> p i f", p=128)

                def load_k(h):
                    for m in range(2):
                        S.dma('sp', kaug[h % 2][m][0:64, :], kdT[h * 128 + m * 64:h * 128 + (m + 1) * 64, :],
                              writes=[('kaug', h % 2, m)], semkey=('dkd', h % 2, m))

                def load_v(h):
                    S.dma('sp', vda[h % 2][:], vdv[:, :, h * 128:(h + 1) * 128], writes=[('vda', h % 2)], semkey=('dvd', h % 2))

                def load_q(h, Q, sidx):
                    for m in range(2):
                        S.dma('sp', qaug[h][sidx % 2][m][0:64, :], qdT[h * 128 + m * 64:h * 128 + (m + 1) * 64, Q * 512:(Q + 1) * 512],
                              writes=[('qaug', h, sidx % 2, m)], semkey=('dqd', sidx % 2, m))

                YB = [(banks[4], 'b4'), (banks[5], 'b5')]
                LB = [(banks[6], 'b6'), (banks[7], 'b7')]

                def stA(it):
                    h, Q, i, idx = it['h'], it['Q'], it['i'], it['idx']
                    if it['first']:
                        if Q == 0:
                            if h == 0:
                                load_k(0)
                                load_v(0)
                                load_q(0, 0, 0)
                            if h + 1 < 4:
                                load_k(h + 1)
                        nh, nQ = (h, Q + 1) if Q + 1 < NQ else (h + 1, 0)
                        if nh < 4:
                            load_q(nh, nQ, it['sidx'] + 1)
                    sb_ = it['sidx'] % 2
                    for m in range(2):
                        bn = (idx % 2) * 2 + m
                        bk, bkey = banks[bn], 'b%d' % bn
                        ka = kaug[h % 2][m]
                        qa = qaug[h][sb_][m]
                        S.op('pe', lambda e, bk=bk, ka=ka, qa=qa: e.matmul(bk[:], lhsT=ka[:, i * 128:(i + 1) * 128], rhs=qa[:], start=True, stop=True),
                             reads=[('kaug', h % 2, m), ('kaug_aug', h % 2, m), ('qaug', h, sb_, m), ('qaug_aug', h, sb_, m)], writes=[bkey])

                def stA2(it):
                    h, Q, i, idx = it['h'], it['Q'], it['i'], it['idx']
                    pi = idx % 2
                    zp = pp[pi]
                    bkeys = ['b%d' % (pi * 2), 'b%d' % (pi * 2 + 1)]
                    wt = w_t[idx % 3]
                    wkey = ('w', idx % 3)
                    if it['diag'] is None:
                        n_idx = 4 * Q - i - 1
                        S.op('act', lambda e: e.activation(out=wt[:], in_=zp[:], func=AF.Exp, bias=bv[h][:, n_idx:n_idx + 1], scale=1.0),
                             reads=bkeys + [('bv', h)], writes=[wkey])
                    else:
                        j = it['diag']
                        at = arg_t[pi]
                        for m in range(2):
                            S.op('dve', lambda e, m=m: e.tensor_tensor(out=at[:, m * 512:(m + 1) * 512], in0=zp[:, m * 512:(m + 1) * 512],
                                                                       in1=fullb[h][j][:], op=ALU.add),
                                 reads=[bkeys[m], ('fullb', h, j)], writes=[('arg', pi)])
                        S.op('act', lambda e: e.activation(out=wt[:], in_=at[:], func=AF.Exp), reads=[('arg', pi)], writes=[wkey])

                def stB(it):
                    h, Q, i, idx = it['h'], it['Q'], it['i'], it['idx']
                    if it['first'] and Q == 0 and h + 1 < 4:
                        load_v(h + 1)
                    wt = w_t[idx % 3]
                    wkey = ('w', idx % 3)
                    sl = it['sidx'] % 2
                    for m in range(2):
                        yk, ykey = YB[m]
                        S.op('pe', lambda e, yk=yk, m=m: e.matmul(yk, lhsT=vda[h % 2][:, i, :], rhs=wt[:, m * 512:(m + 1) * 512],
                                                               start=it['first'], stop=it['last']),
                             reads=[('vda', h % 2), wkey] + ([] if it['first'] else [ykey]), writes=[ykey])
                    for m in range(2):
                        lk, lkey = LB[m]
                        S.op('pe', lambda e, lk=lk, m=m: e.matmul(lk, lhsT=ones[:], rhs=wt[:, m * 512:(m + 1) * 512], start=it['first'], stop=it['last']),
                             reads=['ones', wkey] + ([] if it['first'] else [lkey]), writes=[lkey])
                    if it['last']:
                        for m in range(2):
                            S.op('act', lambda e, m=m: e.activation(out=yc_t[m][:], in_=YB[m][0], func=AF.Copy), reads=[YB[m][1]], writes=[('yc', m)])
                            S.op('dve', lambda e, m=m: e.tensor_copy(out=wsum_t[m][:], in_=LB[m][0]), reads=[LB[m][1]], writes=[('wsum', m)])
                        for m in range(2):
                            S.op('dve', lambda e, m=m: e.reciprocal(out=r_t[m][:], in_=wsum_t[m][:]), reads=[('wsum', m)], writes=[('r', m)])
                        for m in range(2):
                            S.op('dve', lambda e, m=m: e.tensor_tensor(out=t_t[m][:], in0=yc_t[m][:], in1=r_t[m][:], op=ALU.mult),
                                 reads=[('yc', m), ('r', m)], writes=[('t', m)])
                        S.op('dve', lambda e: e.scalar_tensor_tensor(out=yo[sl][:], in0=t_t[1][:], scalar=neglam[:, 0:1], in1=t_t[0][:],
                                                                     op0=ALU.mult, op1=ALU.add),
                             reads=[('t', 0), ('t', 1), 'neglam'], writes=[('yo', sl)])
                        S.dma('sp', ydT[h * 128:(h + 1) * 128, Q * 512:(Q + 1) * 512], yo[sl][:], reads=[('yo', sl)], semkey=('dydo', sl))

                n = len(items)
                for s in range(n + 2):
                    if 0 <= s - 2 < n:
                        stB(items[s - 2])
                    if 0 <= s - 1 < n:
                        stA2(items[s - 1])
                    if s < n:
                        stA(items[s])
                S.barrier()

        def m3_phase():
            TT = 512
            NT = S_LEN // TT
            with ExitStack() as ph:
                wsb = sbt(ph, "m3wsb", [128, 4, D], BF16)
                wda = sbt(ph, "m3wda", [128, 4, D], BF16)
                wo = sbt(ph, "m3wo", [128, 8, D], BF16)
                wgt = sbt(ph, "m3wgt", [128, 8, 2048], BF16)
                xt = [sbt(ph, "m3x%d" % i, [128, 8, TT], F32) for i in range(2)]
                yat = [sbt(ph, "m3ya%d" % i, [128, 4, TT], BF16) for i in range(2)]
                ydt = [sbt(ph, "m3yd%d" % i, [128, 4, TT], BF16) for i in range(2)]
                u2 = [sbt(ph, "m3u%d" % i, [128, 8, TT], BF16) for i in range(2)]
                sq = sbt(ph, "m3sq", [128, 8, TT], BF16)
                mg = sbt(ph, "m3mg", [128, 8, TT], BF16)
                sqy = sbt(ph, "m3sqy", [128, 4, TT], BF16)
                ydn = [sbt(ph, "m3ydn%d" % i, [128, 4, TT], BF16) for i in range(2)]
                ln2 = [sbt(ph, "m3ln2%d" % i, [128, TT], F32) for i in range(2)]
                rs2 = [sbt(ph, "m3rs2%d" % i, [128, TT], F32) for i in range(2)]
                y = sbt(ph, "m3y", [128, 8, TT], F32)
                sg_t = [sbt(ph, "m3sg%d" % i, [128, TT], F32) for i in range(4)]
                mm_t = [sbt(ph, "m3mm%d" % i, [128, TT], F32) for i in range(4)]
                tmp = [sbt(ph, "m3tmp%d" % i, [128, TT], F32) for i in range(2)]
                ln_t = sbt(ph, "m3ln", [128, TT], F32)
                rstd = sbt(ph, "m3rstd", [128, TT], F32)
                load_w_bf16(wsb, w_bsb, 4, D, 'wsb')
                load_w_bf16(wda, w_bda, 4, D, 'wda')
                load_w_bf16(wgt, w_in, 8, 2048, 'wgt', col0=3072)
                load_w_bf16(wo, w_out, 8, D, 'wo')
                srcv = h1T.rearrange("(c p) t -> p c t", p=128)
                dstv = h2T.rearrange("(c p) t -> p c t", p=128)
                yav = yaT.rearrange("(c p) t -> p c t", p=128)
                ydv = ydT.rearrange("(c p) t -> p c t", p=128)

                def load(t):
                    s = t % 2
                    S.dma('sp', xt[s][:], srcv[:, :, t * TT:(t + 1) * TT], writes=[('x', s)], semkey=('dx', s))
                    S.dma('sp', yat[s][:], yav[:, :, t * TT:(t + 1) * TT], writes=[('ya', s)], semkey=('dya', s))
                    S.dma('sp', ydt[s][:], ydv[:, :, t * TT:(t + 1) * TT], writes=[('yd', s)], semkey=('dyd', s))

                def pre_sq(t):
                    s_ = t % 2
                    S.op('dve', lambda e: e.tensor_tensor(out=sq[:], in0=xt[s_][:], in1=xt[s_][:], op=ALU.mult), reads=[('x', s_)], writes=['sq'])
                    S.op('dve', lambda e: e.tensor_tensor(out=sqy[:], in0=ydt[s_][:], in1=ydt[s_][:], op=ALU.mult),
                         reads=[('yd', s_)], writes=['sqy'])

                def pre(t):
                    prenorm(xt[t % 2], ('x', t % 2), u2[t % 2], ('u', t % 2), sq, 1, TT, banks[7], 'b7', ln_t, rstd, tmp, do_sq=False)
                    s_ = t % 2
                    for k in range(4):
                        bk, bkey = banks[6 + k % 2], 'b%d' % (6 + k % 2)
                        S.op('pe', lambda e, bk=bk, k=k: e.matmul(bk, lhsT=ones[:], rhs=sqy[:, k, :], start=True, stop=True),
                             reads=['ones', 'sqy'], writes=[bkey])
                        S.op('act', lambda e, bk=bk, k=k: e.activation(out=ln2[k % 2][:], in_=bk, func=AF.Ln, bias=EPS, scale=1.0 / 128),
                             reads=[bkey], writes=[('ln2', k % 2)])
                        S.op('act', lambda e, k=k: e.activation(out=rs2[k % 2][:], in_=ln2[k % 2][:], func=AF.Exp, scale=-0.5),
                             reads=[('ln2', k % 2)], writes=[('rs2', k % 2)])
                        S.op('dve', lambda e, k=k: e.scalar_tensor_tensor(out=ydn[s_][:, k, :], in0=ydt[s_][:, k, :], scalar=sub8[:, 0:1],
                                                                          in1=rs2[k % 2][:], op0=ALU.mult, op1=ALU.mult),
                             reads=[('yd', s_), ('rs2', k % 2), 'sub8'], writes=[('ydn', s_)])

                load(0)
                pre_sq(0)
                pre(0)
                for t in range(NT):
                    s = t % 2
                    u = u2[s]
                    ukey = ('u', s)
                    if t + 1 < NT:
                        load(t + 1)
                    for c in range(8):
                        if c == 4 and t + 1 < NT:
                            pre_sq(t + 1)
                        cs_ = slice(c * 128, (c + 1) * 128)
                        b0 = (c % 2) * 4
                        bA, bB, bGa, bGd = banks[b0], banks[b0 + 1], banks[b0 + 2], banks[b0 + 3]
                        kA, kB, kGa, kGd = ['b%d' % (b0 + i) for i in range(4)]
                        r0 = (c % 2) * 2
                        for k in range(4):
                            S.op('pe', lambda e, k=k, cs_=cs_, bA=bA: e.matmul(bA[:], lhsT=wsb[:, k, cs_], rhs=yat[s][:, k, :], start=(k == 0), stop=(k == 3)),
                                 reads=['wsb', ('ya', s)], writes=[kA])
                        for k in range(4):
                            S.op('pe', lambda e, k=k, cs_=cs_, bB=bB: e.matmul(bB[:], lhsT=wda[:, k, cs_], rhs=ydn[s][:, k, :], start=(k == 0), stop=(k == 3)),
                                 reads=['wda', ('ydn', s)], writes=[kB])
                        for k in range(8):
                            S.op('pe', lambda e, k=k, c=c, bGa=bGa: e.matmul(bGa[:], lhsT=wgt[:, k, c * 128:(c + 1) * 128], rhs=u[:, k, :], start=(k == 0), stop=(k == 7)),
                                 reads=['wgt', ukey], writes=[kGa])
                        for k in range(8):
                            S.op('pe', lambda e, k=k, c=c, bGd=bGd: e.matmul(bGd[:], lhsT=wgt[:, k, 1024 + c * 128:1024 + (c + 1) * 128], rhs=u[:, k, :], start=(k == 0), stop=(k == 7)),
                                 reads=['wgt', ukey], writes=[kGd])
                        S.op('act', lambda e, bGa=bGa, r0=r0: e.activation(out=sg_t[r0][:], in_=bGa[:], func=AF.Sigmoid), reads=[kGa], writes=[('sg', r0)])
                        S.op('act', lambda e, bGd=bGd, r0=r0: e.activation(out=sg_t[r0 + 1][:], in_=bGd[:], func=AF.Sigmoid), reads=[kGd], writes=[('sg', r0 + 1)])
                        S.op('dve', lambda e, bA=bA, r0=r0: e.tensor_tensor(out=mm_t[r0][:], in0=bA[:], in1=sg_t[r0][:], op=ALU.mult),
                             reads=[kA, ('sg', r0)], writes=[('mm', r0)])
                        S.op('dve', lambda e, bB=bB, r0=r0: e.tensor_tensor(out=mm_t[r0 + 1][:], in0=bB[:], in1=sg_t[r0 + 1][:], op=ALU.mult),
                             reads=[kB, ('sg', r0 + 1)], writes=[('mm', r0 + 1)])
                        S.op('dve', lambda e, c=c, r0=r0: e.tensor_tensor(out=mg[:, c, :], in0=mm_t[r0][:], in1=mm_t[r0 + 1][:], op=ALU.add),
                             reads=[('mm', r0), ('mm', r0 + 1)], writes=['mg'])
                    if t + 1 < NT:
                        pre(t + 1)
                    for c in range(8):
                        bk = banks[c % 6]
                        bkey = 'b%d' % (c % 6)
                        for k in range(8):
                            S.op('pe', lambda e, bk=bk, k=k, c=c: e.matmul(bk[:], lhsT=wo[:, k, c * 128:(c + 1) * 128], rhs=mg[:, k, :], start=(k == 0), stop=(k == 7)),
                                 reads=['wo', 'mg'], writes=[bkey])
                        S.op('act', lambda e, bk=bk, c=c: e.activation(out=y[:, c, :], in_=bk[:], func=AF.Copy), reads=[bkey], writes=[('y', c)])
                        sq_chunk(y, sq, c)
                    postnorm_resid(y, xt[s], ('x', s), sq, 1, TT, banks[7], 'b7', ln_t, rstd, tmp)
                    S.dma('sp', dstv[:, :, t * TT:(t + 1) * TT], xt[s][:], reads=[('x', s)], semkey=('dxo', s))
                S.barrier()

        ffn_phase(xT, h1T, ffn_w[0], 0, "a")
        m1_phase()
        sb_phase()
        da_phase()
        m3_phase()
        ffn_phase(h2T, outT, ffn_w[1], 2, "b")
    return nc


_CACHE = {}


def _layout_inputs(inp, b):
    f = lambda a: np.ascontiguousarray(np.asarray(a, dtype=np.float32))
    d = {}
    d["xT"] = f(np.asarray(inp["x"])[b].T)
    d["cT"] = f(np.asarray(inp["c"])[b].reshape(8, 128).T)
    d["w_ada"] = f(inp["w_ada"][0])
    d["b_ada_r"] = f(np.asarray(inp["b_ada"])[0].reshape(72, 128).T)
    d["npre_r"] = f(np.asarray(inp["norm_pre"])[0].reshape(24, 128).T)
    d["npost_r"] = f(np.asarray(inp["norm_post"])[0].reshape(24, 128).T)
    d["f1_wg"] = f(inp["ffn1_w_gate"][0])
    d["f1_wu"] = f(inp["ffn1_w_up"][0])
    d["f1_wd"] = f(inp["ffn1_w_down"][0])
    d["f2_wg"] = f(inp["ffn2_w_gate"][0])
    d["f2_wu"] = f(inp["ffn2_w_up"][0])
    d["f2_wd"] = f(inp["ffn2_w_down"][0])
    d["w_in"] = f(inp["w_in"][0])
    lam = np.concatenate([np.asarray(inp["da_lambda_q1"])[0], np.asarray(inp["da_lambda_k1"])[0],
                          np.asarray(inp["da_lambda_q2"])[0], np.asarray(inp["da_lambda_k2"])[0]])
    d["lam_r"] = f(np.broadcast_to(lam[None, :], (128, 256)))
    d["subln_r"] = f(np.asarray(inp["da_subln"])[0].reshape(128, 1))
    d["w_bsb"] = f(inp["w_branch_sb"][0])
    d["w_bda"] = f(inp["w_branch_da"][0])
    d["w_out"] = f(inp["w_out"][0])
    return d


def kernel(**inputs):
    x = np.asarray(inputs["x"])
    B, S_LEN, _ = x.shape
    if S_LEN not in _CACHE:
        _CACHE[S_LEN] = build_program(S_LEN)
    nc = _CACHE[S_LEN]
    in_maps = [_layout_inputs(inputs, b) for b in range(B)]
    res = run_bass_kernel_spmd(nc, in_maps, core_ids=list(range(B)))
    out = np.empty((B, S_LEN, D), dtype=np.float32)
    for b in range(B):
        out[b] = np.asarray(res.results[b]["outT"]).T
    return out
```
